# Optimizing a Trainium2 kernel written in Bass

```python
import jax, jax.numpy as jnp
from jax import lax
import numpy as np

D_MODEL = 1024
BATCH = 8
SEQ = 4096
DEPTH = 2

N_HEADS = 16
HEAD_DIM = 64
ATTN_WIDTH = N_HEADS * HEAD_DIM
DILATED_PATTERNS = ((128, 1), (512, 4), (2048, 16))
N_GROUPS = len(DILATED_PATTERNS)
FOX_Q_BLOCK = 128
ROT_DIM = HEAD_DIM // 4
ROPE_THETA = 500000.0
D_FF = -(-8 * D_MODEL // (3 * 256)) * 256
RMS_EPS = 1e-6
NEG_INF = -1e30
N_MIXERS = 2
N_A_LAYERS = (DEPTH + 1) // 2
N_B_LAYERS = DEPTH // 2

kernel_name = "hybrid_dilated_fox_swiglu"


def rmsnorm(x, g):
    xf = x.astype(jnp.float32)
    y = xf * lax.rsqrt(jnp.mean(xf * xf, axis=-1, keepdims=True) + RMS_EPS) * g.astype(jnp.float32)
    return y.astype(x.dtype)


def partial_rotary(t, pos):
    half = ROT_DIM // 2
    inv_freq = ROPE_THETA ** (-jnp.arange(half, dtype=jnp.float32) * 2.0 / ROT_DIM)
    ang = pos[:, None] * inv_freq[None, :]
    cos = jnp.cos(ang)[None, :, None, :]
    sin = jnp.sin(ang)[None, :, None, :]
    t1 = t[..., :half].astype(jnp.float32)
    t2 = t[..., half:ROT_DIM].astype(jnp.float32)
    rot = jnp.concatenate([t1 * cos - t2 * sin, t2 * cos + t1 * sin], axis=-1).astype(t.dtype)
    return jnp.concatenate([rot, t[..., ROT_DIM:]], axis=-1)


def dilated_band_attention(q, k, v, dil, steps):
    B, S, H, D = q.shape
    L = S // dil
    blk = steps
    nb = -(-L // blk)
    Lp = nb * blk

    def by_stride(t):
        t = t.reshape(B, L, dil, H, D).transpose(0, 2, 1, 3, 4)
        return jnp.pad(t, ((0, 0), (0, 0), (0, Lp - L), (0, 0), (0, 0)))

    def band(t):
        tp = jnp.pad(t, ((0, 0), (0, 0), (blk, 0), (0, 0), (0, 0))).reshape(B, dil, nb + 1, blk, H, D)
        return jnp.concatenate([tp[:, :, :-1], tp[:, :, 1:]], axis=3)

    qb = by_stride(q).reshape(B, dil, nb, blk, H, D)
    kb = band(by_stride(k))
    vb = band(by_stride(v))

    s = jnp.einsum('bcnihd,bcnjhd->bcnhij', qb, kb).astype(jnp.float32) * (D ** -0.5)
    i = jnp.arange(blk)[:, None]
    j = jnp.arange(2 * blk)[None, :]
    diff = i + blk - j
    key_step = (jnp.arange(nb)[:, None] - 1) * blk + jnp.arange(2 * blk)[None, :]
    valid = ((diff >= 0) & (diff <= steps))[None, :, :] & (key_step >= 0)[:, None, :]
    s = jnp.where(valid[None, None, :, None, :, :], s, NEG_INF)

    m = jnp.max(s, axis=-1, keepdims=True)
    p = jnp.exp(s - m)
    den = jnp.sum(p, axis=-1, keepdims=True)
    o = jnp.einsum('bcnhij,bcnjhd->bcnihd', p / den, vb.astype(jnp.float32))
    lse = (m + jnp.log(den))[..., 0]

    o = o.reshape(B, dil, Lp, H, D)[:, :, :L].transpose(0, 2, 1, 3, 4).reshape(B, S, H, D)
    lse = lse.transpose(0, 1, 2, 4, 3).reshape(B, dil, Lp, H)[:, :, :L].transpose(0, 2, 1, 3).reshape(B, S, H)
    return o, lse


def dilated_mixer(h, w_in, w_out):
    B, S, _ = h.shape
    proj = (h @ w_in).reshape(B, S, N_GROUPS, 3, N_HEADS, HEAD_DIM)
    pos = jnp.arange(S, dtype=jnp.float32)
    outs, lses = [], []
    for g, (window, dil) in enumerate(DILATED_PATTERNS):
        q = partial_rotary(proj[:, :, g, 0], pos)
        k = partial_rotary(proj[:, :, g, 1], pos)
        v = proj[:, :, g, 2]
        o, lse = dilated_band_attention(q, k, v, dil, window // dil)
        outs.append(o)
        lses.append(lse)
    wts = jax.nn.softmax(jnp.stack(lses, axis=0), axis=0)
    o = jnp.einsum('gbsh,gbshd->bshd', wts, jnp.stack(outs, axis=0))
    return o.reshape(B, S, ATTN_WIDTH).astype(h.dtype) @ w_out


def forgetting_mixer(h, w_in, b_f, w_out):
    B, S, _ = h.shape
    proj = h @ w_in
    qkv = proj[..., :3 * ATTN_WIDTH].reshape(B, S, 3, N_HEADS, HEAD_DIM)
    q, k, v = qkv[:, :, 0], qkv[:, :, 1], qkv[:, :, 2]
    log_f = jax.nn.log_sigmoid(proj[..., 3 * ATTN_WIDTH:].astype(jnp.float32) + b_f.astype(jnp.float32))
    c = lax.cumsum(log_f, axis=1)
    nq = S // FOX_Q_BLOCK
    qb = q.reshape(B, nq, FOX_Q_BLOCK, N_HEADS, HEAD_DIM).transpose(1, 0, 2, 3, 4)
    cqb = c.reshape(B, nq, FOX_Q_BLOCK, N_HEADS).transpose(1, 0, 3, 2)
    ck = c.transpose(0, 2, 1)
    key_pos = jnp.arange(S)
    vf = v.astype(jnp.float32)
    scale = HEAD_DIM ** -0.5

    def block(args):
        qi, ci, n = args
        s = jnp.einsum('bihd,bjhd->bhij', qi, k).astype(jnp.float32) * scale
        s = s + ci[..., None] - ck[:, :, None, :]
        qpos = n * FOX_Q_BLOCK + jnp.arange(FOX_Q_BLOCK)
        s = jnp.where(key_pos[None, :] <= qpos[:, None], s, NEG_INF)
        p = jax.nn.softmax(s, axis=-1)
        return jnp.einsum('bhij,bjhd->bihd', p, vf)

    o = lax.map(block, (qb, cqb, jnp.arange(nq)))
    o = o.transpose(1, 0, 2, 3, 4).reshape(B, S, ATTN_WIDTH).astype(h.dtype)
    return o @ w_out


def swiglu(h, w_gu, w_down):
    gu = h @ w_gu
    g, u = gu[..., :D_FF], gu[..., D_FF:]
    return (jax.nn.silu(g) * u) @ w_down


def setup_inputs(seed: int = 0) -> dict:
    key = jax.random.key(seed)
    ks = jax.random.split(key, 13)
    f32 = jnp.float32
    x = jax.random.normal(ks[0], (BATCH, SEQ, D_MODEL), f32)
    a_norm = 1.0 + 0.02 * jax.random.normal(ks[1], (N_A_LAYERS, D_MODEL), f32)
    a_w_in = jax.random.normal(ks[2], (N_A_LAYERS, D_MODEL, N_GROUPS * 3 * ATTN_WIDTH), f32) * D_MODEL ** -0.5
    a_w_out = jax.random.normal(ks[3], (N_A_LAYERS, ATTN_WIDTH, D_MODEL), f32) * ATTN_WIDTH ** -0.5
    b_norm = 1.0 + 0.02 * jax.random.normal(ks[4], (N_B_LAYERS, D_MODEL), f32)
    b_w_in = jax.random.normal(ks[5], (N_B_LAYERS, D_MODEL, 3 * ATTN_WIDTH + N_HEADS), f32) * D_MODEL ** -0.5
    b_f = jnp.linspace(1.0, 6.0, N_HEADS, dtype=f32)[None, :] + 0.1 * jax.random.normal(ks[6], (N_B_LAYERS, N_HEADS), f32)
    b_w_out = jax.random.normal(ks[7], (N_B_LAYERS, ATTN_WIDTH, D_MODEL), f32) * ATTN_WIDTH ** -0.5
    ffn_norm = 1.0 + 0.02 * jax.random.normal(ks[8], (DEPTH, D_MODEL), f32)
    ffn_w_gu = jax.random.normal(ks[9], (DEPTH, D_MODEL, 2 * D_FF), f32) * D_MODEL ** -0.5
    ffn_w_down = jax.random.normal(ks[10], (DEPTH, D_FF, D_MODEL), f32) * D_FF ** -0.5
    final_norm = 1.0 + 0.02 * jax.random.normal(ks[11], (D_MODEL,), f32)
    return {"x": x, "a_norm": a_norm, "a_w_in": a_w_in, "a_w_out": a_w_out,
            "b_norm": b_norm, "b_w_in": b_w_in, "b_f": b_f, "b_w_out": b_w_out,
            "ffn_norm": ffn_norm, "ffn_w_gu": ffn_w_gu, "ffn_w_down": ffn_w_down,
            "final_norm": final_norm}


def reference(x, a_norm, a_w_in, a_w_out, b_norm, b_w_in, b_f, b_w_out,
              ffn_norm, ffn_w_gu, ffn_w_down, final_norm):
    h = x
    for i in range(DEPTH):
        j = i // N_MIXERS
        if i % N_MIXERS == 0:
            h = h + dilated_mixer(rmsnorm(h, a_norm[j]), a_w_in[j], a_w_out[j])
        else:
            h = h + forgetting_mixer(rmsnorm(h, b_norm[j]), b_w_in[j], b_f[j], b_w_out[j])
        h = h + swiglu(rmsnorm(h, ffn_norm[i]), ffn_w_gu[i], ffn_w_down[i])
    return rmsnorm(h, final_norm)
```

```python
from contextlib import ExitStack
import numpy as np
import ml_dtypes
import concourse.bass as bass
import concourse.mybir as mybir
from concourse.bass_utils import run_bass_kernel_spmd

F32 = mybir.dt.float32
BF16 = mybir.dt.bfloat16
AF = mybir.ActivationFunctionType
ALU = mybir.AluOpType

S = 4096
D = 1024
NT = 32
KC = 8
NH = 16
HD = 64
DFF = 2816
NJ = 22
DILS = (1, 4, 16)
EPS = 1e-6
NEG = -30000.0

ENGS = ["pe", "act", "dve", "pool", "sp"]
BLK = {"pe": "tensor", "act": "scalar", "dve": "vector", "pool": "gpsimd", "sp": "sync"}


class Op:
    __slots__ = ("eng", "fn", "reads", "writes", "is_dma", "pos", "waits", "signal",
                 "slot", "target", "clock", "rank", "barrier")

    def __init__(self, eng, fn, reads, writes, is_dma):
        self.eng = eng
        self.fn = fn
        self.reads = reads
        self.writes = writes
        self.is_dma = is_dma
        self.waits = []
        self.signal = False
        self.slot = None
        self.target = None
        self.clock = None
        self.rank = None
        self.barrier = False


class Phase:
    def __init__(self, name):
        self.name = name
        self.allocs = []
        self.ops = []

    def sb(self, name, shape, dt):
        self.allocs.append((name, "sb", list(shape), dt))
        return name

    def ps(self, name, shape, dt):
        self.allocs.append((name, "ps", list(shape), dt))
        return name

    def op(self, eng, fn, r=(), w=()):
        o = Op(eng, fn, tuple(r), tuple(w), False)
        self.ops.append(o)
        return o

    def dma(self, eng, fn, r=(), w=()):
        o = Op(eng, fn, tuple(r), tuple(w), True)
        self.ops.append(o)
        return o


class Prog:
    def __init__(self, nc, n_dsem=32):
        self.nc = nc
        self.T = {}
        self.phases = []
        self.K = n_dsem
        self.gallocs = []

    def phase(self, name):
        p = Phase(name)
        self.phases.append(p)
        return p

    def gsb(self, name, shape, dt):
        self.gallocs.append((name, "sb", list(shape), dt))

    def analyze(self):
        K = self.K
        res = {}
        know = {e: ({}, {}) for e in ENGS}
        cnt = {e: 0 for e in ENGS}
        last_op = {e: None for e in ENGS}
        slot_last = [None] * K
        slot_uses = [0] * K
        dma_i = 0
        all_ops = {e: [] for e in ENGS}

        def merge(dst, src):
            for k, v in src[0].items():
                if dst[0].get(k, -1) < v:
                    dst[0][k] = v
            for k, v in src[1].items():
                if dst[1].get(k, -1) < v:
                    dst[1][k] = v

        def need(E, X, P):
            if P is None or P is X:
                return
            kn = know[E]
            if P.is_dma:
                if kn[1].get(P.slot, 0) >= P.target:
                    return
                X.waits.append(P)
                kn[1][P.slot] = P.target
                merge(kn, P.clock)
            else:
                if P.eng == E and E == "pe":
                    return
                if kn[0].get(P.eng, -1) >= P.pos:
                    return
                P.signal = True
                X.waits.append(P)
                kn[0][P.eng] = P.pos
                merge(kn, P.clock)

        for ph in self.phases:
            for e in ENGS:
                b = Op(e, None, (), (), False)
                b.barrier = True
                ph.ops.append(b)
            for X in ph.ops:
                E = X.eng
                X.pos = cnt[E]
                cnt[E] += 1
                all_ops[E].append(X)
                if X.barrier:
                    for e2 in ENGS:
                        need(E, X, last_op[e2])
                    for s in range(K):
                        need(E, X, slot_last[s])
                    X.clock = ({}, {})
                    continue
                deps = []
                for k in X.reads:
                    ent = res.get(k)
                    if ent is not None:
                        deps.append(ent[0])
                for k in X.writes:
                    ent = res.get(k)
                    if ent is not None:
                        deps.append(ent[0])
                        deps.extend(ent[1])
                if X.is_dma:
                    s = dma_i % K
                    dma_i += 1
                    need(E, X, slot_last[s])
                    slot_uses[s] += 1
                    X.slot = s
                    X.target = 16 * slot_uses[s]
                    slot_last[s] = X
                for P in deps:
                    need(E, X, P)
                X.clock = (dict(know[E][0]), dict(know[E][1]))
                for k in X.reads:
                    ent = res.get(k)
                    if ent is None:
                        res[k] = [None, [X]]
                    else:
                        ent[1].append(X)
                for k in X.writes:
                    res[k] = [X, []]
                if not X.is_dma:
                    last_op[E] = X
        for e in ENGS:
            r = 0
            for o in all_ops[e]:
                if o.signal:
                    r += 1
                    o.rank = r
        self.stats = {e: (len(all_ops[e]), sum(1 for o in all_ops[e] if o.signal),
                          sum(len(o.waits) for o in all_ops[e])) for e in ENGS}

    def emit(self):
        nc = self.nc
        self.analyze()
        T = self.T
        with ExitStack() as st:
            sem = {e: st.enter_context(nc.semaphore("s_" + e)) for e in ENGS}
            dsem = [st.enter_context(nc.semaphore("d%d" % i)) for i in range(self.K)]
            for (name, kind, shape, dt) in self.gallocs:
                T[name] = st.enter_context(nc.sbuf_tensor("t_" + name, shape, dt))
            for ph in self.phases:
                with ExitStack() as st2:
                    for (name, kind, shape, dt) in ph.allocs:
                        if kind == "sb":
                            T[name] = st2.enter_context(nc.sbuf_tensor("t_%s_%s" % (ph.name, name), shape, dt))
                        else:
                            T[name] = st2.enter_context(nc.psum_tensor("t_%s_%s" % (ph.name, name), shape, dt))
                    with nc.Block() as blk:
                        for e in ENGS:
                            ops = [o for o in ph.ops if o.eng == e]

                            def body(eng, ops=ops, e=e):
                                for o in ops:
                                    for P in o.waits:
                                        if P.is_dma:
                                            eng.wait_ge(dsem[P.slot], P.target)
                                        else:
                                            eng.wait_ge(sem[P.eng], P.rank)
                                    if o.barrier:
                                        continue
                                    ins = o.fn(eng, T)
                                    if o.is_dma:
                                        ins.then_inc(dsem[o.slot], 16)
                                    elif o.signal:
                                        ins.then_inc(sem[e], 1)

                            getattr(blk, BLK[e])(body)
                    for (name, kind, shape, dt) in ph.allocs:
                        T.pop(name, None)


def _bf(a):
    return np.ascontiguousarray(a.astype(ml_dtypes.bfloat16))


def perm_tokens(g):
    dil = DILS[g]
    L = S // dil
    n = np.arange(S)
    return (n % L) * dil + (n // L)


def host_consts():
    c = {}
    c["ident"] = _bf(np.eye(128, dtype=np.float32))
    c["identf"] = np.eye(128, dtype=np.float32)
    r = np.arange(128)
    partner = np.where(r % 16 < 8, r + 8, r - 8)
    pw = np.zeros((128, 128), np.float32)
    pw[partner, r] = 1.0
    c["pswap"] = _bf(pw)
    half = 8
    inv_freq = (np.float32(500000.0) ** (-np.arange(half, dtype=np.float32) * np.float32(2.0) / np.float32(16))).astype(np.float32)
    tabs = []
    for g in range(3):
        pos = perm_tokens(g).astype(np.float32)
        ang = (pos[:, None] * inv_freq[None, :]).astype(np.float32)
        cos = np.cos(ang).astype(np.float32).T
        sin = np.sin(ang).astype(np.float32).T
        fi = r % 8
        sgn = np.where(r % 16 < 8, -1.0, 1.0).astype(np.float32)
        tab = np.stack([cos[fi], sin[fi] * sgn[:, None]], axis=1)
        tabs.append(tab.astype(np.float32))
    c["rottab"] = np.ascontiguousarray(np.stack(tabs, 0))
    k = np.arange(128)[:, None]
    q = np.arange(128)[None, :]
    ncur = np.where(k > q, NEG, 0.0).astype(np.float32)
    nprev = np.where(k < q, NEG, 0.0).astype(np.float32)
    nall = np.full((128, 128), NEG, np.float32)
    z = np.zeros((128, 128), np.float32)
    m0 = np.concatenate([ncur, nprev, ncur, nprev], 1)
    m1 = np.concatenate([ncur, nall, ncur, nprev], 1)
    m2 = np.concatenate([ncur, z, z, z], 1)
    c["masks"] = _bf(np.stack([m0, m1, m2], 1))
    return c


def build(stop_after=None, dbg=()):
    nc = bass.Bass("TRN2", target_bir_lowering=False)
    dbg = set(dbg)

    def din(name, shape, dt):
        return nc.dram_tensor(name, list(shape), dt, kind="ExternalInput").ap()

    def dscr(name, shape, dt):
        kind = "ExternalOutput" if name in dbg else "Internal"
        return nc.dram_tensor(name, list(shape), dt, kind=kind).ap()

    x = din("x", [S, D], F32)
    a_norm = din("a_norm", [1, D], F32)
    a_w_in = din("a_w_in", [D, 9216], F32)
    a_w_out = din("a_w_out", [D, D], F32)
    b_norm = din("b_norm", [1, D], F32)
    b_w_in = din("b_w_in", [D, 3088], F32)
    b_f = din("b_f", [NH, 1], F32)
    b_w_out = din("b_w_out", [D, D], F32)
    ffn_norm = din("ffn_norm", [2, D], F32)
    ffn_w_gu = din("ffn_w_gu", [2, D, 2 * DFF], F32)
    ffn_w_down = din("ffn_w_down", [2, DFF, D], F32)
    final_norm = din("final_norm", [1, D], F32)
    c_ident = din("ident", [128, 128], BF16)
    c_identf = din("identf", [128, 128], F32)
    c_pswap = din("pswap", [128, 128], BF16)
    c_rottab = din("rottab", [3, 128, 2, S], F32)
    c_masks = din("masks", [128, 3, 512], BF16)

    out = nc.dram_tensor("out", [S, D], F32, kind="ExternalOutput").ap()

    qt0 = [dscr("qt0_%d" % g, [NH, HD, S], BF16) for g in range(3)]
    kt0 = [dscr("kt0_%d" % g, [NH, HD, S], BF16) for g in range(3)]
    v0 = [dscr("v0_%d" % g, [S, D], BF16) for g in range(3)]
    att_scr = dscr("att_scr", [D, S], BF16)
    h_scr = dscr("h_scr", [S, D], F32)
    qt1 = dscr("qt1", [NH, HD + 1, S], BF16)
    kt1 = dscr("kt1", [NH, HD + 1, S], BF16)
    v1 = dscr("v1", [S, D], BF16)
    cs_scr = dscr("cs_scr", [128, NT * NH], F32)
    rden_scr = dscr("rden_scr", [NH, S], F32)

    pg = Prog(nc)
    pg.gsb("xnT", [128, KC, S], BF16)
    pg.gsb("ident", [128, 128], BF16)
    pg.gsb("gam", [128, D], F32)

    def tok_rows(src, g, j):
        dil = DILS[g]
        L = S // dil
        n0 = 128 * j
        r = n0 // L
        p0 = n0 % L
        start = p0 * dil + r
        if dil == 1:
            return src[start:start + 128, :]
        return src[start:start + 127 * dil + 1:dil, :]

    def load_consts(P):
        P.dma("sp", lambda e, T: e.dma_start(out=T["ident"][:, :], in_=c_ident[:, :]), w=["ident"])

    def norm_phase(name, src, gamma_row, g, transpose=True, dst=None):
        P = pg.phase(name)
        P.sb("n_ht", [128, 3, D], F32)
        P.sb("n_junk", [128, D], BF16)
        P.sb("n_ss", [128, NT], F32)
        P.sb("n_rstd", [128, NT], F32)
        if transpose:
            P.sb("n_xn", [128, 2, D], BF16)
            P.ps("n_pt0", [128, D], BF16)
            P.ps("n_pt1", [128, D], BF16)
        else:
            P.sb("n_o", [128, 2, D], F32)
        P.dma("sp", lambda e, T: e.dma_start(out=T["gam"][:, :], in_=gamma_row.partition_broadcast(128)),
              w=["gam"])
        for j in range(NT):
            b = j % 3
            P.dma("sp", lambda e, T, j=j, b=b: e.dma_start(out=T["n_ht"][:, b, :], in_=tok_rows(src, g, j)),
                  w=[("n_ht", b)])
            P.op("act", lambda e, T, j=j, b=b: e.activation(out=T["n_junk"][:, :], in_=T["n_ht"][:, b, :],
                                                             func=AF.Square, accum_out=T["n_ss"][:, j:j + 1]),
                 r=[("n_ht", b)], w=[("n_ss", j)])
        allss = [("n_ss", j) for j in range(NT)]
        P.op("dve", lambda e, T: e.tensor_scalar(out=T["n_rstd"][:, :], in0=T["n_ss"][:, :], scalar1=1.0 / D,
                                                  scalar2=EPS, op0=ALU.mult, op1=ALU.add),
             r=allss, w=["n_var"])
        P.op("act", lambda e, T: e.activation(out=T["n_rstd"][:, :], in_=T["n_rstd"][:, :], func=AF.Sqrt),
             r=["n_var"], w=["n_std"])
        P.op("dve", lambda e, T: e.reciprocal(out=T["n_rstd"][:, :], in_=T["n_rstd"][:, :]),
             r=["n_std"], w=["n_rstd"])
        for j in range(NT):
            b = j % 3
            P.dma("sp", lambda e, T, j=j, b=b: e.dma_start(out=T["n_ht"][:, b, :], in_=tok_rows(src, g, j)),
                  w=[("n_ht", b)])
            if transpose:
                xb = j % 2
                P.op("dve", lambda e, T, j=j, b=b, xb=xb: e.scalar_tensor_tensor(
                    out=T["n_xn"][:, xb, :], in0=T["n_ht"][:, b, :], scalar=T["n_rstd"][:, j:j + 1],
                    in1=T["gam"][:, :], op0=ALU.mult, op1=ALU.mult),
                    r=[("n_ht", b), "n_rstd", "gam"], w=[("n_xn", xb)])
                pt = "n_pt%d" % xb
                for k in range(KC):
                    P.op("pe", lambda e, T, k=k, xb=xb, pt=pt: e.transpose(
                        out=T[pt][:, k * 128:(k + 1) * 128], in_=T["n_xn"][:, xb, k * 128:(k + 1) * 128],
                        identity=T["ident"][:, :]),
                        r=[("n_xn", xb), "ident"], w=[pt])
                P.op("act", lambda e, T, j=j, pt=pt: e.activation(
                    out=T["xnT"][:, :, j * 128:(j + 1) * 128],
                    in_=T[pt][:, :].rearrange("p (k t) -> p k t", k=KC), func=AF.Copy),
                    r=[pt], w=[("xnT", j)])
            else:
                ob = j % 2
                P.op("dve", lambda e, T, j=j, b=b, ob=ob: e.scalar_tensor_tensor(
                    out=T["n_o"][:, ob, :], in0=T["n_ht"][:, b, :], scalar=T["n_rstd"][:, j:j + 1],
                    in1=T["gam"][:, :], op0=ALU.mult, op1=ALU.mult),
                    r=[("n_ht", b), "n_rstd", "gam"], w=[("n_o", ob)])
                P.dma("pool", lambda e, T, j=j, ob=ob: e.dma_start(out=dst[j * 128:(j + 1) * 128, :],
                                                                    in_=T["n_o"][:, ob, :]),
                      r=[("n_o", ob)], w=[("dst", j)])
        return P

    def load_w_slab(P, wname, stname, wsrc_cols, slab, rotperm):
        for k in range(KC):
            sb_ = k % 2
            P.dma("sp", lambda e, T, k=k, sb_=sb_: e.dma_start(out=T[stname][:, sb_, :], in_=wsrc_cols(k)),
                  w=[(stname, sb_)])
            if rotperm:
                P.op("dve", lambda e, T, k=k, sb_=sb_: e.tensor_copy(
                    out=T[wname][:, slab, k, 0:256].rearrange("p (h d) -> p h d", d=16),
                    in_=T[stname][:, sb_, :].rearrange("p (h d) -> p h d", d=64)[:, :, 0:16]),
                    r=[(stname, sb_)], w=[(wname, slab, k, "a")])
                P.op("dve", lambda e, T, k=k, sb_=sb_: e.tensor_copy(
                    out=T[wname][:, slab, k, 256:1024].rearrange("p (h d) -> p h d", d=48),
                    in_=T[stname][:, sb_, :].rearrange("p (h d) -> p h d", d=64)[:, :, 16:64]),
                    r=[(stname, sb_)], w=[(wname, slab, k, "b")])
            else:
                P.op("dve", lambda e, T, k=k, sb_=sb_: e.tensor_copy(out=T[wname][:, slab, k, :],
                                                                      in_=T[stname][:, sb_, :]),
                     r=[(stname, sb_)], w=[(wname, slab, k, "a"), (wname, slab, k, "b")])

    def wkeys(wname, slab):
        ks = []
        for k in range(KC):
            ks.append((wname, slab, k, "a"))
            ks.append((wname, slab, k, "b"))
        return ks

    def seg_list(c):
        segs = []
        if c < 2:
            for hh in range(8):
                segs.append((hh * 16, 16, 8 * c + hh, 0))
        else:
            f0 = 128 * (c - 2)
            f = f0
            while f < f0 + 128:
                h = f // 48
                dd = f % 48
                n = min(48 - dd, f0 + 128 - f)
                segs.append((f - f0, n, h, 16 + dd))
                f += n
        return segs

    def inproj_phase(name, g, wcols, qt_dst, kt_dst, v_dst, tabg, extra=None, rotary=True):
        P = pg.phase(name)
        P.sb("w_st", [128, 2, 1024], F32)
        P.sb("w_sl", [128, 2, KC, 1024], BF16)
        P.sb("stage", [128, 2, S], BF16)
        P.sb("raw", [128, 2, 512], BF16)
        P.sb("cs", [128, 2, 2, 512], F32)
        P.sb("t1", [128, 2, 512], F32)
        P.sb("t2", [128, 2, 512], F32)
        P.sb("pswap", [128, 128], BF16)
        P.sb("vst", [128, 2, D], BF16)
        for i in range(3):
            P.ps("pq%d" % i, [128, 512], F32)
        for i in range(2):
            P.ps("psw%d" % i, [128, 512], F32)
        P.dma("sp", lambda e, T: e.dma_start(out=T["pswap"][:, :], in_=c_pswap[:, :]), w=["pswap"])
        cnt = {"pq": 0, "rot": 0, "stage": 0}
        slab_i = [0]

        def next_slab(t, rotperm):
            sl = slab_i[0] % 2
            slab_i[0] += 1
            load_w_slab(P, "w_sl", "w_st", lambda k, t=t: wcols(t, k), sl, rotperm)
            return sl

        pending = []

        def flush_pending():
            while pending:
                pending.pop(0)()

        sl_next = next_slab(0, True)
        for t in range(2):
            sl = sl_next
            sl_next = next_slab(t + 1, t + 1 < 2)
            dst = qt_dst if t == 0 else kt_dst
            for c in range(8):
                sg = cnt["stage"] % 2
                cnt["stage"] += 1
                for tt in range(8):
                    pi = cnt["pq"] % 3
                    cnt["pq"] += 1
                    pq = "pq%d" % pi
                    xk = [("xnT", 4 * tt + i) for i in range(4)]
                    for k in range(KC):
                        P.op("pe", lambda e, T, k=k, sl=sl, c=c, tt=tt, pq=pq: e.matmul(
                            T[pq][:, :], lhsT=T["w_sl"][:, sl, k, c * 128:(c + 1) * 128],
                            rhs=T["xnT"][:, k, tt * 512:(tt + 1) * 512], start=(k == 0), stop=(k == KC - 1)),
                            r=xk + [("w_sl", sl, k, "a" if c < 2 else "b")], w=[pq])
                    if c >= 2 or not rotary:
                        P.op("act", lambda e, T, sg=sg, tt=tt, pq=pq: e.activation(
                            out=T["stage"][:, sg, tt * 512:(tt + 1) * 512], in_=T[pq][:, :], func=AF.Copy),
                            r=[pq], w=[("stage", sg, tt)])
                    else:
                        ri = cnt["rot"] % 2
                        cnt["rot"] += 1
                        P.op("act", lambda e, T, ri=ri, pq=pq: e.activation(
                            out=T["raw"][:, ri, :], in_=T[pq][:, :], func=AF.Copy),
                            r=[pq], w=[("raw", ri)])
                        P.dma("sp", lambda e, T, ri=ri, tt=tt: e.dma_start(
                            out=T["cs"][:, ri, :, :], in_=c_rottab[tabg, :, :, tt * 512:(tt + 1) * 512]),
                            w=[("cs", ri)])

                        def rot(ri=ri, sg=sg, tt=tt):
                            psw = "psw%d" % ri
                            P.op("pe", lambda e, T: e.matmul(T[psw][:, :], lhsT=T["pswap"][:, :],
                                                              rhs=T["raw"][:, ri, :], start=True, stop=True),
                                 r=[("raw", ri), "pswap"], w=[psw])
                            P.op("dve", lambda e, T: e.tensor_tensor(out=T["t1"][:, ri, :], in0=T["raw"][:, ri, :],
                                                                      in1=T["cs"][:, ri, 0, :], op=ALU.mult),
                                 r=[("raw", ri), ("cs", ri)], w=[("t1", ri)])
                            P.op("dve", lambda e, T: e.tensor_tensor(out=T["t2"][:, ri, :], in0=T[psw][:, :],
                                                                      in1=T["cs"][:, ri, 1, :], op=ALU.mult),
                                 r=[psw, ("cs", ri)], w=[("t2", ri)])
                            P.op("dve", lambda e, T: e.tensor_tensor(
                                out=T["stage"][:, sg, tt * 512:(tt + 1) * 512], in0=T["t1"][:, ri, :],
                                in1=T["t2"][:, ri, :], op=ALU.add),
                                r=[("t1", ri), ("t2", ri)], w=[("stage", sg, tt)])
                        flush_pending()
                        pending.append(rot)
                flush_pending()
                for (r0, n, h, d0) in seg_list(c):
                    P.dma("pool", lambda e, T, sg=sg, r0=r0, n=n, h=h, d0=d0, dst=dst: e.dma_start(
                        out=dst[h, d0:d0 + n, :], in_=T["stage"][r0:r0 + n, sg, :]),
                        r=[("stage", sg, tt) for tt in range(8)], w=[("qkdst", t, h, d0)])
        sl = sl_next
        for b in range(NT):
            vb = b % 2
            for s in range(2):
                pi = cnt["pq"] % 3
                cnt["pq"] += 1
                pq = "pq%d" % pi
                for k in range(KC):
                    P.op("pe", lambda e, T, k=k, sl=sl, b=b, s=s, pq=pq: e.matmul(
                        T[pq][:, :], lhsT=T["xnT"][:, k, b * 128:(b + 1) * 128],
                        rhs=T["w_sl"][:, sl, k, s * 512:(s + 1) * 512], start=(k == 0), stop=(k == KC - 1)),
                        r=[("xnT", b), ("w_sl", sl, k, "a"), ("w_sl", sl, k, "b")], w=[pq])
                P.op("act", lambda e, T, vb=vb, s=s, pq=pq: e.activation(
                    out=T["vst"][:, vb, s * 512:(s + 1) * 512], in_=T[pq][:, :], func=AF.Copy),
                    r=[pq], w=[("vst", vb, s)])
            P.dma("pool", lambda e, T, vb=vb, b=b: e.dma_start(out=v_dst[b * 128:(b + 1) * 128, :],
                                                                in_=T["vst"][:, vb, :]),
                  r=[("vst", vb, 0), ("vst", vb, 1)], w=[("vdst", b)])
        if extra is not None:
            extra(P)
        return P

    def att_alloc(P, krows):
        P.sb("mk", [128, 3, 512], BF16)
        P.sb("qT", [krows, 2, S], BF16)
        P.sb("kT", [krows, 2, S], BF16)
        P.sb("vv", [128, 2, NT, HD + 2], BF16)
        P.sb("pt", [128, 3, 512], BF16)
        P.sb("acc", [HD + 1, 2, S], F32)
        P.sb("bc", [HD, S], F32)
        P.sb("ot", [HD, 2, S], BF16)
        for i in range(3):
            P.ps("pss%d" % i, [128, 512], F32)
        for i in range(2):
            P.ps("pso%d" % i, [128, 512], F32)
        P.dma("sp", lambda e, T: e.dma_start(out=T["mk"][:, :, :], in_=c_masks[:, :, :]), w=["mk"])
        for b in range(2):
            P.op("dve", lambda e, T, b=b: e.memset(T["vv"][:, b, :, :], 1.0), w=[("vv1", b), ("vv", b)])

    def att_load(P, b, qsrc, ksrc, vsrc, h, krows):
        P.dma("sp", lambda e, T: e.dma_start(out=T["qT"][0:krows, b, :], in_=qsrc[h, :, :]), w=[("qT", b)])
        P.dma("sp", lambda e, T: e.dma_start(out=T["kT"][0:krows, b, :], in_=ksrc[h, :, :]), w=[("kT", b)])
        P.dma("sp", lambda e, T: e.dma_start(
            out=T["vv"][:, b, :, 0:HD],
            in_=vsrc[:, h * HD:(h + 1) * HD].rearrange("(n p) d -> p n d", p=128)), w=[("vv", b)])

    def att_finish(P, h, a):
        P.op("dve", lambda e, T: e.reciprocal(out=T["acc"][HD:HD + 1, a, :], in_=T["acc"][HD:HD + 1, a, :]),
             r=[("acc", a)], w=[("acc", a)])
        P.dma("pool", lambda e, T: e.dma_start(out=rden_scr[h:h + 1, :], in_=T["acc"][HD:HD + 1, a, :]),
              r=[("acc", a)], w=[("rden", h)])
        P.dma("sp", lambda e, T: e.dma_start(out=T["bc"][:, :],
                                              in_=rden_scr[h:h + 1, :].partition_broadcast(HD)),
              r=[("rden", h)], w=["bc"])
        P.op("dve", lambda e, T: e.tensor_tensor(out=T["ot"][:, a, :], in0=T["acc"][0:HD, a, :],
                                                  in1=T["bc"][:, :], op=ALU.mult),
             r=[("acc", a), "bc"], w=[("ot", a)])
        P.dma("pool", lambda e, T: e.dma_start(out=att_scr[h * HD:(h + 1) * HD, :], in_=T["ot"][:, a, :]),
              r=[("ot", a)], w=[("att_scr", h)])

    def l0_att_phase():
        P = pg.phase("l0att")
        att_alloc(P, HD)
        units = [(h, g) for h in range(NH) for g in range(3)]
        att_load(P, 0, qt0[units[0][1]], kt0[units[0][1]], v0[units[0][1]], units[0][0], HD)
        cs_ = {"s": 0, "p": 0}
        def unit(ui, h, g, b, a):
            if ui + 1 < len(units):
                h2, g2 = units[ui + 1]
                att_load(P, 1 - b, qt0[g2], kt0[g2], v0[g2], h2, HD)
            dil = DILS[g]
            nbc = NT // dil
            L = S // dil
            ld = [("qT", b), ("kT", b)]
            for pr in range(16):
                n0 = 2 * pr
                mi = 1 if (n0 % nbc) == 0 else 0
                si = cs_["s"] % 3
                cs_["s"] += 1
                pss = "pss%d" % si
                mms = []
                for e_ in range(2):
                    n = n0 + e_
                    m = n % nbc
                    mms.append((e_ * 256, n, n))
                    if m > 0:
                        mms.append((e_ * 256 + 128, n - 1, n))
                P.op("pe", lambda e, T, pss=pss, mi=mi: e.matmul(T[pss][:, :], lhsT=T["ident"][:, :],
                                                                  rhs=T["mk"][:, mi, :], start=True, stop=False),
                     r=["ident", "mk"], w=[pss])
                for ii, (c0, kb, qb) in enumerate(mms):
                    P.op("pe", lambda e, T, pss=pss, c0=c0, kb=kb, qb=qb, last=(ii == len(mms) - 1): e.matmul(
                        T[pss][:, c0:c0 + 128], lhsT=T["kT"][0:HD, b, kb * 128:(kb + 1) * 128],
                        rhs=T["qT"][0:HD, b, qb * 128:(qb + 1) * 128], start=False, stop=last),
                        r=ld, w=[pss])
                pi = cs_["p"] % 3
                cs_["p"] += 1
                P.op("act", lambda e, T, pss=pss, pi=pi: e.activation(out=T["pt"][:, pi, :], in_=T[pss][:, :],
                                                                       func=AF.Exp, scale=0.125),
                     r=[pss], w=[("pt", pi)])
                pso = "pso%d" % ((pr // 2) % 2)
                for e_ in range(2):
                    n = n0 + e_
                    m = n % nbc
                    cols = (n % 4) * 128
                    P.op("pe", lambda e, T, pso=pso, cols=cols, n=n, pi=pi, e_=e_, m=m: e.matmul(
                        T[pso][0:HD + 1, cols:cols + 128], lhsT=T["vv"][:, b, n, 0:HD + 1],
                        rhs=T["pt"][:, pi, e_ * 256:e_ * 256 + 128], start=True, stop=(m == 0)),
                        r=[("vv", b), ("vv1", b), ("pt", pi)], w=[pso])
                    if m > 0:
                        P.op("pe", lambda e, T, pso=pso, cols=cols, n=n, pi=pi, e_=e_: e.matmul(
                            T[pso][0:HD + 1, cols:cols + 128], lhsT=T["vv"][:, b, n - 1, 0:HD + 1],
                            rhs=T["pt"][:, pi, e_ * 256 + 128:e_ * 256 + 256], start=False, stop=True),
                            r=[("vv", b), ("vv1", b), ("pt", pi)], w=[pso])
                if pr % 2 == 1:
                    B = pr // 2
                    if g == 0:
                        P.op("dve", lambda e, T, pso=pso, B=B: e.tensor_copy(
                            out=T["acc"][:, a, B * 512:(B + 1) * 512], in_=T[pso][0:HD + 1, :]),
                            r=[pso], w=[("acc", a)])
                    else:
                        nsub = 512 // min(512, L)
                        w_ = 512 // nsub
                        for sub in range(nsub):
                            npos = 512 * B + sub * w_
                            r_ = npos // L
                            p0 = npos % L
                            st_ = p0 * dil + r_
                            en_ = st_ + (w_ - 1) * dil + 1
                            P.op("dve", lambda e, T, pso=pso, st_=st_, en_=en_, sub=sub, w_=w_, dil=dil: e.tensor_tensor(
                                out=T["acc"][:, a, st_:en_:dil], in0=T["acc"][:, a, st_:en_:dil],
                                in1=T[pso][0:HD + 1, sub * w_:(sub + 1) * w_], op=ALU.add),
                                r=[pso, ("acc", a)], w=[("acc", a)])
            if g == 2:
                att_finish(P, h, a)

        for ui, (h, g) in enumerate(units):
            unit(ui, h, g, ui % 2, h % 2)
        return P

    def outproj_phase(name, wsrc, res_src, res_dst):
        P = pg.phase(name)
        P.sb("w_st", [128, 2, D], F32)
        P.sb("wo", [128, KC, D], BF16)
        P.sb("xt", [128, 3, D], F32)
        for i in range(4):
            P.ps("po%d" % i, [128, 512], F32)
        for k in range(KC):
            P.dma("sp", lambda e, T, k=k: e.dma_start(out=T["xnT"][:, k, :], in_=att_scr[k * 128:(k + 1) * 128, :]),
                  w=[("attT", k)])
        for k in range(KC):
            sb_ = k % 2
            P.dma("sp", lambda e, T, k=k, sb_=sb_: e.dma_start(out=T["w_st"][:, sb_, :],
                                                                in_=wsrc[k * 128:(k + 1) * 128, :]),
                  w=[("w_st", sb_)])
            P.op("dve", lambda e, T, k=k, sb_=sb_: e.tensor_copy(out=T["wo"][:, k, :], in_=T["w_st"][:, sb_, :]),
                 r=[("w_st", sb_)], w=[("wo", k)])
        for j in range(NT):
            xb = j % 3
            P.dma("sp", lambda e, T, j=j, xb=xb: e.dma_start(out=T["xt"][:, xb, :],
                                                              in_=res_src[j * 128:(j + 1) * 128, :]),
                  r=[("hs", j)], w=[("xt", xb)])
            for s_ in range(2):
                po = "po%d" % ((2 * j + s_) % 4)
                for k in range(KC):
                    P.op("pe", lambda e, T, j=j, s_=s_, k=k, po=po: e.matmul(
                        T[po][:, :], lhsT=T["xnT"][:, k, j * 128:(j + 1) * 128],
                        rhs=T["wo"][:, k, s_ * 512:(s_ + 1) * 512], start=(k == 0), stop=(k == KC - 1)),
                        r=[("attT", k), ("wo", k)], w=[po])
                P.op("dve", lambda e, T, xb=xb, s_=s_, po=po: e.tensor_tensor(
                    out=T["xt"][:, xb, s_ * 512:(s_ + 1) * 512], in0=T["xt"][:, xb, s_ * 512:(s_ + 1) * 512],
                    in1=T[po][:, :], op=ALU.add),
                    r=[po, ("xt", xb)], w=[("xt", xb)])
            P.dma("pool", lambda e, T, j=j, xb=xb: e.dma_start(out=res_dst[j * 128:(j + 1) * 128, :],
                                                                in_=T["xt"][:, xb, :]),
                  r=[("xt", xb)], w=[("hs", j)])
        return P

    def ffn_phase(name, li):
        P = pg.phase(name)
        P.sb("wd", [128, NJ, D], BF16)
        P.sb("actT", [128, NJ, 1024], BF16)
        P.sb("wgs", [128, 2, 2048], F32)
        P.sb("wg", [128, 2, 2048], BF16)
        P.sb("sg", [128, 2, 512], F32)
        P.sb("xt", [128, 2, D], F32)
        for i in range(8):
            P.ps("pb%d" % i, [128, 512], F32)
        wgu = ffn_w_gu[li]
        wdn = ffn_w_down[li]
        c = {"st": 0, "wg": 0, "sg": 0, "xt": 0}

        def stg():
            b = c["st"] % 2
            c["st"] += 1
            return b

        for j in range(NJ):
            b = stg()
            P.dma("sp", lambda e, T, j=j, b=b: e.dma_start(out=T["wgs"][:, b, 0:D], in_=wdn[j * 128:(j + 1) * 128, :]),
                  w=[("wgs", b)])
            P.op("dve", lambda e, T, j=j, b=b: e.tensor_copy(out=T["wd"][:, j, :], in_=T["wgs"][:, b, 0:D]),
                 r=[("wgs", b)], w=[("wd", j)])

        def load_gu(j):
            b = stg()
            b2 = c["wg"] % 2
            c["wg"] += 1
            P.dma("sp", lambda e, T: e.dma_start(
                out=T["wgs"][:, b, :].rearrange("p (k c) -> p k c", k=KC)[:, :, 0:128],
                in_=wgu[:, j * 128:(j + 1) * 128].rearrange("(k p) c -> p k c", p=128)), w=[("wgs", b, 0)])
            P.dma("sp", lambda e, T: e.dma_start(
                out=T["wgs"][:, b, :].rearrange("p (k c) -> p k c", k=KC)[:, :, 128:256],
                in_=wgu[:, DFF + j * 128:DFF + (j + 1) * 128].rearrange("(k p) c -> p k c", p=128)),
                w=[("wgs", b, 1)])
            P.op("dve", lambda e, T: e.tensor_copy(out=T["wg"][:, b2, :], in_=T["wgs"][:, b, :]),
                 r=[("wgs", b, 0), ("wgs", b, 1), ("wgs", b)], w=[("wg", b2), ("wgs", b)])
            return b2

        for st_ in range(4):
            nxt = load_gu(0)
            for j in range(NJ):
                b2 = nxt
                if j + 1 < NJ:
                    nxt = load_gu(j + 1)
                set_ = j % 2
                for half in range(2):
                    tok0 = st_ * 1024 + half * 512
                    xk = [("xnT", tok0 // 128 + i) for i in range(4)]
                    pgb = "pb%d" % (4 * set_ + half)
                    pub = "pb%d" % (4 * set_ + 2 + half)
                    for (bank, off) in ((pgb, 0), (pub, 128)):
                        for k in range(KC):
                            P.op("pe", lambda e, T, bank=bank, off=off, k=k, b2=b2, tok0=tok0: e.matmul(
                                T[bank][:, :],
                                lhsT=T["wg"][:, b2, :].rearrange("p (k c) -> p k c", k=KC)[:, k, off:off + 128],
                                rhs=T["xnT"][:, k, tok0:tok0 + 512], start=(k == 0), stop=(k == KC - 1)),
                                r=xk + [("wg", b2)], w=[bank])
                    sgi = c["sg"] % 2
                    c["sg"] += 1
                    P.op("act", lambda e, T, pgb=pgb, sgi=sgi: e.activation(out=T["sg"][:, sgi, :], in_=T[pgb][:, :],
                                                                             func=AF.Silu),
                         r=[pgb], w=[("sg", sgi)])
                    P.op("dve", lambda e, T, pub=pub, sgi=sgi, j=j, half=half: e.tensor_tensor(
                        out=T["actT"][:, j, half * 512:(half + 1) * 512], in0=T["sg"][:, sgi, :], in1=T[pub][:, :],
                        op=ALU.mult),
                        r=[pub, ("sg", sgi)], w=[("actT", j, half)])
            for tb in range(8):
                jt = st_ * 8 + tb
                xb = c["xt"] % 2
                c["xt"] += 1
                P.dma("sp", lambda e, T, jt=jt, xb=xb: e.dma_start(out=T["xt"][:, xb, :],
                                                                    in_=h_scr[jt * 128:(jt + 1) * 128, :]),
                      r=[("hs", jt)], w=[("xt", xb)])
                for s_ in range(2):
                    bank = "pb%d" % ((2 * tb + s_) % 8)
                    for j in range(NJ):
                        P.op("pe", lambda e, T, bank=bank, j=j, tb=tb, s_=s_: e.matmul(
                            T[bank][:, :], lhsT=T["actT"][:, j, tb * 128:(tb + 1) * 128],
                            rhs=T["wd"][:, j, s_ * 512:(s_ + 1) * 512], start=(j == 0), stop=(j == NJ - 1)),
                            r=[("actT", j, tb // 4), ("wd", j)], w=[bank])
                    P.op("dve", lambda e, T, bank=bank, xb=xb, s_=s_: e.tensor_tensor(
                        out=T["xt"][:, xb, s_ * 512:(s_ + 1) * 512], in0=T["xt"][:, xb, s_ * 512:(s_ + 1) * 512],
                        in1=T[bank][:, :], op=ALU.add),
                        r=[bank, ("xt", xb)], w=[("xt", xb)])
                P.dma("pool", lambda e, T, jt=jt, xb=xb: e.dma_start(out=h_scr[jt * 128:(jt + 1) * 128, :],
                                                                      in_=T["xt"][:, xb, :]),
                      r=[("xt", xb)], w=[("hs", jt)])
        return P

    def gate_extra(P):
        P.sb("wfs", [128, KC, NH], F32)
        P.sb("wfb", [128, KC, NH], BF16)
        P.sb("negb", [NH, 1], F32)
        P.sb("lf", [NH, S], F32)
        P.sb("csum", [NH, S], F32)
        P.sb("qrow", [NH, S], BF16)
        P.sb("identf", [128, 128], F32)
        P.sb("cst", [128, NT * NH], F32)
        P.ps("psf", [128, 512], F32)
        P.ps("pct", [128, 512], F32)
        P.dma("sp", lambda e, T: e.dma_start(out=T["identf"][:, :], in_=c_identf[:, :]), w=["identf"])
        P.dma("sp", lambda e, T: e.dma_start(
            out=T["wfs"][:, :, :], in_=b_w_in[:, 3 * D:3 * D + NH].rearrange("(k p) c -> p k c", p=128)), w=["wfs"])
        P.op("dve", lambda e, T: e.tensor_copy(out=T["wfb"][:, :, :], in_=T["wfs"][:, :, :]), r=["wfs"], w=["wfb"])
        P.dma("sp", lambda e, T: e.dma_start(out=T["negb"][:, :], in_=b_f[:, :]), w=["negb0"])
        P.op("dve", lambda e, T: e.tensor_scalar(out=T["negb"][:, :], in0=T["negb"][:, :], scalar1=-1.0, scalar2=None,
                                                  op0=ALU.mult), r=["negb0"], w=["negb"])
        for tt in range(8):
            xk = [("xnT", 4 * tt + i) for i in range(4)]
            for k in range(KC):
                P.op("pe", lambda e, T, k=k, tt=tt: e.matmul(
                    T["psf"][0:NH, :], lhsT=T["wfb"][:, k, :], rhs=T["xnT"][:, k, tt * 512:(tt + 1) * 512],
                    start=(k == 0), stop=(k == KC - 1)), r=xk + ["wfb"], w=["psf"])
            P.op("act", lambda e, T, tt=tt: e.activation(out=T["lf"][:, tt * 512:(tt + 1) * 512], in_=T["psf"][0:NH, :],
                                                          func=AF.Exp, bias=T["negb"][:, 0:1], scale=-1.0),
                 r=["psf", "negb"], w=[("lf", tt)])
        P.op("act", lambda e, T: e.activation(out=T["lf"][:, :], in_=T["lf"][:, :], func=AF.Ln, bias=1.0),
             r=[("lf", tt) for tt in range(8)], w=["lf"])
        P.op("dve", lambda e, T: e.tensor_tensor_scan(out=T["csum"][:, :], data0=T["lf"][:, :], data1=T["lf"][:, :],
                                                       initial=0.0, op0=ALU.add, op1=ALU.max),
             r=["lf"], w=["csum"])
        P.op("dve", lambda e, T: e.tensor_scalar(out=T["qrow"][:, :], in0=T["csum"][:, :], scalar1=-8.0, scalar2=None,
                                                  op0=ALU.mult), r=["csum"], w=["qrow"])
        P.dma("pool", lambda e, T: e.dma_start(out=qt1[:, HD, :], in_=T["qrow"][:, :]), r=["qrow"], w=["qt1row"])
        P.op("dve", lambda e, T: e.memset(T["qrow"][:, :], 1.0), r=["qrow"], w=["qrow"])
        P.dma("pool", lambda e, T: e.dma_start(out=kt1[:, HD, :], in_=T["qrow"][:, :]), r=["qrow"], w=["kt1row"])
        for blk in range(NT):
            P.op("pe", lambda e, T, blk=blk: e.transpose(
                out=T["pct"][:, blk * NH:(blk + 1) * NH], in_=T["csum"][0:NH, blk * 128:(blk + 1) * 128],
                identity=T["identf"][0:NH, 0:NH]), r=["csum", "identf"], w=["pct"])
        P.op("act", lambda e, T: e.activation(out=T["cst"][:, :], in_=T["pct"][:, :], func=AF.Copy),
             r=["pct"], w=["cst"])
        P.dma("pool", lambda e, T: e.dma_start(out=cs_scr[:, :], in_=T["cst"][:, :]), r=["cst"], w=["cs_scr"])

    def l1_att_phase():
        P = pg.phase("l1att")
        KR = HD + 1
        att_alloc(P, KR)
        P.sb("cst", [128, NT * NH], F32)
        P.dma("sp", lambda e, T: e.dma_start(out=T["cst"][:, :], in_=cs_scr[:, :]), w=["cst"])
        att_load(P, 0, qt1, kt1, v1, 0, KR)
        cs_ = {"s": 0, "p": 0}
        def head(h, b, a):
            if h + 1 < NH:
                att_load(P, 1 - b, qt1, kt1, v1, h + 1, KR)
            ld = [("qT", b), ("kT", b)]
            for Qi in range(8):
                pso = "pso%d" % (Qi % 2)
                nJ = 4 * Qi + 4
                for J in range(nJ):
                    d_ = J - 4 * Qi
                    c0 = 128 * d_ if d_ > 0 else 0
                    W = 512 - c0
                    si = cs_["s"] % 3
                    cs_["s"] += 1
                    pss = "pss%d" % si
                    if d_ >= 0:
                        P.op("pe", lambda e, T, pss=pss, c0=c0, W=W: e.matmul(
                            T[pss][:, c0:512], lhsT=T["ident"][:, :], rhs=T["mk"][:, 2, 0:W], start=True, stop=False),
                            r=["ident", "mk"], w=[pss])
                    P.op("pe", lambda e, T, pss=pss, c0=c0, J=J, Qi=Qi, d_=d_: e.matmul(
                        T[pss][:, c0:512], lhsT=T["kT"][0:KR, b, J * 128:(J + 1) * 128],
                        rhs=T["qT"][0:KR, b, Qi * 512 + c0:(Qi + 1) * 512], start=(d_ < 0), stop=True),
                        r=ld, w=[pss])
                    pi = cs_["p"] % 3
                    cs_["p"] += 1
                    P.op("act", lambda e, T, pss=pss, pi=pi, c0=c0, J=J: e.activation(
                        out=T["pt"][:, pi, c0:512], in_=T[pss][:, c0:512], func=AF.Exp,
                        bias=T["cst"][:, J * NH + h:J * NH + h + 1], scale=0.125),
                        r=[pss, "cst"], w=[("pt", pi)])
                    P.op("pe", lambda e, T, pso=pso, pi=pi, c0=c0, J=J, nJ=nJ: e.matmul(
                        T[pso][0:KR, c0:512], lhsT=T["vv"][:, b, J, 0:HD + 1], rhs=T["pt"][:, pi, c0:512],
                        start=(J == 0), stop=(J == nJ - 1)),
                        r=[("vv", b), ("vv1", b), ("pt", pi)], w=[pso])
                P.op("dve", lambda e, T, pso=pso, Qi=Qi: e.tensor_copy(
                    out=T["acc"][:, a, Qi * 512:(Qi + 1) * 512], in_=T[pso][0:KR, :]),
                    r=[pso], w=[("acc", a)])
            att_finish(P, h, a)

        for h in range(NH):
            head(h, h % 2, h % 2)
        return P

    P0 = pg.phase("consts")
    load_consts(P0)

    def build_all():
        for g in range(3):
            norm_phase("l0n%d" % g, x, a_norm[0:1, :], g)
            inproj_phase("l0p%d" % g, g,
                         lambda t, k, g=g: a_w_in[k * 128:(k + 1) * 128,
                                                  g * 3072 + t * 1024: g * 3072 + (t + 1) * 1024],
                         qt0[g], kt0[g], v0[g], g)
            if stop_after == "l0p%d" % g:
                return
        l0_att_phase()
        if stop_after == "l0att":
            return
        outproj_phase("l0out", a_w_out, x, h_scr)
        if stop_after == "l0out":
            return
        norm_phase("f0n", h_scr, ffn_norm[0:1, :], 0)
        ffn_phase("f0", 0)
        if stop_after == "f0":
            return
        norm_phase("l1n", h_scr, b_norm[0:1, :], 0)
        inproj_phase("l1p", 0, lambda t, k: b_w_in[k * 128:(k + 1) * 128, t * 1024:(t + 1) * 1024],
                     qt1, kt1, v1, 0, extra=gate_extra, rotary=False)
        if stop_after == "l1p":
            return
        l1_att_phase()
        if stop_after == "l1att":
            return
        outproj_phase("l1out", b_w_out, h_scr, h_scr)
        if stop_after == "l1out":
            return
        norm_phase("f1n", h_scr, ffn_norm[1:2, :], 0)
        ffn_phase("f1", 1)
        if stop_after == "f1":
            return
        norm_phase("fin", h_scr, final_norm[0:1, :], 0, transpose=False, dst=out)

    build_all()
    pg.emit()
    return nc, pg


_CACHE = {}


def core_inputs(inp, c, consts):
    f = lambda a: np.ascontiguousarray(np.asarray(a, dtype=np.float32))
    m = {
        "x": f(inp["x"][c]),
        "a_norm": f(inp["a_norm"]).reshape(1, D),
        "a_w_in": f(inp["a_w_in"][0]),
        "a_w_out": f(inp["a_w_out"][0]),
        "b_norm": f(inp["b_norm"]).reshape(1, D),
        "b_w_in": f(inp["b_w_in"][0]),
        "b_f": f(inp["b_f"]).reshape(NH, 1),
        "b_w_out": f(inp["b_w_out"][0]),
        "ffn_norm": f(inp["ffn_norm"]),
        "ffn_w_gu": f(inp["ffn_w_gu"]),
        "ffn_w_down": f(inp["ffn_w_down"]),
        "final_norm": f(inp["final_norm"]).reshape(1, D),
    }
    m.update(consts)
    return m


def kernel(**inputs):
    if "nc" not in _CACHE:
        _CACHE["nc"] = build()[0]
        _CACHE["consts"] = host_consts()
    nc = _CACHE["nc"]
    consts = _CACHE["consts"]
    nb = inputs["x"].shape[0]
    maps = [core_inputs(inputs, c, consts) for c in range(nb)]
    res = run_bass_kernel_spmd(nc, maps, core_ids=list(range(nb)))
    return np.stack([np.asarray(r["out"], dtype=np.float32) for r in res.results], axis=0)
```

```python
from contextlib import ExitStack
import numpy as np
import ml_dtypes
import concourse.bass as bass
import concourse.mybir as mybir
from concourse.bass_utils import run_bass_kernel_spmd

F32 = mybir.dt.float32
BF16 = mybir.dt.bfloat16
AF = mybir.ActivationFunctionType
ALU = mybir.AluOpType

S = 4096
D = 1024
NT = 32
KC = 8
NH = 16
HD = 64
DFF = 2816
NJ = 22
DILS = (1, 4, 16)
EPS = 1e-6
NEG = -30000.0

ENGS = ["pe", "act", "dve", "pool", "sp"]
BLK = {"pe": "tensor", "act": "scalar", "dve": "vector", "pool": "gpsimd", "sp": "sync"}


class Op:
    __slots__ = ("eng", "fn", "reads", "writes", "is_dma", "pos", "waits", "signal",
                 "slot", "target", "clock", "rank", "barrier")

    def __init__(self, eng, fn, reads, writes, is_dma):
        self.eng = eng
        self.fn = fn
        self.reads = reads
        self.writes = writes
        self.is_dma = is_dma
        self.waits = []
        self.signal = False
        self.slot = None
        self.target = None
        self.clock = None
        self.rank = None
        self.barrier = False


class Phase:
    def __init__(self, name):
        self.name = name
        self.allocs = []
        self.ops = []

    def sb(self, name, shape, dt):
        self.allocs.append((name, "sb", list(shape), dt))
        return name

    def ps(self, name, shape, dt):
        self.allocs.append((name, "ps", list(shape), dt))
        return name

    def op(self, eng, fn, r=(), w=()):
        o = Op(eng, fn, tuple(r), tuple(w), False)
        self.ops.append(o)
        return o

    def dma(self, eng, fn, r=(), w=()):
        o = Op(eng, fn, tuple(r), tuple(w), True)
        self.ops.append(o)
        return o


class Prog:
    def __init__(self, nc, n_dsem=32):
        self.nc = nc
        self.T = {}
        self.phases = []
        self.K = n_dsem
        self.gallocs = []

    def phase(self, name):
        p = Phase(name)
        self.phases.append(p)
        return p

    def gsb(self, name, shape, dt):
        self.gallocs.append((name, "sb", list(shape), dt))

    def analyze(self):
        K = self.K
        res = {}
        know = {e: ({}, {}) for e in ENGS}
        cnt = {e: 0 for e in ENGS}
        last_op = {e: None for e in ENGS}
        slot_last = [None] * K
        slot_uses = [0] * K
        dma_i = 0
        all_ops = {e: [] for e in ENGS}

        def merge(dst, src):
            for k, v in src[0].items():
                if dst[0].get(k, -1) < v:
                    dst[0][k] = v
            for k, v in src[1].items():
                if dst[1].get(k, -1) < v:
                    dst[1][k] = v

        def need(E, X, P):
            if P is None or P is X:
                return
            kn = know[E]
            if P.is_dma:
                if kn[1].get(P.slot, 0) >= P.target:
                    return
                X.waits.append(P)
                kn[1][P.slot] = P.target
                merge(kn, P.clock)
            else:
                if P.eng == E and E == "pe":
                    return
                if kn[0].get(P.eng, -1) >= P.pos:
                    return
                P.signal = True
                X.waits.append(P)
                kn[0][P.eng] = P.pos
                merge(kn, P.clock)

        for ph in self.phases:
            for e in ENGS:
                b = Op(e, None, (), (), False)
                b.barrier = True
                ph.ops.append(b)
            for X in ph.ops:
                E = X.eng
                X.pos = cnt[E]
                cnt[E] += 1
                all_ops[E].append(X)
                if X.barrier:
                    for e2 in ENGS:
                        need(E, X, last_op[e2])
                    for s in range(K):
                        need(E, X, slot_last[s])
                    X.clock = ({}, {})
                    continue
                deps = []
                for k in X.reads:
                    ent = res.get(k)
                    if ent is not None:
                        deps.append(ent[0])
                for k in X.writes:
                    ent = res.get(k)
                    if ent is not None:
                        deps.append(ent[0])
                        deps.extend(ent[1])
                if X.is_dma:
                    s = dma_i % K
                    dma_i += 1
                    need(E, X, slot_last[s])
                    slot_uses[s] += 1
                    X.slot = s
                    X.target = 16 * slot_uses[s]
                    slot_last[s] = X
                for P in deps:
                    need(E, X, P)
                X.clock = (dict(know[E][0]), dict(know[E][1]))
                for k in X.reads:
                    ent = res.get(k)
                    if ent is None:
                        res[k] = [None, [X]]
                    else:
                        ent[1].append(X)
                for k in X.writes:
                    res[k] = [X, []]
                if not X.is_dma:
                    last_op[E] = X
        for e in ENGS:
            r = 0
            for o in all_ops[e]:
                if o.signal:
                    r += 1
                    o.rank = r
        self.stats = {e: (len(all_ops[e]), sum(1 for o in all_ops[e] if o.signal),
                          sum(len(o.waits) for o in all_ops[e])) for e in ENGS}

    def emit(self):
        nc = self.nc
        self.analyze()
        T = self.T
        with ExitStack() as st:
            sem = {e: st.enter_context(nc.semaphore("s_" + e)) for e in ENGS}
            dsem = [st.enter_context(nc.semaphore("d%d" % i)) for i in range(self.K)]
            for (name, kind, shape, dt) in self.gallocs:
                T[name] = st.enter_context(nc.sbuf_tensor("t_" + name, shape, dt))
            for ph in self.phases:
                with ExitStack() as st2:
                    for (name, kind, shape, dt) in ph.allocs:
                        if kind == "sb":
                            T[name] = st2.enter_context(nc.sbuf_tensor("t_%s_%s" % (ph.name, name), shape, dt))
                        else:
                            T[name] = st2.enter_context(nc.psum_tensor("t_%s_%s" % (ph.name, name), shape, dt))
                    with nc.Block() as blk:
                        for e in ENGS:
                            ops = [o for o in ph.ops if o.eng == e]

                            def body(eng, ops=ops, e=e):
                                for o in ops:
                                    for P in o.waits:
                                        if P.is_dma:
                                            eng.wait_ge(dsem[P.slot], P.target)
                                        else:
                                            eng.wait_ge(sem[P.eng], P.rank)
                                    if o.barrier:
                                        continue
                                    ins = o.fn(eng, T)
                                    if o.is_dma:
                                        ins.then_inc(dsem[o.slot], 16)
                                    elif o.signal:
                                        ins.then_inc(sem[e], 1)

                            getattr(blk, BLK[e])(body)
                    for (name, kind, shape, dt) in ph.allocs:
                        T.pop(name, None)


def _bf(a):
    return np.ascontiguousarray(a.astype(ml_dtypes.bfloat16))


def perm_tokens(g):
    dil = DILS[g]
    L = S // dil
    n = np.arange(S)
    return (n % L) * dil + (n // L)


def host_consts():
    c = {}
    c["ident"] = _bf(np.eye(128, dtype=np.float32))
    c["identf"] = np.eye(128, dtype=np.float32)
    r = np.arange(128)
    partner = np.where(r % 16 < 8, r + 8, r - 8)
    pw = np.zeros((128, 128), np.float32)
    pw[partner, r] = 1.0
    c["pswap"] = _bf(pw)
    half = 8
    inv_freq = (np.float32(500000.0) ** (-np.arange(half, dtype=np.float32) * np.float32(2.0) / np.float32(16))).astype(np.float32)
    tabs = []
    for g in range(3):
        pos = perm_tokens(g).astype(np.float32)
        ang = (pos[:, None] * inv_freq[None, :]).astype(np.float32)
        cos = np.cos(ang).astype(np.float32).T
        sin = np.sin(ang).astype(np.float32).T
        fi = r % 8
        sgn = np.where(r % 16 < 8, -1.0, 1.0).astype(np.float32)
        tab = np.stack([cos[fi], sin[fi] * sgn[:, None]], axis=1)
        tabs.append(tab.astype(np.float32))
    c["rottab"] = np.ascontiguousarray(np.stack(tabs, 0))
    k = np.arange(128)[:, None]
    q = np.arange(128)[None, :]
    ncur = np.where(k > q, NEG, 0.0).astype(np.float32)
    nprev = np.where(k < q, NEG, 0.0).astype(np.float32)
    nall = np.full((128, 128), NEG, np.float32)
    z = np.zeros((128, 128), np.float32)
    m0 = np.concatenate([ncur, nprev, ncur, nprev], 1)
    m1 = np.concatenate([ncur, nall, ncur, nprev], 1)
    m2 = np.concatenate([ncur, z, z, z], 1)
    c["masks"] = _bf(np.stack([m0, m1, m2], 1))
    return c


def build(stop_after=None, dbg=()):
    nc = bass.Bass("TRN2", target_bir_lowering=False)
    dbg = set(dbg)

    def din(name, shape, dt):
        return nc.dram_tensor(name, list(shape), dt, kind="ExternalInput").ap()

    def dscr(name, shape, dt):
        kind = "ExternalOutput" if name in dbg else "Internal"
        return nc.dram_tensor(name, list(shape), dt, kind=kind).ap()

    x = din("x", [S, D], F32)
    a_norm = din("a_norm", [1, D], F32)
    a_w_in = din("a_w_in", [D, 9216], F32)
    a_w_out = din("a_w_out", [D, D], F32)
    b_norm = din("b_norm", [1, D], F32)
    b_w_in = din("b_w_in", [D, 3088], F32)
    b_f = din("b_f", [NH, 1], F32)
    b_w_out = din("b_w_out", [D, D], F32)
    ffn_norm = din("ffn_norm", [2, D], F32)
    ffn_w_gu = din("ffn_w_gu", [2, D, 2 * DFF], F32)
    ffn_w_down = din("ffn_w_down", [2, DFF, D], F32)
    final_norm = din("final_norm", [1, D], F32)
    c_ident = din("ident", [128, 128], BF16)
    c_identf = din("identf", [128, 128], F32)
    c_pswap = din("pswap", [128, 128], BF16)
    c_rottab = din("rottab", [3, 128, 2, S], F32)
    c_masks = din("masks", [128, 3, 512], BF16)

    out = nc.dram_tensor("out", [S, D], F32, kind="ExternalOutput").ap()

    qt0 = [dscr("qt0_%d" % g, [NH, HD, S], BF16) for g in range(3)]
    kt0 = [dscr("kt0_%d" % g, [NH, HD, S], BF16) for g in range(3)]
    v0 = [dscr("v0_%d" % g, [S, D], BF16) for g in range(3)]
    att_scr = dscr("att_scr", [D, S], BF16)
    h_scr = dscr("h_scr", [S, D], F32)
    qt1 = dscr("qt1", [NH, HD + 1, S], BF16)
    kt1 = dscr("kt1", [NH, HD + 1, S], BF16)
    v1 = dscr("v1", [S, D], BF16)
    cs_scr = dscr("cs_scr", [128, NT * NH], F32)
    rden_scr = dscr("rden_scr", [NH, S], F32)

    pg = Prog(nc)
    pg.gsb("xnT", [128, KC, S], BF16)
    pg.gsb("ident", [128, 128], BF16)
    pg.gsb("gam", [128, D], F32)

    def tok_rows(src, g, j):
        dil = DILS[g]
        L = S // dil
        n0 = 128 * j
        r = n0 // L
        p0 = n0 % L
        start = p0 * dil + r
        if dil == 1:
            return src[start:start + 128, :]
        return src[start:start + 127 * dil + 1:dil, :]

    def load_consts(P):
        P.dma("sp", lambda e, T: e.dma_start(out=T["ident"][:, :], in_=c_ident[:, :]), w=["ident"])

    def norm_phase(name, src, gamma_row, g, transpose=True, dst=None):
        P = pg.phase(name)
        P.sb("n_ht", [128, 3, D], F32)
        P.sb("n_junk", [128, D], BF16)
        P.sb("n_ss", [128, NT], F32)
        P.sb("n_rstd", [128, NT], F32)
        if transpose:
            P.sb("n_xn", [128, 2, D], BF16)
            P.ps("n_pt0", [128, D], BF16)
            P.ps("n_pt1", [128, D], BF16)
        else:
            P.sb("n_o", [128, 2, D], F32)
        P.dma("sp", lambda e, T: e.dma_start(out=T["gam"][:, :], in_=gamma_row.partition_broadcast(128)),
              w=["gam"])
        for j in range(NT):
            b = j % 3
            P.dma("sp", lambda e, T, j=j, b=b: e.dma_start(out=T["n_ht"][:, b, :], in_=tok_rows(src, g, j)),
                  w=[("n_ht", b)])
            P.op("act", lambda e, T, j=j, b=b: e.activation(out=T["n_junk"][:, :], in_=T["n_ht"][:, b, :],
                                                             func=AF.Square, accum_out=T["n_ss"][:, j:j + 1]),
                 r=[("n_ht", b)], w=[("n_ss", j)])
        allss = [("n_ss", j) for j in range(NT)]
        P.op("dve", lambda e, T: e.tensor_scalar(out=T["n_rstd"][:, :], in0=T["n_ss"][:, :], scalar1=1.0 / D,
                                                  scalar2=EPS, op0=ALU.mult, op1=ALU.add),
             r=allss, w=["n_var"])
        P.op("act", lambda e, T: e.activation(out=T["n_rstd"][:, :], in_=T["n_rstd"][:, :], func=AF.Sqrt),
             r=["n_var"], w=["n_std"])
        P.op("dve", lambda e, T: e.reciprocal(out=T["n_rstd"][:, :], in_=T["n_rstd"][:, :]),
             r=["n_std"], w=["n_rstd"])
        for j in range(NT):
            b = j % 3
            P.dma("sp", lambda e, T, j=j, b=b: e.dma_start(out=T["n_ht"][:, b, :], in_=tok_rows(src, g, j)),
                  w=[("n_ht", b)])
            if transpose:
                xb = j % 2
                P.op("dve", lambda e, T, j=j, b=b, xb=xb: e.scalar_tensor_tensor(
                    out=T["n_xn"][:, xb, :], in0=T["n_ht"][:, b, :], scalar=T["n_rstd"][:, j:j + 1],
                    in1=T["gam"][:, :], op0=ALU.mult, op1=ALU.mult),
                    r=[("n_ht", b), "n_rstd", "gam"], w=[("n_xn", xb)])
                pt = "n_pt%d" % xb
                for k in range(KC):
                    P.op("pe", lambda e, T, k=k, xb=xb, pt=pt: e.transpose(
                        out=T[pt][:, k * 128:(k + 1) * 128], in_=T["n_xn"][:, xb, k * 128:(k + 1) * 128],
                        identity=T["ident"][:, :]),
                        r=[("n_xn", xb), "ident"], w=[pt])
                P.op("act", lambda e, T, j=j, pt=pt: e.activation(
                    out=T["xnT"][:, :, j * 128:(j + 1) * 128],
                    in_=T[pt][:, :].rearrange("p (k t) -> p k t", k=KC), func=AF.Copy),
                    r=[pt], w=[("xnT", j)])
            else:
                ob = j % 2
                P.op("dve", lambda e, T, j=j, b=b, ob=ob: e.scalar_tensor_tensor(
                    out=T["n_o"][:, ob, :], in0=T["n_ht"][:, b, :], scalar=T["n_rstd"][:, j:j + 1],
                    in1=T["gam"][:, :], op0=ALU.mult, op1=ALU.mult),
                    r=[("n_ht", b), "n_rstd", "gam"], w=[("n_o", ob)])
                P.dma("pool", lambda e, T, j=j, ob=ob: e.dma_start(out=dst[j * 128:(j + 1) * 128, :],
                                                                    in_=T["n_o"][:, ob, :]),
                      r=[("n_o", ob)], w=[("dst", j)])
        return P

    def load_w_slab(P, wname, stname, wsrc_cols, slab, rotperm):
        for k in range(KC):
            sb_ = k % 2
            P.dma("sp", lambda e, T, k=k, sb_=sb_: e.dma_start(out=T[stname][:, sb_, :], in_=wsrc_cols(k)),
                  w=[(stname, sb_)])
            if rotperm:
                P.op("dve", lambda e, T, k=k, sb_=sb_: e.tensor_copy(
                    out=T[wname][:, slab, k, 0:256].rearrange("p (h d) -> p h d", d=16),
                    in_=T[stname][:, sb_, :].rearrange("p (h d) -> p h d", d=64)[:, :, 0:16]),
                    r=[(stname, sb_)], w=[(wname, slab, k, "a")])
                P.op("dve", lambda e, T, k=k, sb_=sb_: e.tensor_copy(
                    out=T[wname][:, slab, k, 256:1024].rearrange("p (h d) -> p h d", d=48),
                    in_=T[stname][:, sb_, :].rearrange("p (h d) -> p h d", d=64)[:, :, 16:64]),
                    r=[(stname, sb_)], w=[(wname, slab, k, "b")])
            else:
                P.op("dve", lambda e, T, k=k, sb_=sb_: e.tensor_copy(out=T[wname][:, slab, k, :],
                                                                      in_=T[stname][:, sb_, :]),
                     r=[(stname, sb_)], w=[(wname, slab, k, "a"), (wname, slab, k, "b")])

    def wkeys(wname, slab):
        ks = []
        for k in range(KC):
            ks.append((wname, slab, k, "a"))
            ks.append((wname, slab, k, "b"))
        return ks

    def seg_list(c):
        segs = []
        if c < 2:
            for hh in range(8):
                segs.append((hh * 16, 16, 8 * c + hh, 0))
        else:
            f0 = 128 * (c - 2)
            f = f0
            while f < f0 + 128:
                h = f // 48
                dd = f % 48
                n = min(48 - dd, f0 + 128 - f)
                segs.append((f - f0, n, h, 16 + dd))
                f += n
        return segs

    def inproj_phase(name, g, wcols, qt_dst, kt_dst, v_dst, tabg, extra=None, rotary=True):
        P = pg.phase(name)
        P.sb("w_st", [128, 2, 1024], F32)
        P.sb("w_sl", [128, 2, KC, 1024], BF16)
        P.sb("stage", [128, 2, S], BF16)
        P.sb("raw", [128, 2, 512], BF16)
        P.sb("cs", [128, 2, 2, 512], F32)
        P.sb("t1", [128, 2, 512], F32)
        P.sb("t2", [128, 2, 512], F32)
        P.sb("pswap", [128, 128], BF16)
        P.sb("vst", [128, 2, D], BF16)
        for i in range(3):
            P.ps("pq%d" % i, [128, 512], F32)
        for i in range(2):
            P.ps("psw%d" % i, [128, 512], F32)
        P.dma("sp", lambda e, T: e.dma_start(out=T["pswap"][:, :], in_=c_pswap[:, :]), w=["pswap"])
        cnt = {"pq": 0, "rot": 0, "stage": 0}
        slab_i = [0]

        def next_slab(t, rotperm):
            sl = slab_i[0] % 2
            slab_i[0] += 1
            load_w_slab(P, "w_sl", "w_st", lambda k, t=t: wcols(t, k), sl, rotperm)
            return sl

        pending = []

        def flush_pending():
            while pending:
                pending.pop(0)()

        sl_next = next_slab(0, True)
        for t in range(2):
            sl = sl_next
            sl_next = next_slab(t + 1, t + 1 < 2)
            dst = qt_dst if t == 0 else kt_dst
            for c in range(8):
                sg = cnt["stage"] % 2
                cnt["stage"] += 1
                for tt in range(8):
                    pi = cnt["pq"] % 3
                    cnt["pq"] += 1
                    pq = "pq%d" % pi
                    xk = [("xnT", 4 * tt + i) for i in range(4)]
                    for k in range(KC):
                        P.op("pe", lambda e, T, k=k, sl=sl, c=c, tt=tt, pq=pq: e.matmul(
                            T[pq][:, :], lhsT=T["w_sl"][:, sl, k, c * 128:(c + 1) * 128],
                            rhs=T["xnT"][:, k, tt * 512:(tt + 1) * 512], start=(k == 0), stop=(k == KC - 1)),
                            r=xk + [("w_sl", sl, k, "a" if c < 2 else "b")], w=[pq])
                    if c >= 2 or not rotary:
                        P.op("act", lambda e, T, sg=sg, tt=tt, pq=pq: e.activation(
                            out=T["stage"][:, sg, tt * 512:(tt + 1) * 512], in_=T[pq][:, :], func=AF.Copy),
                            r=[pq], w=[("stage", sg, tt)])
                    else:
                        ri = cnt["rot"] % 2
                        cnt["rot"] += 1
                        P.op("act", lambda e, T, ri=ri, pq=pq: e.activation(
                            out=T["raw"][:, ri, :], in_=T[pq][:, :], func=AF.Copy),
                            r=[pq], w=[("raw", ri)])
                        P.dma("sp", lambda e, T, ri=ri, tt=tt: e.dma_start(
                            out=T["cs"][:, ri, :, :], in_=c_rottab[tabg, :, :, tt * 512:(tt + 1) * 512]),
                            w=[("cs", ri)])

                        def rot(ri=ri, sg=sg, tt=tt):
                            psw = "psw%d" % ri
                            P.op("pe", lambda e, T: e.matmul(T[psw][:, :], lhsT=T["pswap"][:, :],
                                                              rhs=T["raw"][:, ri, :], start=True, stop=True),
                                 r=[("raw", ri), "pswap"], w=[psw])
                            P.op("dve", lambda e, T: e.tensor_tensor(out=T["t1"][:, ri, :], in0=T["raw"][:, ri, :],
                                                                      in1=T["cs"][:, ri, 0, :], op=ALU.mult),
                                 r=[("raw", ri), ("cs", ri)], w=[("t1", ri)])
                            P.op("dve", lambda e, T: e.tensor_tensor(out=T["t2"][:, ri, :], in0=T[psw][:, :],
                                                                      in1=T["cs"][:, ri, 1, :], op=ALU.mult),
                                 r=[psw, ("cs", ri)], w=[("t2", ri)])
                            P.op("dve", lambda e, T: e.tensor_tensor(
                                out=T["stage"][:, sg, tt * 512:(tt + 1) * 512], in0=T["t1"][:, ri, :],
                                in1=T["t2"][:, ri, :], op=ALU.add),
                                r=[("t1", ri), ("t2", ri)], w=[("stage", sg, tt)])
                        flush_pending()
                        pending.append(rot)
                flush_pending()
                for (r0, n, h, d0) in seg_list(c):
                    P.dma("pool", lambda e, T, sg=sg, r0=r0, n=n, h=h, d0=d0, dst=dst: e.dma_start(
                        out=dst[h, d0:d0 + n, :], in_=T["stage"][r0:r0 + n, sg, :]),
                        r=[("stage", sg, tt) for tt in range(8)], w=[("qkdst", t, h, d0)])
        sl = sl_next
        for b in range(NT):
            vb = b % 2
            for s in range(2):
                pi = cnt["pq"] % 3
                cnt["pq"] += 1
                pq = "pq%d" % pi
                for k in range(KC):
                    P.op("pe", lambda e, T, k=k, sl=sl, b=b, s=s, pq=pq: e.matmul(
                        T[pq][:, :], lhsT=T["xnT"][:, k, b * 128:(b + 1) * 128],
                        rhs=T["w_sl"][:, sl, k, s * 512:(s + 1) * 512], start=(k == 0), stop=(k == KC - 1)),
                        r=[("xnT", b), ("w_sl", sl, k, "a"), ("w_sl", sl, k, "b")], w=[pq])
                P.op("act", lambda e, T, vb=vb, s=s, pq=pq: e.activation(
                    out=T["vst"][:, vb, s * 512:(s + 1) * 512], in_=T[pq][:, :], func=AF.Copy),
                    r=[pq], w=[("vst", vb, s)])
            P.dma("pool", lambda e, T, vb=vb, b=b: e.dma_start(out=v_dst[b * 128:(b + 1) * 128, :],
                                                                in_=T["vst"][:, vb, :]),
                  r=[("vst", vb, 0), ("vst", vb, 1)], w=[("vdst", b)])
        if extra is not None:
            extra(P)
        return P

    def att_alloc(P, krows):
        P.sb("mk", [128, 3, 512], BF16)
        P.sb("qT", [krows, 2, S], BF16)
        P.sb("kT", [krows, 2, S], BF16)
        P.sb("vv", [128, 2, NT, HD + 2], BF16)
        P.sb("pt", [128, 4, 512], BF16)
        P.sb("acc", [HD + 1, 2, S], F32)
        P.sb("bc", [HD, S], F32)
        P.sb("ot", [HD, 2, S], BF16)
        for i in range(4):
            P.ps("pss%d" % i, [128, 512], F32)
        for i in range(2):
            P.ps("pso%d" % i, [128, 512], F32)
        P.dma("sp", lambda e, T: e.dma_start(out=T["mk"][:, :, :], in_=c_masks[:, :, :]), w=["mk"])
        for b in range(2):
            P.op("dve", lambda e, T, b=b: e.memset(T["vv"][:, b, :, :], 1.0), w=[("vv1", b), ("vv", b)])

    def att_load(P, b, qsrc, ksrc, vsrc, h, krows):
        P.dma("sp", lambda e, T: e.dma_start(out=T["qT"][0:krows, b, :], in_=qsrc[h, :, :]), w=[("qT", b)])
        P.dma("sp", lambda e, T: e.dma_start(out=T["kT"][0:krows, b, :], in_=ksrc[h, :, :]), w=[("kT", b)])
        P.dma("sp", lambda e, T: e.dma_start(
            out=T["vv"][:, b, :, 0:HD],
            in_=vsrc[:, h * HD:(h + 1) * HD].rearrange("(n p) d -> p n d", p=128)), w=[("vv", b)])

    def att_finish(P, h, a):
        P.op("dve", lambda e, T: e.reciprocal(out=T["acc"][HD:HD + 1, a, :], in_=T["acc"][HD:HD + 1, a, :]),
             r=[("acc", a)], w=[("acc", a)])
        P.dma("pool", lambda e, T: e.dma_start(out=rden_scr[h:h + 1, :], in_=T["acc"][HD:HD + 1, a, :]),
              r=[("acc", a)], w=[("rden", h)])
        P.dma("sp", lambda e, T: e.dma_start(out=T["bc"][:, :],
                                              in_=rden_scr[h:h + 1, :].partition_broadcast(HD)),
              r=[("rden", h)], w=["bc"])
        P.op("dve", lambda e, T: e.tensor_tensor(out=T["ot"][:, a, :], in0=T["acc"][0:HD, a, :],
                                                  in1=T["bc"][:, :], op=ALU.mult),
             r=[("acc", a), "bc"], w=[("ot", a)])
        P.dma("pool", lambda e, T: e.dma_start(out=att_scr[h * HD:(h + 1) * HD, :], in_=T["ot"][:, a, :]),
              r=[("ot", a)], w=[("att_scr", h)])

    NPB = 4
    LA = 3

    def pipeline(items, stage1, stage2):
        n = len(items)
        for k in range(n + LA):
            if k < n:
                stage1(k, items[k])
            if k >= LA:
                stage2(k - LA, items[k - LA])

    def l0_att_phase():
        P = pg.phase("l0att")
        att_alloc(P, HD)
        units = [(h, g) for h in range(NH) for g in range(3)]
        for u0 in range(2):
            att_load(P, u0, qt0[units[u0][1]], kt0[units[u0][1]], v0[units[u0][1]], units[u0][0], HD)
        items = [(ui, pr) for ui in range(len(units)) for pr in range(16)]

        def stage1(k, it):
            ui, pr = it
            h, g = units[ui]
            b = ui % 2
            nbc = NT // DILS[g]
            ld = [("qT", b), ("kT", b)]
            n0 = 2 * pr
            mi = 1 if (n0 % nbc) == 0 else 0
            pss = "pss%d" % (k % NPB)
            pi = k % NPB
            mms = []
            for e_ in range(2):
                n = n0 + e_
                mms.append((e_ * 256, n, n))
                if n % nbc > 0:
                    mms.append((e_ * 256 + 128, n - 1, n))
            P.op("pe", lambda e, T: e.matmul(T[pss][:, :], lhsT=T["ident"][:, :], rhs=T["mk"][:, mi, :],
                                              start=True, stop=False),
                 r=["ident", "mk"], w=[pss])
            for ii, (c0, kb, qb) in enumerate(mms):
                P.op("pe", lambda e, T, c0=c0, kb=kb, qb=qb, last=(ii == len(mms) - 1): e.matmul(
                    T[pss][:, c0:c0 + 128], lhsT=T["kT"][0:HD, b, kb * 128:(kb + 1) * 128],
                    rhs=T["qT"][0:HD, b, qb * 128:(qb + 1) * 128], start=False, stop=last),
                    r=ld, w=[pss])
            P.op("act", lambda e, T: e.activation(out=T["pt"][:, pi, :], in_=T[pss][:, :], func=AF.Exp, scale=0.125),
                 r=[pss], w=[("pt", pi)])

        def stage2(k, it):
            ui, pr = it
            h, g = units[ui]
            b = ui % 2
            a = h % 2
            dil = DILS[g]
            nbc = NT // dil
            L = S // dil
            n0 = 2 * pr
            pi = k % NPB
            pso = "pso%d" % ((pr // 2) % 2)
            for e_ in range(2):
                n = n0 + e_
                m = n % nbc
                cols = (n % 4) * 128
                P.op("pe", lambda e, T, cols=cols, n=n, e_=e_, m=m: e.matmul(
                    T[pso][0:HD + 1, cols:cols + 128], lhsT=T["vv"][:, b, n, 0:HD + 1],
                    rhs=T["pt"][:, pi, e_ * 256:e_ * 256 + 128], start=True, stop=(m == 0)),
                    r=[("vv", b), ("vv1", b), ("pt", pi)], w=[pso])
                if m > 0:
                    P.op("pe", lambda e, T, cols=cols, n=n, e_=e_: e.matmul(
                        T[pso][0:HD + 1, cols:cols + 128], lhsT=T["vv"][:, b, n - 1, 0:HD + 1],
                        rhs=T["pt"][:, pi, e_ * 256 + 128:e_ * 256 + 256], start=False, stop=True),
                        r=[("vv", b), ("vv1", b), ("pt", pi)], w=[pso])
            if pr % 2 == 1:
                B = pr // 2
                if g == 0:
                    P.op("dve", lambda e, T: e.tensor_copy(
                        out=T["acc"][:, a, B * 512:(B + 1) * 512], in_=T[pso][0:HD + 1, :]),
                        r=[pso], w=[("acc", a)])
                else:
                    nsub = 512 // min(512, L)
                    w_ = 512 // nsub
                    for sub in range(nsub):
                        npos = 512 * B + sub * w_
                        r_ = npos // L
                        p0 = npos % L
                        st_ = p0 * dil + r_
                        en_ = st_ + (w_ - 1) * dil + 1
                        P.op("dve", lambda e, T, st_=st_, en_=en_, sub=sub: e.tensor_tensor(
                            out=T["acc"][:, a, st_:en_:dil], in0=T["acc"][:, a, st_:en_:dil],
                            in1=T[pso][0:HD + 1, sub * w_:(sub + 1) * w_], op=ALU.add),
                            r=[pso, ("acc", a)], w=[("acc", a)])
            if pr == 15 and g == 2:
                att_finish(P, h, a)
            if pr == 15 and ui + 2 < len(units):
                h2, g2 = units[ui + 2]
                att_load(P, b, qt0[g2], kt0[g2], v0[g2], h2, HD)

        pipeline(items, stage1, stage2)
        return P

    def outproj_phase(name, wsrc, res_src, res_dst):
        P = pg.phase(name)
        P.sb("w_st", [128, 2, D], F32)
        P.sb("wo", [128, KC, D], BF16)
        P.sb("xt", [128, 3, D], F32)
        for i in range(4):
            P.ps("po%d" % i, [128, 512], F32)
        for k in range(KC):
            P.dma("sp", lambda e, T, k=k: e.dma_start(out=T["xnT"][:, k, :], in_=att_scr[k * 128:(k + 1) * 128, :]),
                  w=[("attT", k)])
        for k in range(KC):
            sb_ = k % 2
            P.dma("sp", lambda e, T, k=k, sb_=sb_: e.dma_start(out=T["w_st"][:, sb_, :],
                                                                in_=wsrc[k * 128:(k + 1) * 128, :]),
                  w=[("w_st", sb_)])
            P.op("dve", lambda e, T, k=k, sb_=sb_: e.tensor_copy(out=T["wo"][:, k, :], in_=T["w_st"][:, sb_, :]),
                 r=[("w_st", sb_)], w=[("wo", k)])
        for j in range(NT):
            xb = j % 3
            P.dma("sp", lambda e, T, j=j, xb=xb: e.dma_start(out=T["xt"][:, xb, :],
                                                              in_=res_src[j * 128:(j + 1) * 128, :]),
                  r=[("hs", j)], w=[("xt", xb)])
            for s_ in range(2):
                po = "po%d" % ((2 * j + s_) % 4)
                for k in range(KC):
                    P.op("pe", lambda e, T, j=j, s_=s_, k=k, po=po: e.matmul(
                        T[po][:, :], lhsT=T["xnT"][:, k, j * 128:(j + 1) * 128],
                        rhs=T["wo"][:, k, s_ * 512:(s_ + 1) * 512], start=(k == 0), stop=(k == KC - 1)),
                        r=[("attT", k), ("wo", k)], w=[po])
                P.op("dve", lambda e, T, xb=xb, s_=s_, po=po: e.tensor_tensor(
                    out=T["xt"][:, xb, s_ * 512:(s_ + 1) * 512], in0=T["xt"][:, xb, s_ * 512:(s_ + 1) * 512],
                    in1=T[po][:, :], op=ALU.add),
                    r=[po, ("xt", xb)], w=[("xt", xb)])
            P.dma("pool", lambda e, T, j=j, xb=xb: e.dma_start(out=res_dst[j * 128:(j + 1) * 128, :],
                                                                in_=T["xt"][:, xb, :]),
                  r=[("xt", xb)], w=[("hs", j)])
        return P

    def ffn_phase(name, li):
        P = pg.phase(name)
        P.sb("wd", [128, NJ, D], BF16)
        P.sb("actT", [128, NJ, 1024], BF16)
        P.sb("wgs", [128, 2, 2048], F32)
        P.sb("wg", [128, 2, 2048], BF16)
        P.sb("sg", [128, 2, 512], F32)
        P.sb("xt", [128, 2, D], F32)
        for i in range(8):
            P.ps("pb%d" % i, [128, 512], F32)
        wgu = ffn_w_gu[li]
        wdn = ffn_w_down[li]
        c = {"st": 0, "wg": 0, "sg": 0, "xt": 0}

        def stg():
            b = c["st"] % 2
            c["st"] += 1
            return b

        for j in range(NJ):
            b = stg()
            P.dma("sp", lambda e, T, j=j, b=b: e.dma_start(out=T["wgs"][:, b, 0:D], in_=wdn[j * 128:(j + 1) * 128, :]),
                  w=[("wgs", b)])
            P.op("dve", lambda e, T, j=j, b=b: e.tensor_copy(out=T["wd"][:, j, :], in_=T["wgs"][:, b, 0:D]),
                 r=[("wgs", b)], w=[("wd", j)])

        def load_gu(j):
            b = stg()
            b2 = c["wg"] % 2
            c["wg"] += 1
            P.dma("sp", lambda e, T: e.dma_start(
                out=T["wgs"][:, b, :].rearrange("p (k c) -> p k c", k=KC)[:, :, 0:128],
                in_=wgu[:, j * 128:(j + 1) * 128].rearrange("(k p) c -> p k c", p=128)), w=[("wgs", b, 0)])
            P.dma("sp", lambda e, T: e.dma_start(
                out=T["wgs"][:, b, :].rearrange("p (k c) -> p k c", k=KC)[:, :, 128:256],
                in_=wgu[:, DFF + j * 128:DFF + (j + 1) * 128].rearrange("(k p) c -> p k c", p=128)),
                w=[("wgs", b, 1)])
            P.op("dve", lambda e, T: e.tensor_copy(out=T["wg"][:, b2, :], in_=T["wgs"][:, b, :]),
                 r=[("wgs", b, 0), ("wgs", b, 1), ("wgs", b)], w=[("wg", b2), ("wgs", b)])
            return b2

        for st_ in range(4):
            nxt = load_gu(0)
            for j in range(NJ):
                b2 = nxt
                if j + 1 < NJ:
                    nxt = load_gu(j + 1)
                set_ = j % 2
                for half in range(2):
                    tok0 = st_ * 1024 + half * 512
                    xk = [("xnT", tok0 // 128 + i) for i in range(4)]
                    pgb = "pb%d" % (4 * set_ + half)
                    pub = "pb%d" % (4 * set_ + 2 + half)
                    for (bank, off) in ((pgb, 0), (pub, 128)):
                        for k in range(KC):
                            P.op("pe", lambda e, T, bank=bank, off=off, k=k, b2=b2, tok0=tok0: e.matmul(
                                T[bank][:, :],
                                lhsT=T["wg"][:, b2, :].rearrange("p (k c) -> p k c", k=KC)[:, k, off:off + 128],
                                rhs=T["xnT"][:, k, tok0:tok0 + 512], start=(k == 0), stop=(k == KC - 1)),
                                r=xk + [("wg", b2)], w=[bank])
                    sgi = c["sg"] % 2
                    c["sg"] += 1
                    P.op("act", lambda e, T, pgb=pgb, sgi=sgi: e.activation(out=T["sg"][:, sgi, :], in_=T[pgb][:, :],
                                                                             func=AF.Silu),
                         r=[pgb], w=[("sg", sgi)])
                    P.op("dve", lambda e, T, pub=pub, sgi=sgi, j=j, half=half: e.tensor_tensor(
                        out=T["actT"][:, j, half * 512:(half + 1) * 512], in0=T["sg"][:, sgi, :], in1=T[pub][:, :],
                        op=ALU.mult),
                        r=[pub, ("sg", sgi)], w=[("actT", j, half)])
            for tb in range(8):
                jt = st_ * 8 + tb
                xb = c["xt"] % 2
                c["xt"] += 1
                P.dma("sp", lambda e, T, jt=jt, xb=xb: e.dma_start(out=T["xt"][:, xb, :],
                                                                    in_=h_scr[jt * 128:(jt + 1) * 128, :]),
                      r=[("hs", jt)], w=[("xt", xb)])
                for s_ in range(2):
                    bank = "pb%d" % ((2 * tb + s_) % 8)
                    for j in range(NJ):
                        P.op("pe", lambda e, T, bank=bank, j=j, tb=tb, s_=s_: e.matmul(
                            T[bank][:, :], lhsT=T["actT"][:, j, tb * 128:(tb + 1) * 128],
                            rhs=T["wd"][:, j, s_ * 512:(s_ + 1) * 512], start=(j == 0), stop=(j == NJ - 1)),
                            r=[("actT", j, tb // 4), ("wd", j)], w=[bank])
                    P.op("dve", lambda e, T, bank=bank, xb=xb, s_=s_: e.tensor_tensor(
                        out=T["xt"][:, xb, s_ * 512:(s_ + 1) * 512], in0=T["xt"][:, xb, s_ * 512:(s_ + 1) * 512],
                        in1=T[bank][:, :], op=ALU.add),
                        r=[bank, ("xt", xb)], w=[("xt", xb)])
                P.dma("pool", lambda e, T, jt=jt, xb=xb: e.dma_start(out=h_scr[jt * 128:(jt + 1) * 128, :],
                                                                      in_=T["xt"][:, xb, :]),
                      r=[("xt", xb)], w=[("hs", jt)])
        return P

    def gate_extra(P):
        P.sb("wfs", [128, KC, NH], F32)
        P.sb("wfb", [128, KC, NH], BF16)
        P.sb("negb", [NH, 1], F32)
        P.sb("lf", [NH, S], F32)
        P.sb("csum", [NH, S], F32)
        P.sb("qrow", [NH, S], BF16)
        P.sb("identf", [128, 128], F32)
        P.sb("cst", [128, NT * NH], F32)
        P.ps("psf", [128, 512], F32)
        P.ps("pct", [128, 512], F32)
        P.dma("sp", lambda e, T: e.dma_start(out=T["identf"][:, :], in_=c_identf[:, :]), w=["identf"])
        P.dma("sp", lambda e, T: e.dma_start(
            out=T["wfs"][:, :, :], in_=b_w_in[:, 3 * D:3 * D + NH].rearrange("(k p) c -> p k c", p=128)), w=["wfs"])
        P.op("dve", lambda e, T: e.tensor_copy(out=T["wfb"][:, :, :], in_=T["wfs"][:, :, :]), r=["wfs"], w=["wfb"])
        P.dma("sp", lambda e, T: e.dma_start(out=T["negb"][:, :], in_=b_f[:, :]), w=["negb0"])
        P.op("dve", lambda e, T: e.tensor_scalar(out=T["negb"][:, :], in0=T["negb"][:, :], scalar1=-1.0, scalar2=None,
                                                  op0=ALU.mult), r=["negb0"], w=["negb"])
        for tt in range(8):
            xk = [("xnT", 4 * tt + i) for i in range(4)]
            for k in range(KC):
                P.op("pe", lambda e, T, k=k, tt=tt: e.matmul(
                    T["psf"][0:NH, :], lhsT=T["wfb"][:, k, :], rhs=T["xnT"][:, k, tt * 512:(tt + 1) * 512],
                    start=(k == 0), stop=(k == KC - 1)), r=xk + ["wfb"], w=["psf"])
            P.op("act", lambda e, T, tt=tt: e.activation(out=T["lf"][:, tt * 512:(tt + 1) * 512], in_=T["psf"][0:NH, :],
                                                          func=AF.Exp, bias=T["negb"][:, 0:1], scale=-1.0),
                 r=["psf", "negb"], w=[("lf", tt)])
        P.op("act", lambda e, T: e.activation(out=T["lf"][:, :], in_=T["lf"][:, :], func=AF.Ln, bias=1.0),
             r=[("lf", tt) for tt in range(8)], w=["lf"])
        P.op("dve", lambda e, T: e.tensor_tensor_scan(out=T["csum"][:, :], data0=T["lf"][:, :], data1=T["lf"][:, :],
                                                       initial=0.0, op0=ALU.add, op1=ALU.max),
             r=["lf"], w=["csum"])
        P.op("dve", lambda e, T: e.tensor_scalar(out=T["qrow"][:, :], in0=T["csum"][:, :], scalar1=-8.0, scalar2=None,
                                                  op0=ALU.mult), r=["csum"], w=["qrow"])
        P.dma("pool", lambda e, T: e.dma_start(out=qt1[:, HD, :], in_=T["qrow"][:, :]), r=["qrow"], w=["qt1row"])
        P.op("dve", lambda e, T: e.memset(T["qrow"][:, :], 1.0), r=["qrow"], w=["qrow"])
        P.dma("pool", lambda e, T: e.dma_start(out=kt1[:, HD, :], in_=T["qrow"][:, :]), r=["qrow"], w=["kt1row"])
        for blk in range(NT):
            P.op("pe", lambda e, T, blk=blk: e.transpose(
                out=T["pct"][:, blk * NH:(blk + 1) * NH], in_=T["csum"][0:NH, blk * 128:(blk + 1) * 128],
                identity=T["identf"][0:NH, 0:NH]), r=["csum", "identf"], w=["pct"])
        P.op("act", lambda e, T: e.activation(out=T["cst"][:, :], in_=T["pct"][:, :], func=AF.Copy),
             r=["pct"], w=["cst"])
        P.dma("pool", lambda e, T: e.dma_start(out=cs_scr[:, :], in_=T["cst"][:, :]), r=["cst"], w=["cs_scr"])

    def l1_att_phase():
        P = pg.phase("l1att")
        KR = HD + 1
        att_alloc(P, KR)
        P.sb("cst", [128, NT * NH], F32)
        P.dma("sp", lambda e, T: e.dma_start(out=T["cst"][:, :], in_=cs_scr[:, :]), w=["cst"])
        att_load(P, 0, qt1, kt1, v1, 0, KR)
        att_load(P, 1, qt1, kt1, v1, 1, KR)
        items = [(h, Qi, J) for h in range(NH) for Qi in range(8) for J in range(4 * Qi + 4)]

        def stage1(k, it):
            h, Qi, J = it
            b = h % 2
            ld = [("qT", b), ("kT", b)]
            d_ = J - 4 * Qi
            c0 = 128 * d_ if d_ > 0 else 0
            W = 512 - c0
            pss = "pss%d" % (k % NPB)
            pi = k % NPB
            if d_ >= 0:
                P.op("pe", lambda e, T: e.matmul(T[pss][:, c0:512], lhsT=T["ident"][:, :], rhs=T["mk"][:, 2, 0:W],
                                                  start=True, stop=False),
                     r=["ident", "mk"], w=[pss])
            P.op("pe", lambda e, T: e.matmul(
                T[pss][:, c0:512], lhsT=T["kT"][0:KR, b, J * 128:(J + 1) * 128],
                rhs=T["qT"][0:KR, b, Qi * 512 + c0:(Qi + 1) * 512], start=(d_ < 0), stop=True),
                r=ld, w=[pss])
            P.op("act", lambda e, T: e.activation(
                out=T["pt"][:, pi, c0:512], in_=T[pss][:, c0:512], func=AF.Exp,
                bias=T["cst"][:, J * NH + h:J * NH + h + 1], scale=0.125),
                r=[pss, "cst"], w=[("pt", pi)])

        def stage2(k, it):
            h, Qi, J = it
            b = h % 2
            a = h % 2
            nJ = 4 * Qi + 4
            d_ = J - 4 * Qi
            c0 = 128 * d_ if d_ > 0 else 0
            pi = k % NPB
            pso = "pso%d" % (Qi % 2)
            P.op("pe", lambda e, T: e.matmul(
                T[pso][0:KR, c0:512], lhsT=T["vv"][:, b, J, 0:HD + 1], rhs=T["pt"][:, pi, c0:512],
                start=(J == 0), stop=(J == nJ - 1)),
                r=[("vv", b), ("vv1", b), ("pt", pi)], w=[pso])
            if J == nJ - 1:
                P.op("dve", lambda e, T: e.tensor_copy(
                    out=T["acc"][:, a, Qi * 512:(Qi + 1) * 512], in_=T[pso][0:KR, :]),
                    r=[pso], w=[("acc", a)])
                if Qi == 7:
                    att_finish(P, h, a)
                    if h + 2 < NH:
                        att_load(P, b, qt1, kt1, v1, h + 2, KR)

        pipeline(items, stage1, stage2)
        return P

    P0 = pg.phase("consts")
    load_consts(P0)

    def build_all():
        for g in range(3):
            norm_phase("l0n%d" % g, x, a_norm[0:1, :], g)
            inproj_phase("l0p%d" % g, g,
                         lambda t, k, g=g: a_w_in[k * 128:(k + 1) * 128,
                                                  g * 3072 + t * 1024: g * 3072 + (t + 1) * 1024],
                         qt0[g], kt0[g], v0[g], g)
            if stop_after == "l0p%d" % g:
                return
        l0_att_phase()
        if stop_after == "l0att":
            return
        outproj_phase("l0out", a_w_out, x, h_scr)
        if stop_after == "l0out":
            return
        norm_phase("f0n", h_scr, ffn_norm[0:1, :], 0)
        ffn_phase("f0", 0)
        if stop_after == "f0":
            return
        norm_phase("l1n", h_scr, b_norm[0:1, :], 0)
        inproj_phase("l1p", 0, lambda t, k: b_w_in[k * 128:(k + 1) * 128, t * 1024:(t + 1) * 1024],
                     qt1, kt1, v1, 0, extra=gate_extra, rotary=False)
        if stop_after == "l1p":
            return
        l1_att_phase()
        if stop_after == "l1att":
            return
        outproj_phase("l1out", b_w_out, h_scr, h_scr)
        if stop_after == "l1out":
            return
        norm_phase("f1n", h_scr, ffn_norm[1:2, :], 0)
        ffn_phase("f1", 1)
        if stop_after == "f1":
            return
        norm_phase("fin", h_scr, final_norm[0:1, :], 0, transpose=False, dst=out)

    build_all()
    pg.emit()
    return nc, pg


_CACHE = {}


def core_inputs(inp, c, consts):
    f = lambda a: np.ascontiguousarray(np.asarray(a, dtype=np.float32))
    m = {
        "x": f(inp["x"][c]),
        "a_norm": f(inp["a_norm"]).reshape(1, D),
        "a_w_in": f(inp["a_w_in"][0]),
        "a_w_out": f(inp["a_w_out"][0]),
        "b_norm": f(inp["b_norm"]).reshape(1, D),
        "b_w_in": f(inp["b_w_in"][0]),
        "b_f": f(inp["b_f"]).reshape(NH, 1),
        "b_w_out": f(inp["b_w_out"][0]),
        "ffn_norm": f(inp["ffn_norm"]),
        "ffn_w_gu": f(inp["ffn_w_gu"]),
        "ffn_w_down": f(inp["ffn_w_down"]),
        "final_norm": f(inp["final_norm"]).reshape(1, D),
    }
    m.update(consts)
    return m


def kernel(**inputs):
    if "nc" not in _CACHE:
        _CACHE["nc"] = build()[0]
        _CACHE["consts"] = host_consts()
    nc = _CACHE["nc"]
    consts = _CACHE["consts"]
    nb = inputs["x"].shape[0]
    maps = [core_inputs(inputs, c, consts) for c in range(nb)]
    res = run_bass_kernel_spmd(nc, maps, core_ids=list(range(nb)))
    return np.stack([np.asarray(r["out"], dtype=np.float32) for r in res.results], axis=0)
```

```python
from contextlib import ExitStack
import numpy as np
import ml_dtypes
import concourse.bass as bass
import concourse.mybir as mybir
from concourse.bass_utils import run_bass_kernel_spmd

F32 = mybir.dt.float32
BF16 = mybir.dt.bfloat16
AF = mybir.ActivationFunctionType
ALU = mybir.AluOpType

S = 4096
D = 1024
NT = 32
KC = 8
NH = 16
HD = 64
DFF = 2816
NJ = 22
DILS = (1, 4, 16)
EPS = 1e-6
NEG = -30000.0

ENGS = ["pe", "act", "dve", "pool", "sp"]
BLK = {"pe": "tensor", "act": "scalar", "dve": "vector", "pool": "gpsimd", "sp": "sync"}


class Op:
    __slots__ = ("eng", "fn", "reads", "writes", "is_dma", "pos", "waits", "signal",
                 "slot", "target", "clock", "rank", "barrier")

    def __init__(self, eng, fn, reads, writes, is_dma):
        self.eng = eng
        self.fn = fn
        self.reads = reads
        self.writes = writes
        self.is_dma = is_dma
        self.waits = []
        self.signal = False
        self.slot = None
        self.target = None
        self.clock = None
        self.rank = None
        self.barrier = False


class Phase:
    def __init__(self, name):
        self.name = name
        self.allocs = []
        self.ops = []

    def sb(self, name, shape, dt):
        self.allocs.append((name, "sb", list(shape), dt))
        return name

    def ps(self, name, shape, dt):
        self.allocs.append((name, "ps", list(shape), dt))
        return name

    def op(self, eng, fn, r=(), w=()):
        o = Op(eng, fn, tuple(r), tuple(w), False)
        self.ops.append(o)
        return o

    def dma(self, eng, fn, r=(), w=()):
        o = Op(eng, fn, tuple(r), tuple(w), True)
        self.ops.append(o)
        return o


class Prog:
    def __init__(self, nc, n_dsem=32):
        self.nc = nc
        self.T = {}
        self.phases = []
        self.K = n_dsem
        self.gallocs = []

    def phase(self, name):
        p = Phase(name)
        self.phases.append(p)
        return p

    def gsb(self, name, shape, dt):
        self.gallocs.append((name, "sb", list(shape), dt))

    def analyze(self):
        K = self.K
        res = {}
        know = {e: ({}, {}) for e in ENGS}
        cnt = {e: 0 for e in ENGS}
        last_op = {e: None for e in ENGS}
        slot_last = [None] * K
        slot_uses = [0] * K
        dma_i = 0
        all_ops = {e: [] for e in ENGS}

        def merge(dst, src):
            for k, v in src[0].items():
                if dst[0].get(k, -1) < v:
                    dst[0][k] = v
            for k, v in src[1].items():
                if dst[1].get(k, -1) < v:
                    dst[1][k] = v

        def need(E, X, P):
            if P is None or P is X:
                return
            kn = know[E]
            if P.is_dma:
                if kn[1].get(P.slot, 0) >= P.target:
                    return
                X.waits.append(P)
                kn[1][P.slot] = P.target
                merge(kn, P.clock)
            else:
                if P.eng == E and E == "pe":
                    return
                if kn[0].get(P.eng, -1) >= P.pos:
                    return
                P.signal = True
                X.waits.append(P)
                kn[0][P.eng] = P.pos
                merge(kn, P.clock)

        for ph in self.phases:
            for e in ENGS:
                b = Op(e, None, (), (), False)
                b.barrier = True
                ph.ops.append(b)
            for X in ph.ops:
                E = X.eng
                X.pos = cnt[E]
                cnt[E] += 1
                all_ops[E].append(X)
                if X.barrier:
                    for e2 in ENGS:
                        need(E, X, last_op[e2])
                    for s in range(K):
                        need(E, X, slot_last[s])
                    X.clock = ({}, {})
                    continue
                deps = []
                for k in X.reads:
                    ent = res.get(k)
                    if ent is not None:
                        deps.append(ent[0])
                for k in X.writes:
                    ent = res.get(k)
                    if ent is not None:
                        deps.append(ent[0])
                        deps.extend(ent[1])
                if X.is_dma:
                    s = dma_i % K
                    dma_i += 1
                    need(E, X, slot_last[s])
                    slot_uses[s] += 1
                    X.slot = s
                    X.target = 16 * slot_uses[s]
                    slot_last[s] = X
                for P in deps:
                    need(E, X, P)
                X.clock = (dict(know[E][0]), dict(know[E][1]))
                for k in X.reads:
                    ent = res.get(k)
                    if ent is None:
                        res[k] = [None, [X]]
                    else:
                        ent[1].append(X)
                for k in X.writes:
                    res[k] = [X, []]
                if not X.is_dma:
                    last_op[E] = X
        for e in ENGS:
            r = 0
            for o in all_ops[e]:
                if o.signal:
                    r += 1
                    o.rank = r
        self.stats = {e: (len(all_ops[e]), sum(1 for o in all_ops[e] if o.signal),
                          sum(len(o.waits) for o in all_ops[e])) for e in ENGS}

    def emit(self):
        nc = self.nc
        self.analyze()
        T = self.T
        with ExitStack() as st:
            sem = {e: st.enter_context(nc.semaphore("s_" + e)) for e in ENGS}
            dsem = [st.enter_context(nc.semaphore("d%d" % i)) for i in range(self.K)]
            for (name, kind, shape, dt) in self.gallocs:
                T[name] = st.enter_context(nc.sbuf_tensor("t_" + name, shape, dt))
            for ph in self.phases:
                with ExitStack() as st2:
                    for (name, kind, shape, dt) in ph.allocs:
                        if kind == "sb":
                            T[name] = st2.enter_context(nc.sbuf_tensor("t_%s_%s" % (ph.name, name), shape, dt))
                        else:
                            T[name] = st2.enter_context(nc.psum_tensor("t_%s_%s" % (ph.name, name), shape, dt))
                    with nc.Block() as blk:
                        for e in ENGS:
                            ops = [o for o in ph.ops if o.eng == e]

                            def body(eng, ops=ops, e=e):
                                for o in ops:
                                    for P in o.waits:
                                        if P.is_dma:
                                            eng.wait_ge(dsem[P.slot], P.target)
                                        else:
                                            eng.wait_ge(sem[P.eng], P.rank)
                                    if o.barrier:
                                        continue
                                    ins = o.fn(eng, T)
                                    if o.is_dma:
                                        ins.then_inc(dsem[o.slot], 16)
                                    elif o.signal:
                                        ins.then_inc(sem[e], 1)

                            getattr(blk, BLK[e])(body)
                    for (name, kind, shape, dt) in ph.allocs:
                        T.pop(name, None)


def _bf(a):
    return np.ascontiguousarray(a.astype(ml_dtypes.bfloat16))


def perm_tokens(g):
    dil = DILS[g]
    L = S // dil
    n = np.arange(S)
    return (n % L) * dil + (n // L)


def host_consts():
    c = {}
    c["ident"] = _bf(np.eye(128, dtype=np.float32))
    c["identf"] = np.eye(128, dtype=np.float32)
    r = np.arange(128)
    partner = np.where(r % 16 < 8, r + 8, r - 8)
    pw = np.zeros((128, 128), np.float32)
    pw[partner, r] = 1.0
    c["pswap"] = _bf(pw)
    half = 8
    inv_freq = (np.float32(500000.0) ** (-np.arange(half, dtype=np.float32) * np.float32(2.0) / np.float32(16))).astype(np.float32)
    tabs = []
    for g in range(3):
        pos = perm_tokens(g).astype(np.float32)
        ang = (pos[:, None] * inv_freq[None, :]).astype(np.float32)
        cos = np.cos(ang).astype(np.float32).T
        sin = np.sin(ang).astype(np.float32).T
        fi = r % 8
        sgn = np.where(r % 16 < 8, -1.0, 1.0).astype(np.float32)
        tab = np.stack([cos[fi], sin[fi] * sgn[:, None]], axis=1)
        tabs.append(tab.astype(np.float32))
    c["rottab"] = np.ascontiguousarray(np.stack(tabs, 0))
    k = np.arange(128)[:, None]
    q = np.arange(128)[None, :]
    ncur = np.where(k > q, NEG, 0.0).astype(np.float32)
    nprev = np.where(k < q, NEG, 0.0).astype(np.float32)
    nall = np.full((128, 128), NEG, np.float32)
    z = np.zeros((128, 128), np.float32)
    m0 = np.concatenate([ncur, nprev, ncur, nprev], 1)
    m1 = np.concatenate([ncur, nall, ncur, nprev], 1)
    m2 = np.concatenate([ncur, z, z, z], 1)
    c["masks"] = _bf(np.stack([m0, m1, m2], 1))
    return c


def build(stop_after=None, dbg=()):
    nc = bass.Bass("TRN2", target_bir_lowering=False)
    dbg = set(dbg)

    def din(name, shape, dt):
        return nc.dram_tensor(name, list(shape), dt, kind="ExternalInput").ap()

    def dscr(name, shape, dt):
        kind = "ExternalOutput" if name in dbg else "Internal"
        return nc.dram_tensor(name, list(shape), dt, kind=kind).ap()

    x = din("x", [S, D], F32)
    a_norm = din("a_norm", [1, D], F32)
    a_w_in = din("a_w_in", [D, 9216], F32)
    a_w_out = din("a_w_out", [D, D], F32)
    b_norm = din("b_norm", [1, D], F32)
    b_w_in = din("b_w_in", [D, 3088], F32)
    b_f = din("b_f", [NH, 1], F32)
    b_w_out = din("b_w_out", [D, D], F32)
    ffn_norm = din("ffn_norm", [2, D], F32)
    ffn_w_gu = din("ffn_w_gu", [2, D, 2 * DFF], F32)
    ffn_w_down = din("ffn_w_down", [2, DFF, D], F32)
    final_norm = din("final_norm", [1, D], F32)
    c_ident = din("ident", [128, 128], BF16)
    c_identf = din("identf", [128, 128], F32)
    c_pswap = din("pswap", [128, 128], BF16)
    c_rottab = din("rottab", [3, 128, 2, S], F32)
    c_masks = din("masks", [128, 3, 512], BF16)

    out = nc.dram_tensor("out", [S, D], F32, kind="ExternalOutput").ap()

    qt0 = [dscr("qt0_%d" % g, [NH, HD, S], BF16) for g in range(3)]
    kt0 = [dscr("kt0_%d" % g, [NH, HD, S], BF16) for g in range(3)]
    v0 = [dscr("v0_%d" % g, [S, D], BF16) for g in range(3)]
    att_scr = dscr("att_scr", [D, S], BF16)
    h_scr = dscr("h_scr", [S, D], F32)
    qt1 = dscr("qt1", [NH, HD + 1, S], BF16)
    kt1 = dscr("kt1", [NH, HD + 1, S], BF16)
    v1 = dscr("v1", [S, D], BF16)
    cs_scr = dscr("cs_scr", [128, NT * NH], F32)
    rden_scr = dscr("rden_scr", [NH, S], F32)

    pg = Prog(nc)
    pg.gsb("xnT", [128, KC, S], BF16)
    pg.gsb("ident", [128, 128], BF16)
    pg.gsb("gam", [128, D], F32)

    def tok_rows(src, g, j):
        dil = DILS[g]
        L = S // dil
        n0 = 128 * j
        r = n0 // L
        p0 = n0 % L
        start = p0 * dil + r
        if dil == 1:
            return src[start:start + 128, :]
        return src[start:start + 127 * dil + 1:dil, :]

    def load_consts(P):
        P.dma("sp", lambda e, T: e.dma_start(out=T["ident"][:, :], in_=c_ident[:, :]), w=["ident"])

    def norm_phase(name, src, gamma_row, g, transpose=True, dst=None):
        P = pg.phase(name)
        P.sb("n_ht", [128, 3, D], F32)
        P.sb("n_junk", [128, D], BF16)
        P.sb("n_ss", [128, NT], F32)
        P.sb("n_rstd", [128, NT], F32)
        if transpose:
            P.sb("n_xn", [128, 2, D], BF16)
            P.ps("n_pt0", [128, D], BF16)
            P.ps("n_pt1", [128, D], BF16)
        else:
            P.sb("n_o", [128, 2, D], F32)
        P.dma("sp", lambda e, T: e.dma_start(out=T["gam"][:, :], in_=gamma_row.partition_broadcast(128)),
              w=["gam"])
        for j in range(NT):
            b = j % 3
            P.dma("sp", lambda e, T, j=j, b=b: e.dma_start(out=T["n_ht"][:, b, :], in_=tok_rows(src, g, j)),
                  w=[("n_ht", b)])
            P.op("act", lambda e, T, j=j, b=b: e.activation(out=T["n_junk"][:, :], in_=T["n_ht"][:, b, :],
                                                             func=AF.Square, accum_out=T["n_ss"][:, j:j + 1]),
                 r=[("n_ht", b)], w=[("n_ss", j)])
        allss = [("n_ss", j) for j in range(NT)]
        P.op("dve", lambda e, T: e.tensor_scalar(out=T["n_rstd"][:, :], in0=T["n_ss"][:, :], scalar1=1.0 / D,
                                                  scalar2=EPS, op0=ALU.mult, op1=ALU.add),
             r=allss, w=["n_var"])
        P.op("act", lambda e, T: e.activation(out=T["n_rstd"][:, :], in_=T["n_rstd"][:, :], func=AF.Sqrt),
             r=["n_var"], w=["n_std"])
        P.op("dve", lambda e, T: e.reciprocal(out=T["n_rstd"][:, :], in_=T["n_rstd"][:, :]),
             r=["n_std"], w=["n_rstd"])
        for j in range(NT):
            b = j % 3
            P.dma("sp", lambda e, T, j=j, b=b: e.dma_start(out=T["n_ht"][:, b, :], in_=tok_rows(src, g, j)),
                  w=[("n_ht", b)])
            if transpose:
                xb = j % 2
                P.op("dve", lambda e, T, j=j, b=b, xb=xb: e.scalar_tensor_tensor(
                    out=T["n_xn"][:, xb, :], in0=T["n_ht"][:, b, :], scalar=T["n_rstd"][:, j:j + 1],
                    in1=T["gam"][:, :], op0=ALU.mult, op1=ALU.mult),
                    r=[("n_ht", b), "n_rstd", "gam"], w=[("n_xn", xb)])
                pt = "n_pt%d" % xb
                for k in range(KC):
                    P.op("pe", lambda e, T, k=k, xb=xb, pt=pt: e.transpose(
                        out=T[pt][:, k * 128:(k + 1) * 128], in_=T["n_xn"][:, xb, k * 128:(k + 1) * 128],
                        identity=T["ident"][:, :]),
                        r=[("n_xn", xb), "ident"], w=[pt])
                P.op("act", lambda e, T, j=j, pt=pt: e.activation(
                    out=T["xnT"][:, :, j * 128:(j + 1) * 128],
                    in_=T[pt][:, :].rearrange("p (k t) -> p k t", k=KC), func=AF.Copy),
                    r=[pt], w=[("xnT", j)])
            else:
                ob = j % 2
                P.op("dve", lambda e, T, j=j, b=b, ob=ob: e.scalar_tensor_tensor(
                    out=T["n_o"][:, ob, :], in0=T["n_ht"][:, b, :], scalar=T["n_rstd"][:, j:j + 1],
                    in1=T["gam"][:, :], op0=ALU.mult, op1=ALU.mult),
                    r=[("n_ht", b), "n_rstd", "gam"], w=[("n_o", ob)])
                P.dma("pool", lambda e, T, j=j, ob=ob: e.dma_start(out=dst[j * 128:(j + 1) * 128, :],
                                                                    in_=T["n_o"][:, ob, :]),
                      r=[("n_o", ob)], w=[("dst", j)])
        return P

    def load_w_slab(P, wname, stname, wsrc_cols, slab, rotperm):
        for k in range(KC):
            sb_ = k % 2
            P.dma("sp", lambda e, T, k=k, sb_=sb_: e.dma_start(out=T[stname][:, sb_, :], in_=wsrc_cols(k)),
                  w=[(stname, sb_)])
            if rotperm:
                P.op("dve", lambda e, T, k=k, sb_=sb_: e.tensor_copy(
                    out=T[wname][:, slab, k, 0:256].rearrange("p (h d) -> p h d", d=16),
                    in_=T[stname][:, sb_, :].rearrange("p (h d) -> p h d", d=64)[:, :, 0:16]),
                    r=[(stname, sb_)], w=[(wname, slab, k, "a")])
                P.op("dve", lambda e, T, k=k, sb_=sb_: e.tensor_copy(
                    out=T[wname][:, slab, k, 256:1024].rearrange("p (h d) -> p h d", d=48),
                    in_=T[stname][:, sb_, :].rearrange("p (h d) -> p h d", d=64)[:, :, 16:64]),
                    r=[(stname, sb_)], w=[(wname, slab, k, "b")])
            else:
                P.op("dve", lambda e, T, k=k, sb_=sb_: e.tensor_copy(out=T[wname][:, slab, k, :],
                                                                      in_=T[stname][:, sb_, :]),
                     r=[(stname, sb_)], w=[(wname, slab, k, "a"), (wname, slab, k, "b")])

    def wkeys(wname, slab):
        ks = []
        for k in range(KC):
            ks.append((wname, slab, k, "a"))
            ks.append((wname, slab, k, "b"))
        return ks

    def seg_list(c):
        segs = []
        if c < 2:
            for hh in range(8):
                segs.append((hh * 16, 16, 8 * c + hh, 0))
        else:
            f0 = 128 * (c - 2)
            f = f0
            while f < f0 + 128:
                h = f // 48
                dd = f % 48
                n = min(48 - dd, f0 + 128 - f)
                segs.append((f - f0, n, h, 16 + dd))
                f += n
        return segs

    def inproj_phase(name, g, wcols, qt_dst, kt_dst, v_dst, tabg, extra=None, rotary=True):
        P = pg.phase(name)
        P.sb("w_st", [128, 2, 1024], F32)
        P.sb("w_sl", [128, 2, KC, 1024], BF16)
        P.sb("stage", [128, 2, S], BF16)
        P.sb("raw", [128, 2, 512], BF16)
        P.sb("cs", [128, 2, 2, 512], F32)
        P.sb("t1", [128, 2, 512], F32)
        P.sb("t2", [128, 2, 512], F32)
        P.sb("pswap", [128, 128], BF16)
        P.sb("vst", [128, 2, D], BF16)
        for i in range(3):
            P.ps("pq%d" % i, [128, 512], F32)
        for i in range(2):
            P.ps("psw%d" % i, [128, 512], F32)
        P.dma("sp", lambda e, T: e.dma_start(out=T["pswap"][:, :], in_=c_pswap[:, :]), w=["pswap"])
        cnt = {"pq": 0, "rot": 0, "stage": 0}
        slab_i = [0]

        def next_slab(t, rotperm):
            sl = slab_i[0] % 2
            slab_i[0] += 1
            load_w_slab(P, "w_sl", "w_st", lambda k, t=t: wcols(t, k), sl, rotperm)
            return sl

        pending = []

        def flush_pending():
            while pending:
                pending.pop(0)()

        sl_next = next_slab(0, True)
        for t in range(2):
            sl = sl_next
            sl_next = next_slab(t + 1, t + 1 < 2)
            dst = qt_dst if t == 0 else kt_dst
            for c in range(8):
                sg = cnt["stage"] % 2
                cnt["stage"] += 1
                for tt in range(8):
                    pi = cnt["pq"] % 3
                    cnt["pq"] += 1
                    pq = "pq%d" % pi
                    xk = [("xnT", 4 * tt + i) for i in range(4)]
                    for k in range(KC):
                        P.op("pe", lambda e, T, k=k, sl=sl, c=c, tt=tt, pq=pq: e.matmul(
                            T[pq][:, :], lhsT=T["w_sl"][:, sl, k, c * 128:(c + 1) * 128],
                            rhs=T["xnT"][:, k, tt * 512:(tt + 1) * 512], start=(k == 0), stop=(k == KC - 1)),
                            r=xk + [("w_sl", sl, k, "a" if c < 2 else "b")], w=[pq])
                    if c >= 2 or not rotary:
                        P.op("act", lambda e, T, sg=sg, tt=tt, pq=pq: e.activation(
                            out=T["stage"][:, sg, tt * 512:(tt + 1) * 512], in_=T[pq][:, :], func=AF.Copy),
                            r=[pq], w=[("stage", sg, tt)])
                    else:
                        ri = cnt["rot"] % 2
                        cnt["rot"] += 1
                        P.op("act", lambda e, T, ri=ri, pq=pq: e.activation(
                            out=T["raw"][:, ri, :], in_=T[pq][:, :], func=AF.Copy),
                            r=[pq], w=[("raw", ri)])
                        P.dma("sp", lambda e, T, ri=ri, tt=tt: e.dma_start(
                            out=T["cs"][:, ri, :, :], in_=c_rottab[tabg, :, :, tt * 512:(tt + 1) * 512]),
                            w=[("cs", ri)])

                        def rot(ri=ri, sg=sg, tt=tt):
                            psw = "psw%d" % ri
                            P.op("pe", lambda e, T: e.matmul(T[psw][:, :], lhsT=T["pswap"][:, :],
                                                              rhs=T["raw"][:, ri, :], start=True, stop=True),
                                 r=[("raw", ri), "pswap"], w=[psw])
                            P.op("dve", lambda e, T: e.tensor_tensor(out=T["t1"][:, ri, :], in0=T["raw"][:, ri, :],
                                                                      in1=T["cs"][:, ri, 0, :], op=ALU.mult),
                                 r=[("raw", ri), ("cs", ri)], w=[("t1", ri)])
                            P.op("dve", lambda e, T: e.tensor_tensor(out=T["t2"][:, ri, :], in0=T[psw][:, :],
                                                                      in1=T["cs"][:, ri, 1, :], op=ALU.mult),
                                 r=[psw, ("cs", ri)], w=[("t2", ri)])
                            P.op("dve", lambda e, T: e.tensor_tensor(
                                out=T["stage"][:, sg, tt * 512:(tt + 1) * 512], in0=T["t1"][:, ri, :],
                                in1=T["t2"][:, ri, :], op=ALU.add),
                                r=[("t1", ri), ("t2", ri)], w=[("stage", sg, tt)])
                        flush_pending()
                        pending.append(rot)
                flush_pending()
                for (r0, n, h, d0) in seg_list(c):
                    P.dma("pool", lambda e, T, sg=sg, r0=r0, n=n, h=h, d0=d0, dst=dst: e.dma_start(
                        out=dst[h, d0:d0 + n, :], in_=T["stage"][r0:r0 + n, sg, :]),
                        r=[("stage", sg, tt) for tt in range(8)], w=[("qkdst", t, h, d0)])
        sl = sl_next
        for b in range(NT):
            vb = b % 2
            for s in range(2):
                pi = cnt["pq"] % 3
                cnt["pq"] += 1
                pq = "pq%d" % pi
                for k in range(KC):
                    P.op("pe", lambda e, T, k=k, sl=sl, b=b, s=s, pq=pq: e.matmul(
                        T[pq][:, :], lhsT=T["xnT"][:, k, b * 128:(b + 1) * 128],
                        rhs=T["w_sl"][:, sl, k, s * 512:(s + 1) * 512], start=(k == 0), stop=(k == KC - 1)),
                        r=[("xnT", b), ("w_sl", sl, k, "a"), ("w_sl", sl, k, "b")], w=[pq])
                P.op("act", lambda e, T, vb=vb, s=s, pq=pq: e.activation(
                    out=T["vst"][:, vb, s * 512:(s + 1) * 512], in_=T[pq][:, :], func=AF.Copy),
                    r=[pq], w=[("vst", vb, s)])
            P.dma("pool", lambda e, T, vb=vb, b=b: e.dma_start(out=v_dst[b * 128:(b + 1) * 128, :],
                                                                in_=T["vst"][:, vb, :]),
                  r=[("vst", vb, 0), ("vst", vb, 1)], w=[("vdst", b)])
        if extra is not None:
            extra(P)
        return P

    def att_alloc(P, krows):
        P.sb("mk", [128, 3, 512], BF16)
        P.sb("qT", [krows, 2, S], BF16)
        P.sb("kT", [krows, 2, S], BF16)
        P.sb("vv", [128, 2, NT, 128], BF16)
        P.sb("pt", [128, 4, 512], BF16)
        P.sb("acc", [HD + 1, 2, S], F32)
        P.sb("bc", [HD, S], F32)
        P.sb("ot", [HD, 2, S], BF16)
        for i in range(4):
            P.ps("pss%d" % i, [128, 512], F32)
        for i in range(2):
            P.ps("pso%d" % i, [128, 512], F32)
        P.dma("sp", lambda e, T: e.dma_start(out=T["mk"][:, :, :], in_=c_masks[:, :, :]), w=["mk"])
        for b in range(2):
            P.op("dve", lambda e, T, b=b: e.memset(T["vv"][:, b, :, :], 1.0), w=[("vv1", b), ("vv", b, 0), ("vv", b, 1)])

    def att_load(P, b, qsrc, ksrc, vsrc, h, krows):
        P.dma("sp", lambda e, T: e.dma_start(out=T["qT"][0:krows, b, :], in_=qsrc[h, :, :]), w=[("qT", b)])
        P.dma("sp", lambda e, T: e.dma_start(out=T["kT"][0:krows, b, :], in_=ksrc[h, :, :]), w=[("kT", b)])
        for hf in range(2):
            P.dma("sp", lambda e, T, hf=hf: e.dma_start(
                out=T["vv"][:, b, hf * 16:(hf + 1) * 16, 0:HD],
                in_=vsrc[hf * 2048:(hf + 1) * 2048, h * HD:(h + 1) * HD].rearrange("(n p) d -> p n d", p=128)),
                w=[("vv", b, hf)])

    def att_finish(P, h, a):
        P.op("dve", lambda e, T: e.reciprocal(out=T["acc"][HD:HD + 1, a, :], in_=T["acc"][HD:HD + 1, a, :]),
             r=[("acc", a)], w=[("acc", a)])
        P.dma("pool", lambda e, T: e.dma_start(out=rden_scr[h:h + 1, :], in_=T["acc"][HD:HD + 1, a, :]),
              r=[("acc", a)], w=[("rden", h)])
        P.dma("sp", lambda e, T: e.dma_start(out=T["bc"][:, :],
                                              in_=rden_scr[h:h + 1, :].partition_broadcast(HD)),
              r=[("rden", h)], w=["bc"])
        P.op("dve", lambda e, T: e.tensor_tensor(out=T["ot"][:, a, :], in0=T["acc"][0:HD, a, :],
                                                  in1=T["bc"][:, :], op=ALU.mult),
             r=[("acc", a), "bc"], w=[("ot", a)])
        P.dma("pool", lambda e, T: e.dma_start(out=att_scr[h * HD:(h + 1) * HD, :], in_=T["ot"][:, a, :]),
              r=[("ot", a)], w=[("att_scr", h)])

    NPB = 4
    LA = 3

    def pipeline(items, stage1, stage2):
        n = len(items)
        for k in range(n + LA):
            if k < n:
                stage1(k, items[k])
            if k >= LA:
                stage2(k - LA, items[k - LA])

    def l0_att_phase():
        P = pg.phase("l0att")
        att_alloc(P, HD)
        units = [(h, g) for h in range(NH) for g in range(3)]
        for u0 in range(2):
            att_load(P, u0, qt0[units[u0][1]], kt0[units[u0][1]], v0[units[u0][1]], units[u0][0], HD)
        items = [(ui, pr) for ui in range(len(units)) for pr in range(16)]

        def stage1(k, it):
            ui, pr = it
            h, g = units[ui]
            b = ui % 2
            nbc = NT // DILS[g]
            ld = [("qT", b), ("kT", b)]
            n0 = 2 * pr
            mi = 1 if (n0 % nbc) == 0 else 0
            pss = "pss%d" % (k % NPB)
            pi = k % NPB
            mms = []
            for e_ in range(2):
                n = n0 + e_
                mms.append((e_ * 256, n, n))
                if n % nbc > 0:
                    mms.append((e_ * 256 + 128, n - 1, n))
            P.op("pe", lambda e, T: e.matmul(T[pss][:, :], lhsT=T["ident"][:, :], rhs=T["mk"][:, mi, :],
                                              start=True, stop=False),
                 r=["ident", "mk"], w=[pss])
            for ii, (c0, kb, qb) in enumerate(mms):
                P.op("pe", lambda e, T, c0=c0, kb=kb, qb=qb, last=(ii == len(mms) - 1): e.matmul(
                    T[pss][:, c0:c0 + 128], lhsT=T["kT"][0:HD, b, kb * 128:(kb + 1) * 128],
                    rhs=T["qT"][0:HD, b, qb * 128:(qb + 1) * 128], start=False, stop=last),
                    r=ld, w=[pss])
            P.op("act", lambda e, T: e.activation(out=T["pt"][:, pi, :], in_=T[pss][:, :], func=AF.Exp, scale=0.125),
                 r=[pss], w=[("pt", pi)])

        def stage2(k, it):
            ui, pr = it
            h, g = units[ui]
            b = ui % 2
            a = h % 2
            dil = DILS[g]
            nbc = NT // dil
            L = S // dil
            n0 = 2 * pr
            pi = k % NPB
            pso = "pso%d" % ((pr // 2) % 2)
            for e_ in range(2):
                n = n0 + e_
                m = n % nbc
                cols = (n % 4) * 128
                P.op("pe", lambda e, T, cols=cols, n=n, e_=e_, m=m: e.matmul(
                    T[pso][0:HD + 1, cols:cols + 128], lhsT=T["vv"][:, b, n, 0:HD + 1],
                    rhs=T["pt"][:, pi, e_ * 256:e_ * 256 + 128], start=True, stop=(m == 0)),
                    r=[("vv", b, 0), ("vv", b, 1), ("vv1", b), ("pt", pi)], w=[pso])
                if m > 0:
                    P.op("pe", lambda e, T, cols=cols, n=n, e_=e_: e.matmul(
                        T[pso][0:HD + 1, cols:cols + 128], lhsT=T["vv"][:, b, n - 1, 0:HD + 1],
                        rhs=T["pt"][:, pi, e_ * 256 + 128:e_ * 256 + 256], start=False, stop=True),
                        r=[("vv", b, 0), ("vv", b, 1), ("vv1", b), ("pt", pi)], w=[pso])
            if pr % 2 == 1:
                B = pr // 2
                if g == 0:
                    P.op("dve", lambda e, T: e.tensor_copy(
                        out=T["acc"][:, a, B * 512:(B + 1) * 512], in_=T[pso][0:HD + 1, :]),
                        r=[pso], w=[("acc", a)])
                else:
                    nsub = 512 // min(512, L)
                    w_ = 512 // nsub
                    for sub in range(nsub):
                        npos = 512 * B + sub * w_
                        r_ = npos // L
                        p0 = npos % L
                        st_ = p0 * dil + r_
                        en_ = st_ + (w_ - 1) * dil + 1
                        P.op("dve", lambda e, T, st_=st_, en_=en_, sub=sub: e.tensor_tensor(
                            out=T["acc"][:, a, st_:en_:dil], in0=T["acc"][:, a, st_:en_:dil],
                            in1=T[pso][0:HD + 1, sub * w_:(sub + 1) * w_], op=ALU.add),
                            r=[pso, ("acc", a)], w=[("acc", a)])
            if pr == 15 and g == 2:
                att_finish(P, h, a)
            if pr == 15 and ui + 2 < len(units):
                h2, g2 = units[ui + 2]
                att_load(P, b, qt0[g2], kt0[g2], v0[g2], h2, HD)

        pipeline(items, stage1, stage2)
        return P

    def outproj_phase(name, wsrc, res_src, res_dst):
        P = pg.phase(name)
        P.sb("w_st", [128, 2, D], F32)
        P.sb("wo", [128, KC, D], BF16)
        P.sb("xt", [128, 3, D], F32)
        for i in range(4):
            P.ps("po%d" % i, [128, 512], F32)
        for k in range(KC):
            P.dma("sp", lambda e, T, k=k: e.dma_start(out=T["xnT"][:, k, :], in_=att_scr[k * 128:(k + 1) * 128, :]),
                  w=[("attT", k)])
        for k in range(KC):
            sb_ = k % 2
            P.dma("sp", lambda e, T, k=k, sb_=sb_: e.dma_start(out=T["w_st"][:, sb_, :],
                                                                in_=wsrc[k * 128:(k + 1) * 128, :]),
                  w=[("w_st", sb_)])
            P.op("dve", lambda e, T, k=k, sb_=sb_: e.tensor_copy(out=T["wo"][:, k, :], in_=T["w_st"][:, sb_, :]),
                 r=[("w_st", sb_)], w=[("wo", k)])
        for j in range(NT):
            xb = j % 3
            P.dma("sp", lambda e, T, j=j, xb=xb: e.dma_start(out=T["xt"][:, xb, :],
                                                              in_=res_src[j * 128:(j + 1) * 128, :]),
                  r=[("hs", j)], w=[("xt", xb)])
            for s_ in range(2):
                po = "po%d" % ((2 * j + s_) % 4)
                for k in range(KC):
                    P.op("pe", lambda e, T, j=j, s_=s_, k=k, po=po: e.matmul(
                        T[po][:, :], lhsT=T["xnT"][:, k, j * 128:(j + 1) * 128],
                        rhs=T["wo"][:, k, s_ * 512:(s_ + 1) * 512], start=(k == 0), stop=(k == KC - 1)),
                        r=[("attT", k), ("wo", k)], w=[po])
                P.op("dve", lambda e, T, xb=xb, s_=s_, po=po: e.tensor_tensor(
                    out=T["xt"][:, xb, s_ * 512:(s_ + 1) * 512], in0=T["xt"][:, xb, s_ * 512:(s_ + 1) * 512],
                    in1=T[po][:, :], op=ALU.add),
                    r=[po, ("xt", xb)], w=[("xt", xb)])
            P.dma("pool", lambda e, T, j=j, xb=xb: e.dma_start(out=res_dst[j * 128:(j + 1) * 128, :],
                                                                in_=T["xt"][:, xb, :]),
                  r=[("xt", xb)], w=[("hs", j)])
        return P

    def ffn_phase(name, li):
        P = pg.phase(name)
        P.sb("wd", [128, NJ, D], BF16)
        P.sb("actT", [128, NJ, 1024], BF16)
        P.sb("wgs", [128, 2, 2048], F32)
        P.sb("wg", [128, 2, 2048], BF16)
        P.sb("sg", [128, 2, 512], F32)
        P.sb("xt", [128, 2, D], F32)
        for i in range(8):
            P.ps("pb%d" % i, [128, 512], F32)
        wgu = ffn_w_gu[li]
        wdn = ffn_w_down[li]
        c = {"st": 0, "wg": 0, "sg": 0, "xt": 0}

        def stg():
            b = c["st"] % 2
            c["st"] += 1
            return b

        for j in range(NJ):
            b = stg()
            P.dma("sp", lambda e, T, j=j, b=b: e.dma_start(out=T["wgs"][:, b, 0:D], in_=wdn[j * 128:(j + 1) * 128, :]),
                  w=[("wgs", b, 0), ("wgs", b, 1)])
            P.op("dve", lambda e, T, j=j, b=b: e.tensor_copy(out=T["wd"][:, j, :], in_=T["wgs"][:, b, 0:D]),
                 r=[("wgs", b, 0), ("wgs", b, 1)], w=[("wd", j)])

        def load_gu(j):
            b = stg()
            b2 = c["wg"] % 2
            c["wg"] += 1
            P.dma("sp", lambda e, T: e.dma_start(
                out=T["wgs"][:, b, :].rearrange("p (k c) -> p k c", k=KC)[:, :, 0:128],
                in_=wgu[:, j * 128:(j + 1) * 128].rearrange("(k p) c -> p k c", p=128)), w=[("wgs", b, 0)])
            P.dma("sp", lambda e, T: e.dma_start(
                out=T["wgs"][:, b, :].rearrange("p (k c) -> p k c", k=KC)[:, :, 128:256],
                in_=wgu[:, DFF + j * 128:DFF + (j + 1) * 128].rearrange("(k p) c -> p k c", p=128)),
                w=[("wgs", b, 1)])
            P.op("dve", lambda e, T: e.tensor_copy(out=T["wg"][:, b2, :], in_=T["wgs"][:, b, :]),
                 r=[("wgs", b, 0), ("wgs", b, 1)], w=[("wg", b2)])
            return b2

        for st_ in range(4):
            nxt = load_gu(0)
            for j in range(NJ):
                b2 = nxt
                if j + 1 < NJ:
                    nxt = load_gu(j + 1)
                set_ = j % 2
                for half in range(2):
                    tok0 = st_ * 1024 + half * 512
                    xk = [("xnT", tok0 // 128 + i) for i in range(4)]
                    pgb = "pb%d" % (4 * set_ + half)
                    pub = "pb%d" % (4 * set_ + 2 + half)
                    for (bank, off) in ((pgb, 0), (pub, 128)):
                        for k in range(KC):
                            P.op("pe", lambda e, T, bank=bank, off=off, k=k, b2=b2, tok0=tok0: e.matmul(
                                T[bank][:, :],
                                lhsT=T["wg"][:, b2, :].rearrange("p (k c) -> p k c", k=KC)[:, k, off:off + 128],
                                rhs=T["xnT"][:, k, tok0:tok0 + 512], start=(k == 0), stop=(k == KC - 1)),
                                r=xk + [("wg", b2)], w=[bank])
                    sgi = c["sg"] % 2
                    c["sg"] += 1
                    P.op("act", lambda e, T, pgb=pgb, sgi=sgi: e.activation(out=T["sg"][:, sgi, :], in_=T[pgb][:, :],
                                                                             func=AF.Silu),
                         r=[pgb], w=[("sg", sgi)])
                    P.op("dve", lambda e, T, pub=pub, sgi=sgi, j=j, half=half: e.tensor_tensor(
                        out=T["actT"][:, j, half * 512:(half + 1) * 512], in0=T["sg"][:, sgi, :], in1=T[pub][:, :],
                        op=ALU.mult),
                        r=[pub, ("sg", sgi)], w=[("actT", j, half)])
            for tb in range(8):
                jt = st_ * 8 + tb
                xb = c["xt"] % 2
                c["xt"] += 1
                P.dma("sp", lambda e, T, jt=jt, xb=xb: e.dma_start(out=T["xt"][:, xb, :],
                                                                    in_=h_scr[jt * 128:(jt + 1) * 128, :]),
                      r=[("hs", jt)], w=[("xt", xb)])
                for s_ in range(2):
                    bank = "pb%d" % ((2 * tb + s_) % 8)
                    for j in range(NJ):
                        P.op("pe", lambda e, T, bank=bank, j=j, tb=tb, s_=s_: e.matmul(
                            T[bank][:, :], lhsT=T["actT"][:, j, tb * 128:(tb + 1) * 128],
                            rhs=T["wd"][:, j, s_ * 512:(s_ + 1) * 512], start=(j == 0), stop=(j == NJ - 1)),
                            r=[("actT", j, tb // 4), ("wd", j)], w=[bank])
                    P.op("dve", lambda e, T, bank=bank, xb=xb, s_=s_: e.tensor_tensor(
                        out=T["xt"][:, xb, s_ * 512:(s_ + 1) * 512], in0=T["xt"][:, xb, s_ * 512:(s_ + 1) * 512],
                        in1=T[bank][:, :], op=ALU.add),
                        r=[bank, ("xt", xb)], w=[("xt", xb)])
                P.dma("pool", lambda e, T, jt=jt, xb=xb: e.dma_start(out=h_scr[jt * 128:(jt + 1) * 128, :],
                                                                      in_=T["xt"][:, xb, :]),
                      r=[("xt", xb)], w=[("hs", jt)])
        return P

    def gate_extra(P):
        P.sb("wfs", [128, KC, 128], F32)
        P.sb("wfb", [128, KC, NH], BF16)
        P.sb("negb", [NH, 1], F32)
        P.sb("lf", [NH, S], F32)
        P.sb("csum", [NH, S], F32)
        P.sb("qrow", [NH, S], BF16)
        P.sb("identf", [128, 128], F32)
        P.sb("cst", [128, NT * NH], F32)
        P.ps("psf", [128, 512], F32)
        P.ps("pct", [128, 512], F32)
        P.dma("sp", lambda e, T: e.dma_start(out=T["identf"][:, :], in_=c_identf[:, :]), w=["identf"])
        P.dma("sp", lambda e, T: e.dma_start(
            out=T["wfs"][:, :, 0:NH], in_=b_w_in[:, 3 * D:3 * D + NH].rearrange("(k p) c -> p k c", p=128)), w=["wfs"])
        P.op("dve", lambda e, T: e.tensor_copy(out=T["wfb"][:, :, :], in_=T["wfs"][:, :, 0:NH]), r=["wfs"], w=["wfb"])
        P.dma("sp", lambda e, T: e.dma_start(out=T["negb"][:, :], in_=b_f[:, :]), w=["negb0"])
        P.op("dve", lambda e, T: e.tensor_scalar(out=T["negb"][:, :], in0=T["negb"][:, :], scalar1=-1.0, scalar2=None,
                                                  op0=ALU.mult), r=["negb0"], w=["negb"])
        for tt in range(8):
            xk = [("xnT", 4 * tt + i) for i in range(4)]
            for k in range(KC):
                P.op("pe", lambda e, T, k=k, tt=tt: e.matmul(
                    T["psf"][0:NH, :], lhsT=T["wfb"][:, k, :], rhs=T["xnT"][:, k, tt * 512:(tt + 1) * 512],
                    start=(k == 0), stop=(k == KC - 1)), r=xk + ["wfb"], w=["psf"])
            P.op("act", lambda e, T, tt=tt: e.activation(out=T["lf"][:, tt * 512:(tt + 1) * 512], in_=T["psf"][0:NH, :],
                                                          func=AF.Exp, bias=T["negb"][:, 0:1], scale=-1.0),
                 r=["psf", "negb"], w=[("lf", tt)])
        P.op("act", lambda e, T: e.activation(out=T["lf"][:, :], in_=T["lf"][:, :], func=AF.Ln, bias=1.0),
             r=[("lf", tt) for tt in range(8)], w=["lf"])
        P.op("dve", lambda e, T: e.tensor_tensor_scan(out=T["csum"][:, :], data0=T["lf"][:, :], data1=T["lf"][:, :],
                                                       initial=0.0, op0=ALU.add, op1=ALU.max),
             r=["lf"], w=["csum"])
        P.op("dve", lambda e, T: e.tensor_scalar(out=T["qrow"][:, :], in0=T["csum"][:, :], scalar1=-8.0, scalar2=None,
                                                  op0=ALU.mult), r=["csum"], w=["qrow"])
        P.dma("pool", lambda e, T: e.dma_start(out=qt1[:, HD, :], in_=T["qrow"][:, :]), r=["qrow"], w=["qt1row"])
        P.op("dve", lambda e, T: e.memset(T["qrow"][:, :], 1.0), r=["qrow"], w=["qrow"])
        P.dma("pool", lambda e, T: e.dma_start(out=kt1[:, HD, :], in_=T["qrow"][:, :]), r=["qrow"], w=["kt1row"])
        for blk in range(NT):
            P.op("pe", lambda e, T, blk=blk: e.transpose(
                out=T["pct"][:, blk * NH:(blk + 1) * NH], in_=T["csum"][0:NH, blk * 128:(blk + 1) * 128],
                identity=T["identf"][0:NH, 0:NH]), r=["csum", "identf"], w=["pct"])
        P.op("act", lambda e, T: e.activation(out=T["cst"][:, :], in_=T["pct"][:, :], func=AF.Copy),
             r=["pct"], w=["cst"])
        P.dma("pool", lambda e, T: e.dma_start(out=cs_scr[:, :], in_=T["cst"][:, :]), r=["cst"], w=["cs_scr"])

    def l1_att_phase():
        P = pg.phase("l1att")
        KR = HD + 1
        att_alloc(P, KR)
        P.sb("cst", [128, NT * NH], F32)
        P.dma("sp", lambda e, T: e.dma_start(out=T["cst"][:, :], in_=cs_scr[:, :]), w=["cst"])
        att_load(P, 0, qt1, kt1, v1, 0, KR)
        att_load(P, 1, qt1, kt1, v1, 1, KR)
        items = [(h, Qi, J) for h in range(NH) for Qi in range(8) for J in range(4 * Qi + 4)]

        def stage1(k, it):
            h, Qi, J = it
            b = h % 2
            ld = [("qT", b), ("kT", b)]
            d_ = J - 4 * Qi
            c0 = 128 * d_ if d_ > 0 else 0
            W = 512 - c0
            pss = "pss%d" % (k % NPB)
            pi = k % NPB
            if d_ >= 0:
                P.op("pe", lambda e, T: e.matmul(T[pss][:, c0:512], lhsT=T["ident"][:, :], rhs=T["mk"][:, 2, 0:W],
                                                  start=True, stop=False),
                     r=["ident", "mk"], w=[pss])
            P.op("pe", lambda e, T: e.matmul(
                T[pss][:, c0:512], lhsT=T["kT"][0:KR, b, J * 128:(J + 1) * 128],
                rhs=T["qT"][0:KR, b, Qi * 512 + c0:(Qi + 1) * 512], start=(d_ < 0), stop=True),
                r=ld, w=[pss])
            P.op("act", lambda e, T: e.activation(
                out=T["pt"][:, pi, c0:512], in_=T[pss][:, c0:512], func=AF.Exp,
                bias=T["cst"][:, J * NH + h:J * NH + h + 1], scale=0.125),
                r=[pss, "cst"], w=[("pt", pi)])

        def stage2(k, it):
            h, Qi, J = it
            b = h % 2
            a = h % 2
            nJ = 4 * Qi + 4
            d_ = J - 4 * Qi
            c0 = 128 * d_ if d_ > 0 else 0
            pi = k % NPB
            pso = "pso%d" % (Qi % 2)
            P.op("pe", lambda e, T: e.matmul(
                T[pso][0:KR, c0:512], lhsT=T["vv"][:, b, J, 0:HD + 1], rhs=T["pt"][:, pi, c0:512],
                start=(J == 0), stop=(J == nJ - 1)),
                r=[("vv", b, 0), ("vv", b, 1), ("vv1", b), ("pt", pi)], w=[pso])
            if J == nJ - 1:
                P.op("dve", lambda e, T: e.tensor_copy(
                    out=T["acc"][:, a, Qi * 512:(Qi + 1) * 512], in_=T[pso][0:KR, :]),
                    r=[pso], w=[("acc", a)])
                if Qi == 7:
                    att_finish(P, h, a)
                    if h + 2 < NH:
                        att_load(P, b, qt1, kt1, v1, h + 2, KR)

        pipeline(items, stage1, stage2)
        return P

    P0 = pg.phase("consts")
    load_consts(P0)

    def build_all():
        for g in range(3):
            norm_phase("l0n%d" % g, x, a_norm[0:1, :], g)
            inproj_phase("l0p%d" % g, g,
                         lambda t, k, g=g: a_w_in[k * 128:(k + 1) * 128,
                                                  g * 3072 + t * 1024: g * 3072 + (t + 1) * 1024],
                         qt0[g], kt0[g], v0[g], g)
            if stop_after == "l0p%d" % g:
                return
        l0_att_phase()
        if stop_after == "l0att":
            return
        outproj_phase("l0out", a_w_out, x, h_scr)
        if stop_after == "l0out":
            return
        norm_phase("f0n", h_scr, ffn_norm[0:1, :], 0)
        ffn_phase("f0", 0)
        if stop_after == "f0":
            return
        norm_phase("l1n", h_scr, b_norm[0:1, :], 0)
        inproj_phase("l1p", 0, lambda t, k: b_w_in[k * 128:(k + 1) * 128, t * 1024:(t + 1) * 1024],
                     qt1, kt1, v1, 0, extra=gate_extra, rotary=False)
        if stop_after == "l1p":
            return
        l1_att_phase()
        if stop_after == "l1att":
            return
        outproj_phase("l1out", b_w_out, h_scr, h_scr)
        if stop_after == "l1out":
            return
        norm_phase("f1n", h_scr, ffn_norm[1:2, :], 0)
        ffn_phase("f1", 1)
        if stop_after == "f1":
            return
        norm_phase("fin", h_scr, final_norm[0:1, :], 0, transpose=False, dst=out)

    build_all()
    pg.emit()
    return nc, pg


_CACHE = {}


def core_inputs(inp, c, consts):
    f = lambda a: np.ascontiguousarray(np.asarray(a, dtype=np.float32))
    m = {
        "x": f(inp["x"][c]),
        "a_norm": f(inp["a_norm"]).reshape(1, D),
        "a_w_in": f(inp["a_w_in"][0]),
        "a_w_out": f(inp["a_w_out"][0]),
        "b_norm": f(inp["b_norm"]).reshape(1, D),
        "b_w_in": f(inp["b_w_in"][0]),
        "b_f": f(inp["b_f"]).reshape(NH, 1),
        "b_w_out": f(inp["b_w_out"][0]),
        "ffn_norm": f(inp["ffn_norm"]),
        "ffn_w_gu": f(inp["ffn_w_gu"]),
        "ffn_w_down": f(inp["ffn_w_down"]),
        "final_norm": f(inp["final_norm"]).reshape(1, D),
    }
    m.update(consts)
    return m


def kernel(**inputs):
    if "nc" not in _CACHE:
        _CACHE["nc"] = build()[0]
        _CACHE["consts"] = host_consts()
    nc = _CACHE["nc"]
    consts = _CACHE["consts"]
    nb = inputs["x"].shape[0]
    maps = [core_inputs(inputs, c, consts) for c in range(nb)]
    res = run_bass_kernel_spmd(nc, maps, core_ids=list(range(nb)))
    return np.stack([np.asarray(r["out"], dtype=np.float32) for r in res.results], axis=0)
```

```python
from contextlib import ExitStack
import numpy as np
import ml_dtypes
import concourse.bass as bass
import concourse.mybir as mybir
from concourse.bass_utils import run_bass_kernel_spmd

F32 = mybir.dt.float32
BF16 = mybir.dt.bfloat16
AF = mybir.ActivationFunctionType
ALU = mybir.AluOpType

S = 4096
D = 1024
NT = 32
KC = 8
NH = 16
HD = 64
DFF = 2816
NJ = 22
DILS = (1, 4, 16)
EPS = 1e-6
NEG = -30000.0

ENGS = ["pe", "act", "dve", "pool", "sp"]
BLK = {"pe": "tensor", "act": "scalar", "dve": "vector", "pool": "gpsimd", "sp": "sync"}


class Op:
    __slots__ = ("eng", "fn", "reads", "writes", "is_dma", "pos", "waits", "signal",
                 "slot", "target", "clock", "rank", "barrier")

    def __init__(self, eng, fn, reads, writes, is_dma):
        self.eng = eng
        self.fn = fn
        self.reads = reads
        self.writes = writes
        self.is_dma = is_dma
        self.waits = []
        self.signal = False
        self.slot = None
        self.target = None
        self.clock = None
        self.rank = None
        self.barrier = False


class Phase:
    def __init__(self, name):
        self.name = name
        self.allocs = []
        self.ops = []

    def sb(self, name, shape, dt):
        self.allocs.append((name, "sb", list(shape), dt))
        return name

    def ps(self, name, shape, dt):
        self.allocs.append((name, "ps", list(shape), dt))
        return name

    def op(self, eng, fn, r=(), w=()):
        o = Op(eng, fn, tuple(r), tuple(w), False)
        self.ops.append(o)
        return o

    def dma(self, eng, fn, r=(), w=()):
        o = Op(eng, fn, tuple(r), tuple(w), True)
        self.ops.append(o)
        return o


class Prog:
    def __init__(self, nc, n_dsem=32):
        self.nc = nc
        self.T = {}
        self.phases = []
        self.K = n_dsem
        self.gallocs = []

    def phase(self, name):
        p = Phase(name)
        self.phases.append(p)
        return p

    def gsb(self, name, shape, dt):
        self.gallocs.append((name, "sb", list(shape), dt))

    def analyze(self):
        K = self.K
        res = {}
        know = {e: ({}, {}) for e in ENGS}
        cnt = {e: 0 for e in ENGS}
        last_op = {e: None for e in ENGS}
        slot_last = [None] * K
        slot_uses = [0] * K
        dma_i = 0
        all_ops = {e: [] for e in ENGS}

        def merge(dst, src):
            for k, v in src[0].items():
                if dst[0].get(k, -1) < v:
                    dst[0][k] = v
            for k, v in src[1].items():
                if dst[1].get(k, -1) < v:
                    dst[1][k] = v

        def need(E, X, P):
            if P is None or P is X:
                return
            kn = know[E]
            if P.is_dma:
                if kn[1].get(P.slot, 0) >= P.target:
                    return
                X.waits.append(P)
                kn[1][P.slot] = P.target
                merge(kn, P.clock)
            else:
                if P.eng == E and E == "pe":
                    return
                if kn[0].get(P.eng, -1) >= P.pos:
                    return
                P.signal = True
                X.waits.append(P)
                kn[0][P.eng] = P.pos
                merge(kn, P.clock)

        for ph in self.phases:
            for e in ENGS:
                b = Op(e, None, (), (), False)
                b.barrier = True
                ph.ops.append(b)
            for X in ph.ops:
                E = X.eng
                X.pos = cnt[E]
                cnt[E] += 1
                all_ops[E].append(X)
                if X.barrier:
                    for e2 in ENGS:
                        need(E, X, last_op[e2])
                    for s in range(K):
                        need(E, X, slot_last[s])
                    X.clock = ({}, {})
                    continue
                deps = []
                for k in X.reads:
                    ent = res.get(k)
                    if ent is not None:
                        deps.append(ent[0])
                for k in X.writes:
                    ent = res.get(k)
                    if ent is not None:
                        deps.append(ent[0])
                        deps.extend(ent[1])
                if X.is_dma:
                    s = dma_i % K
                    dma_i += 1
                    need(E, X, slot_last[s])
                    slot_uses[s] += 1
                    X.slot = s
                    X.target = 16 * slot_uses[s]
                    slot_last[s] = X
                for P in deps:
                    need(E, X, P)
                X.clock = (dict(know[E][0]), dict(know[E][1]))
                for k in X.reads:
                    ent = res.get(k)
                    if ent is None:
                        res[k] = [None, [X]]
                    else:
                        ent[1].append(X)
                for k in X.writes:
                    res[k] = [X, []]
                if not X.is_dma:
                    last_op[E] = X
        for e in ENGS:
            r = 0
            for o in all_ops[e]:
                if o.signal:
                    r += 1
                    o.rank = r
        self.stats = {e: (len(all_ops[e]), sum(1 for o in all_ops[e] if o.signal),
                          sum(len(o.waits) for o in all_ops[e])) for e in ENGS}

    def emit(self):
        nc = self.nc
        self.analyze()
        T = self.T
        with ExitStack() as st:
            sem = {e: st.enter_context(nc.semaphore("s_" + e)) for e in ENGS}
            dsem = [st.enter_context(nc.semaphore("d%d" % i)) for i in range(self.K)]
            for (name, kind, shape, dt) in self.gallocs:
                T[name] = st.enter_context(nc.sbuf_tensor("t_" + name, shape, dt))
            for ph in self.phases:
                with ExitStack() as st2:
                    for (name, kind, shape, dt) in ph.allocs:
                        if kind == "sb":
                            T[name] = st2.enter_context(nc.sbuf_tensor("t_%s_%s" % (ph.name, name), shape, dt))
                        else:
                            T[name] = st2.enter_context(nc.psum_tensor("t_%s_%s" % (ph.name, name), shape, dt))
                    with nc.Block() as blk:
                        for e in ENGS:
                            ops = [o for o in ph.ops if o.eng == e]

                            def body(eng, ops=ops, e=e):
                                for o in ops:
                                    for P in o.waits:
                                        if P.is_dma:
                                            eng.wait_ge(dsem[P.slot], P.target)
                                        else:
                                            eng.wait_ge(sem[P.eng], P.rank)
                                    if o.barrier:
                                        continue
                                    ins = o.fn(eng, T)
                                    if o.is_dma:
                                        ins.then_inc(dsem[o.slot], 16)
                                    elif o.signal:
                                        ins.then_inc(sem[e], 1)

                            getattr(blk, BLK[e])(body)
                    for (name, kind, shape, dt) in ph.allocs:
                        T.pop(name, None)


def _bf(a):
    return np.ascontiguousarray(a.astype(ml_dtypes.bfloat16))


def perm_tokens(g):
    dil = DILS[g]
    L = S // dil
    n = np.arange(S)
    return (n % L) * dil + (n // L)


def host_consts():
    c = {}
    c["ident"] = _bf(np.eye(128, dtype=np.float32))
    c["identf"] = np.eye(128, dtype=np.float32)
    r = np.arange(128)
    partner = np.where(r % 16 < 8, r + 8, r - 8)
    pw = np.zeros((128, 128), np.float32)
    pw[partner, r] = 1.0
    c["pswap"] = _bf(pw)
    half = 8
    inv_freq = (np.float32(500000.0) ** (-np.arange(half, dtype=np.float32) * np.float32(2.0) / np.float32(16))).astype(np.float32)
    tabs = []
    for g in range(3):
        pos = perm_tokens(g).astype(np.float32)
        ang = (pos[:, None] * inv_freq[None, :]).astype(np.float32)
        cos = np.cos(ang).astype(np.float32).T
        sin = np.sin(ang).astype(np.float32).T
        fi = r % 8
        sgn = np.where(r % 16 < 8, -1.0, 1.0).astype(np.float32)
        tab = np.stack([cos[fi], sin[fi] * sgn[:, None]], axis=1)
        tabs.append(tab.astype(np.float32))
    c["rottab"] = np.ascontiguousarray(np.stack(tabs, 0))
    k = np.arange(128)[:, None]
    q = np.arange(128)[None, :]
    ncur = np.where(k > q, NEG, 0.0).astype(np.float32)
    nprev = np.where(k < q, NEG, 0.0).astype(np.float32)
    nall = np.full((128, 128), NEG, np.float32)
    z = np.zeros((128, 128), np.float32)
    m0 = np.concatenate([ncur, nprev, ncur, nprev], 1)
    m1 = np.concatenate([ncur, nall, ncur, nprev], 1)
    m2 = np.concatenate([ncur, z, z, z], 1)
    c["masks"] = _bf(np.stack([m0, m1, m2], 1))
    return c


def build(stop_after=None, dbg=()):
    nc = bass.Bass("TRN2", target_bir_lowering=False)
    dbg = set(dbg)

    def din(name, shape, dt):
        return nc.dram_tensor(name, list(shape), dt, kind="ExternalInput").ap()

    def dscr(name, shape, dt):
        kind = "ExternalOutput" if name in dbg else "Internal"
        return nc.dram_tensor(name, list(shape), dt, kind=kind).ap()

    x = din("x", [S, D], F32)
    a_norm = din("a_norm", [1, D], F32)
    a_w_in = din("a_w_in", [D, 9216], F32)
    a_w_out = din("a_w_out", [D, D], F32)
    b_norm = din("b_norm", [1, D], F32)
    b_w_in = din("b_w_in", [D, 3088], F32)
    b_f = din("b_f", [NH, 1], F32)
    b_w_out = din("b_w_out", [D, D], F32)
    ffn_norm = din("ffn_norm", [2, D], F32)
    ffn_w_gu = din("ffn_w_gu", [2, D, 2 * DFF], F32)
    ffn_w_down = din("ffn_w_down", [2, DFF, D], F32)
    final_norm = din("final_norm", [1, D], F32)
    c_ident = din("ident", [128, 128], BF16)
    c_identf = din("identf", [128, 128], F32)
    c_pswap = din("pswap", [128, 128], BF16)
    c_rottab = din("rottab", [3, 128, 2, S], F32)
    c_masks = din("masks", [128, 3, 512], BF16)

    out = nc.dram_tensor("out", [S, D], F32, kind="ExternalOutput").ap()

    qt0 = [dscr("qt0_%d" % g, [NH, HD, S], BF16) for g in range(3)]
    kt0 = [dscr("kt0_%d" % g, [NH, HD, S], BF16) for g in range(3)]
    v0 = [dscr("v0_%d" % g, [S, D], BF16) for g in range(3)]
    att_scr = dscr("att_scr", [D, S], BF16)
    h_scr = dscr("h_scr", [S, D], F32)
    qt1 = dscr("qt1", [NH, HD + 1, S], BF16)
    kt1 = dscr("kt1", [NH, HD + 1, S], BF16)
    v1 = dscr("v1", [S, D], BF16)
    cs_scr = dscr("cs_scr", [128, NT * NH], F32)
    rden_scr = dscr("rden_scr", [NH, S], F32)

    pg = Prog(nc)
    pg.gsb("xnT", [128, KC, S], BF16)
    pg.gsb("ident", [128, 128], BF16)
    pg.gsb("gam", [128, D], F32)
    pg.gsb("g_ss", [128, NT], F32)

    def tok_rows(src, g, j):
        dil = DILS[g]
        L = S // dil
        n0 = 128 * j
        r = n0 // L
        p0 = n0 % L
        start = p0 * dil + r
        if dil == 1:
            return src[start:start + 128, :]
        return src[start:start + 127 * dil + 1:dil, :]

    def load_consts(P):
        P.dma("sp", lambda e, T: e.dma_start(out=T["ident"][:, :], in_=c_ident[:, :]), w=["ident"])

    def norm_phase(name, src, gamma_row, g, transpose=True, dst=None, have_ss=False):
        P = pg.phase(name)
        P.sb("n_ht", [128, 3, D], F32)
        P.sb("n_junk", [128, D], BF16)
        P.sb("n_ss", [128, NT], F32)
        P.sb("n_rstd", [128, NT], F32)
        if transpose:
            P.sb("n_xn", [128, 2, D], BF16)
            P.ps("n_pt0", [128, D], BF16)
            P.ps("n_pt1", [128, D], BF16)
        else:
            P.sb("n_o", [128, 2, D], F32)
        P.dma("sp", lambda e, T: e.dma_start(out=T["gam"][:, :], in_=gamma_row.partition_broadcast(128)),
              w=["gam"])
        ssn = "g_ss" if have_ss else "n_ss"
        if not have_ss:
            for j in range(NT):
                b = j % 3
                P.dma("sp", lambda e, T, j=j, b=b: e.dma_start(out=T["n_ht"][:, b, :], in_=tok_rows(src, g, j)),
                      w=[("n_ht", b)])
                P.op("act", lambda e, T, j=j, b=b: e.activation(out=T["n_junk"][:, :], in_=T["n_ht"][:, b, :],
                                                                 func=AF.Square, accum_out=T["n_ss"][:, j:j + 1]),
                     r=[("n_ht", b)], w=[("n_ss", j)])
        allss = [(ssn, j) for j in range(NT)]
        P.op("dve", lambda e, T: e.tensor_scalar(out=T["n_rstd"][:, :], in0=T[ssn][:, :], scalar1=1.0 / D,
                                                  scalar2=EPS, op0=ALU.mult, op1=ALU.add),
             r=allss, w=["n_var"])
        P.op("act", lambda e, T: e.activation(out=T["n_rstd"][:, :], in_=T["n_rstd"][:, :], func=AF.Sqrt),
             r=["n_var"], w=["n_std"])
        P.op("dve", lambda e, T: e.reciprocal(out=T["n_rstd"][:, :], in_=T["n_rstd"][:, :]),
             r=["n_std"], w=["n_rstd"])
        for j in range(NT):
            b = j % 3
            P.dma("sp", lambda e, T, j=j, b=b: e.dma_start(out=T["n_ht"][:, b, :], in_=tok_rows(src, g, j)),
                  w=[("n_ht", b)])
            if transpose:
                xb = j % 2
                P.op("dve", lambda e, T, j=j, b=b, xb=xb: e.scalar_tensor_tensor(
                    out=T["n_xn"][:, xb, :], in0=T["n_ht"][:, b, :], scalar=T["n_rstd"][:, j:j + 1],
                    in1=T["gam"][:, :], op0=ALU.mult, op1=ALU.mult),
                    r=[("n_ht", b), "n_rstd", "gam"], w=[("n_xn", xb)])
                pt = "n_pt%d" % xb
                for k in range(KC):
                    P.op("pe", lambda e, T, k=k, xb=xb, pt=pt: e.transpose(
                        out=T[pt][:, k * 128:(k + 1) * 128], in_=T["n_xn"][:, xb, k * 128:(k + 1) * 128],
                        identity=T["ident"][:, :]),
                        r=[("n_xn", xb), "ident"], w=[pt])
                P.op("act", lambda e, T, j=j, pt=pt: e.activation(
                    out=T["xnT"][:, :, j * 128:(j + 1) * 128],
                    in_=T[pt][:, :].rearrange("p (k t) -> p k t", k=KC), func=AF.Copy),
                    r=[pt], w=[("xnT", j)])
            else:
                ob = j % 2
                P.op("dve", lambda e, T, j=j, b=b, ob=ob: e.scalar_tensor_tensor(
                    out=T["n_o"][:, ob, :], in0=T["n_ht"][:, b, :], scalar=T["n_rstd"][:, j:j + 1],
                    in1=T["gam"][:, :], op0=ALU.mult, op1=ALU.mult),
                    r=[("n_ht", b), "n_rstd", "gam"], w=[("n_o", ob)])
                P.dma("pool", lambda e, T, j=j, ob=ob: e.dma_start(out=dst[j * 128:(j + 1) * 128, :],
                                                                    in_=T["n_o"][:, ob, :]),
                      r=[("n_o", ob)], w=[("dst", j)])
        return P

    def load_w_slab(P, wname, stname, wsrc_cols, slab, rotperm):
        for k in range(KC):
            sb_ = k % 2
            P.dma("sp", lambda e, T, k=k, sb_=sb_: e.dma_start(out=T[stname][:, sb_, :], in_=wsrc_cols(k)),
                  w=[(stname, sb_)])
            if rotperm:
                P.op("dve", lambda e, T, k=k, sb_=sb_: e.tensor_copy(
                    out=T[wname][:, slab, k, 0:256].rearrange("p (h d) -> p h d", d=16),
                    in_=T[stname][:, sb_, :].rearrange("p (h d) -> p h d", d=64)[:, :, 0:16]),
                    r=[(stname, sb_)], w=[(wname, slab, k, "a")])
                P.op("dve", lambda e, T, k=k, sb_=sb_: e.tensor_copy(
                    out=T[wname][:, slab, k, 256:1024].rearrange("p (h d) -> p h d", d=48),
                    in_=T[stname][:, sb_, :].rearrange("p (h d) -> p h d", d=64)[:, :, 16:64]),
                    r=[(stname, sb_)], w=[(wname, slab, k, "b")])
            else:
                P.op("dve", lambda e, T, k=k, sb_=sb_: e.tensor_copy(out=T[wname][:, slab, k, :],
                                                                      in_=T[stname][:, sb_, :]),
                     r=[(stname, sb_)], w=[(wname, slab, k, "a"), (wname, slab, k, "b")])

    def wkeys(wname, slab):
        ks = []
        for k in range(KC):
            ks.append((wname, slab, k, "a"))
            ks.append((wname, slab, k, "b"))
        return ks

    def seg_list(c):
        segs = []
        if c < 2:
            for hh in range(8):
                segs.append((hh * 16, 16, 8 * c + hh, 0))
        else:
            f0 = 128 * (c - 2)
            f = f0
            while f < f0 + 128:
                h = f // 48
                dd = f % 48
                n = min(48 - dd, f0 + 128 - f)
                segs.append((f - f0, n, h, 16 + dd))
                f += n
        return segs

    def inproj_phase(name, g, wcols, qt_dst, kt_dst, v_dst, tabg, extra=None, rotary=True):
        P = pg.phase(name)
        P.sb("w_st", [128, 2, 1024], F32)
        P.sb("w_sl", [128, 2, KC, 1024], BF16)
        P.sb("stage", [128, 2, S], BF16)
        P.sb("raw", [128, 2, 512], BF16)
        P.sb("cs", [128, 2, 2, 512], F32)
        P.sb("t1", [128, 2, 512], F32)
        P.sb("t2", [128, 2, 512], F32)
        P.sb("pswap", [128, 128], BF16)
        P.sb("vst", [128, 2, D], BF16)
        for i in range(3):
            P.ps("pq%d" % i, [128, 512], F32)
        for i in range(2):
            P.ps("psw%d" % i, [128, 512], F32)
        P.dma("sp", lambda e, T: e.dma_start(out=T["pswap"][:, :], in_=c_pswap[:, :]), w=["pswap"])
        cnt = {"pq": 0, "rot": 0, "stage": 0}
        slab_i = [0]

        def next_slab(t, rotperm):
            sl = slab_i[0] % 2
            slab_i[0] += 1
            load_w_slab(P, "w_sl", "w_st", lambda k, t=t: wcols(t, k), sl, rotperm)
            return sl

        pending = []

        def flush_pending():
            while pending:
                pending.pop(0)()

        sl_next = next_slab(0, True)
        for t in range(2):
            sl = sl_next
            sl_next = next_slab(t + 1, t + 1 < 2)
            dst = qt_dst if t == 0 else kt_dst
            for c in range(8):
                sg = cnt["stage"] % 2
                cnt["stage"] += 1
                for tt in range(8):
                    pi = cnt["pq"] % 3
                    cnt["pq"] += 1
                    pq = "pq%d" % pi
                    xk = [("xnT", 4 * tt + i) for i in range(4)]
                    for k in range(KC):
                        P.op("pe", lambda e, T, k=k, sl=sl, c=c, tt=tt, pq=pq: e.matmul(
                            T[pq][:, :], lhsT=T["w_sl"][:, sl, k, c * 128:(c + 1) * 128],
                            rhs=T["xnT"][:, k, tt * 512:(tt + 1) * 512], start=(k == 0), stop=(k == KC - 1)),
                            r=xk + [("w_sl", sl, k, "a" if c < 2 else "b")], w=[pq])
                    if c >= 2 or not rotary:
                        P.op("act", lambda e, T, sg=sg, tt=tt, pq=pq: e.activation(
                            out=T["stage"][:, sg, tt * 512:(tt + 1) * 512], in_=T[pq][:, :], func=AF.Copy),
                            r=[pq], w=[("stage", sg, tt)])
                    else:
                        ri = cnt["rot"] % 2
                        cnt["rot"] += 1
                        P.op("act", lambda e, T, ri=ri, pq=pq: e.activation(
                            out=T["raw"][:, ri, :], in_=T[pq][:, :], func=AF.Copy),
                            r=[pq], w=[("raw", ri)])
                        P.dma("sp", lambda e, T, ri=ri, tt=tt: e.dma_start(
                            out=T["cs"][:, ri, :, :], in_=c_rottab[tabg, :, :, tt * 512:(tt + 1) * 512]),
                            w=[("cs", ri)])

                        def rot(ri=ri, sg=sg, tt=tt):
                            psw = "psw%d" % ri
                            P.op("pe", lambda e, T: e.matmul(T[psw][:, :], lhsT=T["pswap"][:, :],
                                                              rhs=T["raw"][:, ri, :], start=True, stop=True),
                                 r=[("raw", ri), "pswap"], w=[psw])
                            P.op("dve", lambda e, T: e.tensor_tensor(out=T["t1"][:, ri, :], in0=T["raw"][:, ri, :],
                                                                      in1=T["cs"][:, ri, 0, :], op=ALU.mult),
                                 r=[("raw", ri), ("cs", ri)], w=[("t1", ri)])
                            P.op("dve", lambda e, T: e.tensor_tensor(out=T["t2"][:, ri, :], in0=T[psw][:, :],
                                                                      in1=T["cs"][:, ri, 1, :], op=ALU.mult),
                                 r=[psw, ("cs", ri)], w=[("t2", ri)])
                            P.op("dve", lambda e, T: e.tensor_tensor(
                                out=T["stage"][:, sg, tt * 512:(tt + 1) * 512], in0=T["t1"][:, ri, :],
                                in1=T["t2"][:, ri, :], op=ALU.add),
                                r=[("t1", ri), ("t2", ri)], w=[("stage", sg, tt)])
                        flush_pending()
                        pending.append(rot)
                flush_pending()
                for (r0, n, h, d0) in seg_list(c):
                    P.dma("pool", lambda e, T, sg=sg, r0=r0, n=n, h=h, d0=d0, dst=dst: e.dma_start(
                        out=dst[h, d0:d0 + n, :], in_=T["stage"][r0:r0 + n, sg, :]),
                        r=[("stage", sg, tt) for tt in range(8)], w=[("qkdst", t, h, d0)])
        sl = sl_next
        for b in range(NT):
            vb = b % 2
            for s in range(2):
                pi = cnt["pq"] % 3
                cnt["pq"] += 1
                pq = "pq%d" % pi
                for k in range(KC):
                    P.op("pe", lambda e, T, k=k, sl=sl, b=b, s=s, pq=pq: e.matmul(
                        T[pq][:, :], lhsT=T["xnT"][:, k, b * 128:(b + 1) * 128],
                        rhs=T["w_sl"][:, sl, k, s * 512:(s + 1) * 512], start=(k == 0), stop=(k == KC - 1)),
                        r=[("xnT", b), ("w_sl", sl, k, "a"), ("w_sl", sl, k, "b")], w=[pq])
                P.op("act", lambda e, T, vb=vb, s=s, pq=pq: e.activation(
                    out=T["vst"][:, vb, s * 512:(s + 1) * 512], in_=T[pq][:, :], func=AF.Copy),
                    r=[pq], w=[("vst", vb, s)])
            P.dma("pool", lambda e, T, vb=vb, b=b: e.dma_start(out=v_dst[b * 128:(b + 1) * 128, :],
                                                                in_=T["vst"][:, vb, :]),
                  r=[("vst", vb, 0), ("vst", vb, 1)], w=[("vdst", b)])
        if extra is not None:
            extra(P)
        return P

    def att_alloc(P, krows):
        P.sb("mk", [128, 3, 512], BF16)
        P.sb("qT", [krows, 2, S], BF16)
        P.sb("kT", [krows, 2, S], BF16)
        P.sb("vv", [128, 2, NT, 128], BF16)
        P.sb("pt", [128, 4, 512], BF16)
        P.sb("acc", [HD + 1, 2, S], F32)
        P.sb("bc", [HD, S], F32)
        P.sb("ot", [HD, 2, S], BF16)
        for i in range(4):
            P.ps("pss%d" % i, [128, 512], F32)
        for i in range(2):
            P.ps("pso%d" % i, [128, 512], F32)
        P.dma("sp", lambda e, T: e.dma_start(out=T["mk"][:, :, :], in_=c_masks[:, :, :]), w=["mk"])
        for b in range(2):
            P.op("dve", lambda e, T, b=b: e.memset(T["vv"][:, b, :, :], 1.0), w=[("vv1", b), ("vv", b, 0), ("vv", b, 1)])

    def att_load(P, b, qsrc, ksrc, vsrc, h, krows):
        P.dma("sp", lambda e, T: e.dma_start(out=T["qT"][0:krows, b, :], in_=qsrc[h, :, :]), w=[("qT", b)])
        P.dma("sp", lambda e, T: e.dma_start(out=T["kT"][0:krows, b, :], in_=ksrc[h, :, :]), w=[("kT", b)])
        for hf in range(2):
            P.dma("sp", lambda e, T, hf=hf: e.dma_start(
                out=T["vv"][:, b, hf * 16:(hf + 1) * 16, 0:HD],
                in_=vsrc[hf * 2048:(hf + 1) * 2048, h * HD:(h + 1) * HD].rearrange("(n p) d -> p n d", p=128)),
                w=[("vv", b, hf)])

    def att_finish(P, h, a):
        P.op("dve", lambda e, T: e.reciprocal(out=T["acc"][HD:HD + 1, a, :], in_=T["acc"][HD:HD + 1, a, :]),
             r=[("acc", a)], w=[("acc", a)])
        P.dma("pool", lambda e, T: e.dma_start(out=rden_scr[h:h + 1, :], in_=T["acc"][HD:HD + 1, a, :]),
              r=[("acc", a)], w=[("rden", h)])
        P.dma("sp", lambda e, T: e.dma_start(out=T["bc"][:, :],
                                              in_=rden_scr[h:h + 1, :].partition_broadcast(HD)),
              r=[("rden", h)], w=["bc"])
        P.op("dve", lambda e, T: e.tensor_tensor(out=T["ot"][:, a, :], in0=T["acc"][0:HD, a, :],
                                                  in1=T["bc"][:, :], op=ALU.mult),
             r=[("acc", a), "bc"], w=[("ot", a)])
        P.dma("pool", lambda e, T: e.dma_start(out=att_scr[h * HD:(h + 1) * HD, :], in_=T["ot"][:, a, :]),
              r=[("ot", a)], w=[("att_scr", h)])

    NPB = 4
    LA = 3

    def pipeline(items, stage1, stage2):
        n = len(items)
        for k in range(n + LA):
            if k < n:
                stage1(k, items[k])
            if k >= LA:
                stage2(k - LA, items[k - LA])

    def l0_att_phase():
        P = pg.phase("l0att")
        att_alloc(P, HD)
        units = [(h, g) for h in range(NH) for g in range(3)]
        for u0 in range(2):
            att_load(P, u0, qt0[units[u0][1]], kt0[units[u0][1]], v0[units[u0][1]], units[u0][0], HD)
        items = [(ui, pr) for ui in range(len(units)) for pr in range(16)]

        def stage1(k, it):
            ui, pr = it
            h, g = units[ui]
            b = ui % 2
            nbc = NT // DILS[g]
            ld = [("qT", b), ("kT", b)]
            n0 = 2 * pr
            mi = 1 if (n0 % nbc) == 0 else 0
            pss = "pss%d" % (k % NPB)
            pi = k % NPB
            mms = []
            for e_ in range(2):
                n = n0 + e_
                mms.append((e_ * 256, n, n))
                if n % nbc > 0:
                    mms.append((e_ * 256 + 128, n - 1, n))
            P.op("pe", lambda e, T: e.matmul(T[pss][:, :], lhsT=T["ident"][:, :], rhs=T["mk"][:, mi, :],
                                              start=True, stop=False),
                 r=["ident", "mk"], w=[pss])
            for ii, (c0, kb, qb) in enumerate(mms):
                P.op("pe", lambda e, T, c0=c0, kb=kb, qb=qb, last=(ii == len(mms) - 1): e.matmul(
                    T[pss][:, c0:c0 + 128], lhsT=T["kT"][0:HD, b, kb * 128:(kb + 1) * 128],
                    rhs=T["qT"][0:HD, b, qb * 128:(qb + 1) * 128], start=False, stop=last),
                    r=ld, w=[pss])
            P.op("act", lambda e, T: e.activation(out=T["pt"][:, pi, :], in_=T[pss][:, :], func=AF.Exp, scale=0.125),
                 r=[pss], w=[("pt", pi)])

        def stage2(k, it):
            ui, pr = it
            h, g = units[ui]
            b = ui % 2
            a = h % 2
            dil = DILS[g]
            nbc = NT // dil
            L = S // dil
            n0 = 2 * pr
            pi = k % NPB
            pso = "pso%d" % ((pr // 2) % 2)
            for e_ in range(2):
                n = n0 + e_
                m = n % nbc
                cols = (n % 4) * 128
                P.op("pe", lambda e, T, cols=cols, n=n, e_=e_, m=m: e.matmul(
                    T[pso][0:HD + 1, cols:cols + 128], lhsT=T["vv"][:, b, n, 0:HD + 1],
                    rhs=T["pt"][:, pi, e_ * 256:e_ * 256 + 128], start=True, stop=(m == 0)),
                    r=[("vv", b, 0), ("vv", b, 1), ("vv1", b), ("pt", pi)], w=[pso])
                if m > 0:
                    P.op("pe", lambda e, T, cols=cols, n=n, e_=e_: e.matmul(
                        T[pso][0:HD + 1, cols:cols + 128], lhsT=T["vv"][:, b, n - 1, 0:HD + 1],
                        rhs=T["pt"][:, pi, e_ * 256 + 128:e_ * 256 + 256], start=False, stop=True),
                        r=[("vv", b, 0), ("vv", b, 1), ("vv1", b), ("pt", pi)], w=[pso])
            if pr % 2 == 1:
                B = pr // 2
                if g == 0:
                    P.op("dve", lambda e, T: e.tensor_copy(
                        out=T["acc"][:, a, B * 512:(B + 1) * 512], in_=T[pso][0:HD + 1, :]),
                        r=[pso], w=[("acc", a)])
                else:
                    nsub = 512 // min(512, L)
                    w_ = 512 // nsub
                    for sub in range(nsub):
                        npos = 512 * B + sub * w_
                        r_ = npos // L
                        p0 = npos % L
                        st_ = p0 * dil + r_
                        en_ = st_ + (w_ - 1) * dil + 1
                        P.op("dve", lambda e, T, st_=st_, en_=en_, sub=sub: e.tensor_tensor(
                            out=T["acc"][:, a, st_:en_:dil], in0=T["acc"][:, a, st_:en_:dil],
                            in1=T[pso][0:HD + 1, sub * w_:(sub + 1) * w_], op=ALU.add),
                            r=[pso, ("acc", a)], w=[("acc", a)])
            if pr == 15 and g == 2:
                att_finish(P, h, a)
            if pr == 15 and ui + 2 < len(units):
                h2, g2 = units[ui + 2]
                att_load(P, b, qt0[g2], kt0[g2], v0[g2], h2, HD)

        pipeline(items, stage1, stage2)
        return P

    def outproj_phase(name, wsrc, res_src, res_dst):
        P = pg.phase(name)
        P.sb("w_st", [128, 2, D], F32)
        P.sb("wo", [128, KC, D], BF16)
        P.sb("junk", [128, D], BF16)
        P.sb("xt", [128, 3, D], F32)
        for i in range(4):
            P.ps("po%d" % i, [128, 512], F32)
        for k in range(KC):
            P.dma("sp", lambda e, T, k=k: e.dma_start(out=T["xnT"][:, k, :], in_=att_scr[k * 128:(k + 1) * 128, :]),
                  w=[("attT", k)])
        for k in range(KC):
            sb_ = k % 2
            P.dma("sp", lambda e, T, k=k, sb_=sb_: e.dma_start(out=T["w_st"][:, sb_, :],
                                                                in_=wsrc[k * 128:(k + 1) * 128, :]),
                  w=[("w_st", sb_)])
            P.op("dve", lambda e, T, k=k, sb_=sb_: e.tensor_copy(out=T["wo"][:, k, :], in_=T["w_st"][:, sb_, :]),
                 r=[("w_st", sb_)], w=[("wo", k)])
        for j in range(NT):
            xb = j % 3
            P.dma("sp", lambda e, T, j=j, xb=xb: e.dma_start(out=T["xt"][:, xb, :],
                                                              in_=res_src[j * 128:(j + 1) * 128, :]),
                  r=[("hs", j)], w=[("xt", xb)])
            for s_ in range(2):
                po = "po%d" % ((2 * j + s_) % 4)
                for k in range(KC):
                    P.op("pe", lambda e, T, j=j, s_=s_, k=k, po=po: e.matmul(
                        T[po][:, :], lhsT=T["xnT"][:, k, j * 128:(j + 1) * 128],
                        rhs=T["wo"][:, k, s_ * 512:(s_ + 1) * 512], start=(k == 0), stop=(k == KC - 1)),
                        r=[("attT", k), ("wo", k)], w=[po])
                P.op("dve", lambda e, T, xb=xb, s_=s_, po=po: e.tensor_tensor(
                    out=T["xt"][:, xb, s_ * 512:(s_ + 1) * 512], in0=T["xt"][:, xb, s_ * 512:(s_ + 1) * 512],
                    in1=T[po][:, :], op=ALU.add),
                    r=[po, ("xt", xb)], w=[("xt", xb)])
            P.op("act", lambda e, T, j=j, xb=xb: e.activation(out=T["junk"][:, :], in_=T["xt"][:, xb, :],
                                                               func=AF.Square, accum_out=T["g_ss"][:, j:j + 1]),
                 r=[("xt", xb)], w=[("g_ss", j)])
            P.dma("pool", lambda e, T, j=j, xb=xb: e.dma_start(out=res_dst[j * 128:(j + 1) * 128, :],
                                                                in_=T["xt"][:, xb, :]),
                  r=[("xt", xb)], w=[("hs", j)])
        return P

    def ffn_phase(name, li):
        P = pg.phase(name)
        P.sb("wd", [128, NJ, D], BF16)
        P.sb("actT", [128, NJ, 1024], BF16)
        P.sb("wgs", [128, 2, 2048], F32)
        P.sb("wg", [128, 2, 2048], BF16)
        P.sb("sg", [128, 2, 512], F32)
        P.sb("xt", [128, 2, D], F32)
        P.sb("junk", [128, D], BF16)
        for i in range(8):
            P.ps("pb%d" % i, [128, 512], F32)
        wgu = ffn_w_gu[li]
        wdn = ffn_w_down[li]
        c = {"st": 0, "wg": 0, "sg": 0, "xt": 0}

        def stg():
            b = c["st"] % 2
            c["st"] += 1
            return b

        for j in range(NJ):
            b = stg()
            P.dma("sp", lambda e, T, j=j, b=b: e.dma_start(out=T["wgs"][:, b, 0:D], in_=wdn[j * 128:(j + 1) * 128, :]),
                  w=[("wgs", b, 0), ("wgs", b, 1)])
            P.op("dve", lambda e, T, j=j, b=b: e.tensor_copy(out=T["wd"][:, j, :], in_=T["wgs"][:, b, 0:D]),
                 r=[("wgs", b, 0), ("wgs", b, 1)], w=[("wd", j)])

        def load_gu(j):
            b = stg()
            b2 = c["wg"] % 2
            c["wg"] += 1
            P.dma("sp", lambda e, T: e.dma_start(
                out=T["wgs"][:, b, :].rearrange("p (k c) -> p k c", k=KC)[:, :, 0:128],
                in_=wgu[:, j * 128:(j + 1) * 128].rearrange("(k p) c -> p k c", p=128)), w=[("wgs", b, 0)])
            P.dma("sp", lambda e, T: e.dma_start(
                out=T["wgs"][:, b, :].rearrange("p (k c) -> p k c", k=KC)[:, :, 128:256],
                in_=wgu[:, DFF + j * 128:DFF + (j + 1) * 128].rearrange("(k p) c -> p k c", p=128)),
                w=[("wgs", b, 1)])
            P.op("dve", lambda e, T: e.tensor_copy(out=T["wg"][:, b2, :], in_=T["wgs"][:, b, :]),
                 r=[("wgs", b, 0), ("wgs", b, 1)], w=[("wg", b2)])
            return b2

        for st_ in range(4):
            nxt = load_gu(0)
            for j in range(NJ):
                b2 = nxt
                if j + 1 < NJ:
                    nxt = load_gu(j + 1)
                set_ = j % 2
                for half in range(2):
                    tok0 = st_ * 1024 + half * 512
                    xk = [("xnT", tok0 // 128 + i) for i in range(4)]
                    pgb = "pb%d" % (4 * set_ + half)
                    pub = "pb%d" % (4 * set_ + 2 + half)
                    for (bank, off) in ((pgb, 0), (pub, 128)):
                        for k in range(KC):
                            P.op("pe", lambda e, T, bank=bank, off=off, k=k, b2=b2, tok0=tok0: e.matmul(
                                T[bank][:, :],
                                lhsT=T["wg"][:, b2, :].rearrange("p (k c) -> p k c", k=KC)[:, k, off:off + 128],
                                rhs=T["xnT"][:, k, tok0:tok0 + 512], start=(k == 0), stop=(k == KC - 1)),
                                r=xk + [("wg", b2)], w=[bank])
                    sgi = c["sg"] % 2
                    c["sg"] += 1
                    P.op("act", lambda e, T, pgb=pgb, sgi=sgi: e.activation(out=T["sg"][:, sgi, :], in_=T[pgb][:, :],
                                                                             func=AF.Silu),
                         r=[pgb], w=[("sg", sgi)])
                    P.op("dve", lambda e, T, pub=pub, sgi=sgi, j=j, half=half: e.tensor_tensor(
                        out=T["actT"][:, j, half * 512:(half + 1) * 512], in0=T["sg"][:, sgi, :], in1=T[pub][:, :],
                        op=ALU.mult),
                        r=[pub, ("sg", sgi)], w=[("actT", j, half)])
            for tb in range(8):
                jt = st_ * 8 + tb
                xb = c["xt"] % 2
                c["xt"] += 1
                P.dma("sp", lambda e, T, jt=jt, xb=xb: e.dma_start(out=T["xt"][:, xb, :],
                                                                    in_=h_scr[jt * 128:(jt + 1) * 128, :]),
                      r=[("hs", jt)], w=[("xt", xb)])
                for s_ in range(2):
                    bank = "pb%d" % ((2 * tb + s_) % 8)
                    for j in range(NJ):
                        P.op("pe", lambda e, T, bank=bank, j=j, tb=tb, s_=s_: e.matmul(
                            T[bank][:, :], lhsT=T["actT"][:, j, tb * 128:(tb + 1) * 128],
                            rhs=T["wd"][:, j, s_ * 512:(s_ + 1) * 512], start=(j == 0), stop=(j == NJ - 1)),
                            r=[("actT", j, tb // 4), ("wd", j)], w=[bank])
                    P.op("dve", lambda e, T, bank=bank, xb=xb, s_=s_: e.tensor_tensor(
                        out=T["xt"][:, xb, s_ * 512:(s_ + 1) * 512], in0=T["xt"][:, xb, s_ * 512:(s_ + 1) * 512],
                        in1=T[bank][:, :], op=ALU.add),
                        r=[bank, ("xt", xb)], w=[("xt", xb)])
                P.op("act", lambda e, T, jt=jt, xb=xb: e.activation(out=T["junk"][:, :], in_=T["xt"][:, xb, :],
                                                                     func=AF.Square, accum_out=T["g_ss"][:, jt:jt + 1]),
                     r=[("xt", xb)], w=[("g_ss", jt)])
                P.dma("pool", lambda e, T, jt=jt, xb=xb: e.dma_start(out=h_scr[jt * 128:(jt + 1) * 128, :],
                                                                      in_=T["xt"][:, xb, :]),
                      r=[("xt", xb)], w=[("hs", jt)])
        return P

    def gate_extra(P):
        P.sb("wfs", [128, KC, 128], F32)
        P.sb("wfb", [128, KC, NH], BF16)
        P.sb("negb", [NH, 1], F32)
        P.sb("lf", [NH, S], F32)
        P.sb("csum", [NH, S], F32)
        P.sb("qrow", [NH, S], BF16)
        P.sb("identf", [128, 128], F32)
        P.sb("cst", [128, NT * NH], F32)
        P.ps("psf", [128, 512], F32)
        P.ps("pct", [128, 512], F32)
        P.dma("sp", lambda e, T: e.dma_start(out=T["identf"][:, :], in_=c_identf[:, :]), w=["identf"])
        P.dma("sp", lambda e, T: e.dma_start(
            out=T["wfs"][:, :, 0:NH], in_=b_w_in[:, 3 * D:3 * D + NH].rearrange("(k p) c -> p k c", p=128)), w=["wfs"])
        P.op("dve", lambda e, T: e.tensor_copy(out=T["wfb"][:, :, :], in_=T["wfs"][:, :, 0:NH]), r=["wfs"], w=["wfb"])
        P.dma("sp", lambda e, T: e.dma_start(out=T["negb"][:, :], in_=b_f[:, :]), w=["negb0"])
        P.op("dve", lambda e, T: e.tensor_scalar(out=T["negb"][:, :], in0=T["negb"][:, :], scalar1=-1.0, scalar2=None,
                                                  op0=ALU.mult), r=["negb0"], w=["negb"])
        for tt in range(8):
            xk = [("xnT", 4 * tt + i) for i in range(4)]
            for k in range(KC):
                P.op("pe", lambda e, T, k=k, tt=tt: e.matmul(
                    T["psf"][0:NH, :], lhsT=T["wfb"][:, k, :], rhs=T["xnT"][:, k, tt * 512:(tt + 1) * 512],
                    start=(k == 0), stop=(k == KC - 1)), r=xk + ["wfb"], w=["psf"])
            P.op("act", lambda e, T, tt=tt: e.activation(out=T["lf"][:, tt * 512:(tt + 1) * 512], in_=T["psf"][0:NH, :],
                                                          func=AF.Exp, bias=T["negb"][:, 0:1], scale=-1.0),
                 r=["psf", "negb"], w=[("lf", tt)])
        P.op("act", lambda e, T: e.activation(out=T["lf"][:, :], in_=T["lf"][:, :], func=AF.Ln, bias=1.0),
             r=[("lf", tt) for tt in range(8)], w=["lf"])
        P.op("dve", lambda e, T: e.tensor_tensor_scan(out=T["csum"][:, :], data0=T["lf"][:, :], data1=T["lf"][:, :],
                                                       initial=0.0, op0=ALU.add, op1=ALU.max),
             r=["lf"], w=["csum"])
        P.op("dve", lambda e, T: e.tensor_scalar(out=T["qrow"][:, :], in0=T["csum"][:, :], scalar1=-8.0, scalar2=None,
                                                  op0=ALU.mult), r=["csum"], w=["qrow"])
        P.dma("pool", lambda e, T: e.dma_start(out=qt1[:, HD, :], in_=T["qrow"][:, :]), r=["qrow"], w=["qt1row"])
        P.op("dve", lambda e, T: e.memset(T["qrow"][:, :], 1.0), r=["qrow"], w=["qrow"])
        P.dma("pool", lambda e, T: e.dma_start(out=kt1[:, HD, :], in_=T["qrow"][:, :]), r=["qrow"], w=["kt1row"])
        for blk in range(NT):
            P.op("pe", lambda e, T, blk=blk: e.transpose(
                out=T["pct"][:, blk * NH:(blk + 1) * NH], in_=T["csum"][0:NH, blk * 128:(blk + 1) * 128],
                identity=T["identf"][0:NH, 0:NH]), r=["csum", "identf"], w=["pct"])
        P.op("act", lambda e, T: e.activation(out=T["cst"][:, :], in_=T["pct"][:, :], func=AF.Copy),
             r=["pct"], w=["cst"])
        P.dma("pool", lambda e, T: e.dma_start(out=cs_scr[:, :], in_=T["cst"][:, :]), r=["cst"], w=["cs_scr"])

    def l1_att_phase():
        P = pg.phase("l1att")
        KR = HD + 1
        att_alloc(P, KR)
        P.sb("cst", [128, NT * NH], F32)
        P.dma("sp", lambda e, T: e.dma_start(out=T["cst"][:, :], in_=cs_scr[:, :]), w=["cst"])
        att_load(P, 0, qt1, kt1, v1, 0, KR)
        att_load(P, 1, qt1, kt1, v1, 1, KR)
        items = [(h, Qi, J) for h in range(NH) for Qi in range(8) for J in range(4 * Qi + 4)]

        def stage1(k, it):
            h, Qi, J = it
            b = h % 2
            ld = [("qT", b), ("kT", b)]
            d_ = J - 4 * Qi
            c0 = 128 * d_ if d_ > 0 else 0
            W = 512 - c0
            pss = "pss%d" % (k % NPB)
            pi = k % NPB
            if d_ >= 0:
                P.op("pe", lambda e, T: e.matmul(T[pss][:, c0:512], lhsT=T["ident"][:, :], rhs=T["mk"][:, 2, 0:W],
                                                  start=True, stop=False),
                     r=["ident", "mk"], w=[pss])
            P.op("pe", lambda e, T: e.matmul(
                T[pss][:, c0:512], lhsT=T["kT"][0:KR, b, J * 128:(J + 1) * 128],
                rhs=T["qT"][0:KR, b, Qi * 512 + c0:(Qi + 1) * 512], start=(d_ < 0), stop=True),
                r=ld, w=[pss])
            P.op("act", lambda e, T: e.activation(
                out=T["pt"][:, pi, c0:512], in_=T[pss][:, c0:512], func=AF.Exp,
                bias=T["cst"][:, J * NH + h:J * NH + h + 1], scale=0.125),
                r=[pss, "cst"], w=[("pt", pi)])

        def stage2(k, it):
            h, Qi, J = it
            b = h % 2
            a = h % 2
            nJ = 4 * Qi + 4
            d_ = J - 4 * Qi
            c0 = 128 * d_ if d_ > 0 else 0
            pi = k % NPB
            pso = "pso%d" % (Qi % 2)
            P.op("pe", lambda e, T: e.matmul(
                T[pso][0:KR, c0:512], lhsT=T["vv"][:, b, J, 0:HD + 1], rhs=T["pt"][:, pi, c0:512],
                start=(J == 0), stop=(J == nJ - 1)),
                r=[("vv", b, 0), ("vv", b, 1), ("vv1", b), ("pt", pi)], w=[pso])
            if J == nJ - 1:
                P.op("dve", lambda e, T: e.tensor_copy(
                    out=T["acc"][:, a, Qi * 512:(Qi + 1) * 512], in_=T[pso][0:KR, :]),
                    r=[pso], w=[("acc", a)])
                if Qi == 7:
                    att_finish(P, h, a)
                    if h + 2 < NH:
                        att_load(P, b, qt1, kt1, v1, h + 2, KR)

        pipeline(items, stage1, stage2)
        return P

    P0 = pg.phase("consts")
    load_consts(P0)

    def build_all():
        for g in range(3):
            norm_phase("l0n%d" % g, x, a_norm[0:1, :], g)
            inproj_phase("l0p%d" % g, g,
                         lambda t, k, g=g: a_w_in[k * 128:(k + 1) * 128,
                                                  g * 3072 + t * 1024: g * 3072 + (t + 1) * 1024],
                         qt0[g], kt0[g], v0[g], g)
            if stop_after == "l0p%d" % g:
                return
        l0_att_phase()
        if stop_after == "l0att":
            return
        outproj_phase("l0out", a_w_out, x, h_scr)
        if stop_after == "l0out":
            return
        norm_phase("f0n", h_scr, ffn_norm[0:1, :], 0, have_ss=True)
        ffn_phase("f0", 0)
        if stop_after == "f0":
            return
        norm_phase("l1n", h_scr, b_norm[0:1, :], 0, have_ss=True)
        inproj_phase("l1p", 0, lambda t, k: b_w_in[k * 128:(k + 1) * 128, t * 1024:(t + 1) * 1024],
                     qt1, kt1, v1, 0, extra=gate_extra, rotary=False)
        if stop_after == "l1p":
            return
        l1_att_phase()
        if stop_after == "l1att":
            return
        outproj_phase("l1out", b_w_out, h_scr, h_scr)
        if stop_after == "l1out":
            return
        norm_phase("f1n", h_scr, ffn_norm[1:2, :], 0, have_ss=True)
        ffn_phase("f1", 1)
        if stop_after == "f1":
            return
        norm_phase("fin", h_scr, final_norm[0:1, :], 0, transpose=False, dst=out, have_ss=True)

    build_all()
    pg.emit()
    return nc, pg


_CACHE = {}


def core_inputs(inp, c, consts):
    f = lambda a: np.ascontiguousarray(np.asarray(a, dtype=np.float32))
    m = {
        "x": f(inp["x"][c]),
        "a_norm": f(inp["a_norm"]).reshape(1, D),
        "a_w_in": f(inp["a_w_in"][0]),
        "a_w_out": f(inp["a_w_out"][0]),
        "b_norm": f(inp["b_norm"]).reshape(1, D),
        "b_w_in": f(inp["b_w_in"][0]),
        "b_f": f(inp["b_f"]).reshape(NH, 1),
        "b_w_out": f(inp["b_w_out"][0]),
        "ffn_norm": f(inp["ffn_norm"]),
        "ffn_w_gu": f(inp["ffn_w_gu"]),
        "ffn_w_down": f(inp["ffn_w_down"]),
        "final_norm": f(inp["final_norm"]).reshape(1, D),
    }
    m.update(consts)
    return m


def kernel(**inputs):
    if "nc" not in _CACHE:
        _CACHE["nc"] = build()[0]
        _CACHE["consts"] = host_consts()
    nc = _CACHE["nc"]
    consts = _CACHE["consts"]
    nb = inputs["x"].shape[0]
    maps = [core_inputs(inputs, c, consts) for c in range(nb)]
    res = run_bass_kernel_spmd(nc, maps, core_ids=list(range(nb)))
    return np.stack([np.asarray(r["out"], dtype=np.float32) for r in res.results], axis=0)
```

```python
from contextlib import ExitStack
import numpy as np
import ml_dtypes
import concourse.bass as bass
import concourse.mybir as mybir
from concourse.bass_utils import run_bass_kernel_spmd

F32 = mybir.dt.float32
BF16 = mybir.dt.bfloat16
AF = mybir.ActivationFunctionType
ALU = mybir.AluOpType

S = 4096
D = 1024
NT = 32
KC = 8
NH = 16
HD = 64
DFF = 2816
NJ = 22
DILS = (1, 4, 16)
EPS = 1e-6
NEG = -30000.0

ENGS = ["pe", "act", "dve", "pool", "sp"]
BLK = {"pe": "tensor", "act": "scalar", "dve": "vector", "pool": "gpsimd", "sp": "sync"}


class Op:
    __slots__ = ("eng", "fn", "reads", "writes", "is_dma", "pos", "waits", "signal",
                 "slot", "target", "clock", "rank", "barrier")

    def __init__(self, eng, fn, reads, writes, is_dma):
        self.eng = eng
        self.fn = fn
        self.reads = reads
        self.writes = writes
        self.is_dma = is_dma
        self.waits = []
        self.signal = False
        self.slot = None
        self.target = None
        self.clock = None
        self.rank = None
        self.barrier = False


class Phase:
    def __init__(self, name):
        self.name = name
        self.allocs = []
        self.ops = []

    def sb(self, name, shape, dt):
        self.allocs.append((name, "sb", list(shape), dt))
        return name

    def ps(self, name, shape, dt):
        self.allocs.append((name, "ps", list(shape), dt))
        return name

    def op(self, eng, fn, r=(), w=()):
        o = Op(eng, fn, tuple(r), tuple(w), False)
        self.ops.append(o)
        return o

    def dma(self, eng, fn, r=(), w=()):
        o = Op(eng, fn, tuple(r), tuple(w), True)
        self.ops.append(o)
        return o


class Prog:
    def __init__(self, nc, n_dsem=32):
        self.nc = nc
        self.T = {}
        self.phases = []
        self.K = n_dsem
        self.gallocs = []

    def phase(self, name):
        p = Phase(name)
        self.phases.append(p)
        return p

    def gsb(self, name, shape, dt):
        self.gallocs.append((name, "sb", list(shape), dt))

    def analyze(self):
        K = self.K
        res = {}
        know = {e: ({}, {}) for e in ENGS}
        cnt = {e: 0 for e in ENGS}
        last_op = {e: None for e in ENGS}
        slot_last = [None] * K
        slot_uses = [0] * K
        dma_i = 0
        all_ops = {e: [] for e in ENGS}

        def merge(dst, src):
            for k, v in src[0].items():
                if dst[0].get(k, -1) < v:
                    dst[0][k] = v
            for k, v in src[1].items():
                if dst[1].get(k, -1) < v:
                    dst[1][k] = v

        def need(E, X, P):
            if P is None or P is X:
                return
            kn = know[E]
            if P.is_dma:
                if kn[1].get(P.slot, 0) >= P.target:
                    return
                X.waits.append(P)
                kn[1][P.slot] = P.target
                merge(kn, P.clock)
            else:
                if P.eng == E and E == "pe":
                    return
                if kn[0].get(P.eng, -1) >= P.pos:
                    return
                P.signal = True
                X.waits.append(P)
                kn[0][P.eng] = P.pos
                merge(kn, P.clock)

        for ph in self.phases:
            for e in ENGS:
                b = Op(e, None, (), (), False)
                b.barrier = True
                ph.ops.append(b)
            for X in ph.ops:
                E = X.eng
                X.pos = cnt[E]
                cnt[E] += 1
                all_ops[E].append(X)
                if X.barrier:
                    for e2 in ENGS:
                        need(E, X, last_op[e2])
                    for s in range(K):
                        need(E, X, slot_last[s])
                    X.clock = ({}, {})
                    continue
                deps = []
                for k in X.reads:
                    ent = res.get(k)
                    if ent is not None:
                        deps.append(ent[0])
                for k in X.writes:
                    ent = res.get(k)
                    if ent is not None:
                        deps.append(ent[0])
                        deps.extend(ent[1])
                if X.is_dma:
                    s = dma_i % K
                    dma_i += 1
                    need(E, X, slot_last[s])
                    slot_uses[s] += 1
                    X.slot = s
                    X.target = 16 * slot_uses[s]
                    slot_last[s] = X
                for P in deps:
                    need(E, X, P)
                X.clock = (dict(know[E][0]), dict(know[E][1]))
                for k in X.reads:
                    ent = res.get(k)
                    if ent is None:
                        res[k] = [None, [X]]
                    else:
                        ent[1].append(X)
                for k in X.writes:
                    res[k] = [X, []]
                if not X.is_dma:
                    last_op[E] = X
        for e in ENGS:
            r = 0
            for o in all_ops[e]:
                if o.signal:
                    r += 1
                    o.rank = r
        self.stats = {e: (len(all_ops[e]), sum(1 for o in all_ops[e] if o.signal),
                          sum(len(o.waits) for o in all_ops[e])) for e in ENGS}

    def emit(self):
        nc = self.nc
        self.analyze()
        T = self.T
        with ExitStack() as st:
            sem = {e: st.enter_context(nc.semaphore("s_" + e)) for e in ENGS}
            dsem = [st.enter_context(nc.semaphore("d%d" % i)) for i in range(self.K)]
            for (name, kind, shape, dt) in self.gallocs:
                T[name] = st.enter_context(nc.sbuf_tensor("t_" + name, shape, dt))
            for ph in self.phases:
                with ExitStack() as st2:
                    for (name, kind, shape, dt) in ph.allocs:
                        if kind == "sb":
                            T[name] = st2.enter_context(nc.sbuf_tensor("t_%s_%s" % (ph.name, name), shape, dt))
                        else:
                            T[name] = st2.enter_context(nc.psum_tensor("t_%s_%s" % (ph.name, name), shape, dt))
                    with nc.Block() as blk:
                        for e in ENGS:
                            ops = [o for o in ph.ops if o.eng == e]

                            def body(eng, ops=ops, e=e):
                                for o in ops:
                                    for P in o.waits:
                                        if P.is_dma:
                                            eng.wait_ge(dsem[P.slot], P.target)
                                        else:
                                            eng.wait_ge(sem[P.eng], P.rank)
                                    if o.barrier:
                                        continue
                                    ins = o.fn(eng, T)
                                    if o.is_dma:
                                        ins.then_inc(dsem[o.slot], 16)
                                    elif o.signal:
                                        ins.then_inc(sem[e], 1)

                            getattr(blk, BLK[e])(body)
                    for (name, kind, shape, dt) in ph.allocs:
                        T.pop(name, None)


def _bf(a):
    return np.ascontiguousarray(a.astype(ml_dtypes.bfloat16))


def perm_tokens(g):
    dil = DILS[g]
    L = S // dil
    n = np.arange(S)
    return (n % L) * dil + (n // L)


def host_consts():
    c = {}
    c["ident"] = _bf(np.eye(128, dtype=np.float32))
    c["identf"] = np.eye(128, dtype=np.float32)
    r = np.arange(128)
    partner = np.where(r % 16 < 8, r + 8, r - 8)
    pw = np.zeros((128, 128), np.float32)
    pw[partner, r] = 1.0
    c["pswap"] = _bf(pw)
    half = 8
    inv_freq = (np.float32(500000.0) ** (-np.arange(half, dtype=np.float32) * np.float32(2.0) / np.float32(16))).astype(np.float32)
    tabs = []
    for g in range(3):
        pos = perm_tokens(g).astype(np.float32)
        ang = (pos[:, None] * inv_freq[None, :]).astype(np.float32)
        cos = np.cos(ang).astype(np.float32).T
        sin = np.sin(ang).astype(np.float32).T
        fi = r % 8
        sgn = np.where(r % 16 < 8, -1.0, 1.0).astype(np.float32)
        tab = np.stack([cos[fi], sin[fi] * sgn[:, None]], axis=1)
        tabs.append(tab.astype(np.float32))
    c["rottab"] = np.ascontiguousarray(np.stack(tabs, 0))
    k = np.arange(128)[:, None]
    q = np.arange(128)[None, :]
    ncur = np.where(k > q, NEG, 0.0).astype(np.float32)
    nprev = np.where(k < q, NEG, 0.0).astype(np.float32)
    nall = np.full((128, 128), NEG, np.float32)
    z = np.zeros((128, 128), np.float32)
    m0 = np.concatenate([ncur, nprev, ncur, nprev], 1)
    m1 = np.concatenate([ncur, nall, ncur, nprev], 1)
    m2 = np.concatenate([ncur, z, z, z], 1)
    c["masks"] = _bf(np.stack([m0, m1, m2], 1))
    return c


def build(stop_after=None, dbg=()):
    nc = bass.Bass("TRN2", target_bir_lowering=False)
    dbg = set(dbg)

    def din(name, shape, dt):
        return nc.dram_tensor(name, list(shape), dt, kind="ExternalInput").ap()

    def dscr(name, shape, dt):
        kind = "ExternalOutput" if name in dbg else "Internal"
        return nc.dram_tensor(name, list(shape), dt, kind=kind).ap()

    x = din("x", [S, D], F32)
    a_norm = din("a_norm", [1, D], F32)
    a_w_in = din("a_w_in", [D, 9216], F32)
    a_w_out = din("a_w_out", [D, D], F32)
    b_norm = din("b_norm", [1, D], F32)
    b_w_in = din("b_w_in", [D, 3088], F32)
    b_f = din("b_f", [NH, 1], F32)
    b_w_out = din("b_w_out", [D, D], F32)
    ffn_norm = din("ffn_norm", [2, D], F32)
    ffn_w_gu = din("ffn_w_gu", [2, D, 2 * DFF], F32)
    ffn_w_down = din("ffn_w_down", [2, DFF, D], F32)
    final_norm = din("final_norm", [1, D], F32)
    c_ident = din("ident", [128, 128], BF16)
    c_identf = din("identf", [128, 128], F32)
    c_pswap = din("pswap", [128, 128], BF16)
    c_rottab = din("rottab", [3, 128, 2, S], F32)
    c_masks = din("masks", [128, 3, 512], BF16)

    out = nc.dram_tensor("out", [S, D], F32, kind="ExternalOutput").ap()

    qt0 = [dscr("qt0_%d" % g, [NH, HD, S], BF16) for g in range(3)]
    kt0 = [dscr("kt0_%d" % g, [NH, HD, S], BF16) for g in range(3)]
    v0 = [dscr("v0_%d" % g, [S, D], BF16) for g in range(3)]
    att_scr = dscr("att_scr", [D, S], BF16)
    h_scr = dscr("h_scr", [S, D], F32)
    qt1 = dscr("qt1", [NH, HD + 1, S], BF16)
    kt1 = dscr("kt1", [NH, HD + 1, S], BF16)
    v1 = dscr("v1", [S, D], BF16)
    cs_scr = dscr("cs_scr", [128, NT * NH], F32)
    rden_scr = dscr("rden_scr", [NH, S], F32)

    pg = Prog(nc)
    pg.gsb("xnT", [128, KC, S], BF16)
    pg.gsb("ident", [128, 128], BF16)
    pg.gsb("gam", [128, D], F32)
    pg.gsb("g_ss", [128, NT], F32)

    def tok_rows(src, g, j):
        dil = DILS[g]
        L = S // dil
        n0 = 128 * j
        r = n0 // L
        p0 = n0 % L
        start = p0 * dil + r
        if dil == 1:
            return src[start:start + 128, :]
        return src[start:start + 127 * dil + 1:dil, :]

    def load_consts(P):
        P.dma("sp", lambda e, T: e.dma_start(out=T["ident"][:, :], in_=c_ident[:, :]), w=["ident"])

    def norm_phase(name, src, gamma_row, g, transpose=True, dst=None, have_ss=False):
        P = pg.phase(name)
        P.sb("n_ht", [128, 3, D], F32)
        P.sb("n_junk", [128, D], BF16)
        P.sb("n_ss", [128, NT], F32)
        P.sb("n_rstd", [128, NT], F32)
        if transpose:
            P.sb("n_xn", [128, 2, D], BF16)
            P.ps("n_pt0", [128, D], BF16)
            P.ps("n_pt1", [128, D], BF16)
        else:
            P.sb("n_o", [128, 2, D], F32)
        P.dma("sp", lambda e, T: e.dma_start(out=T["gam"][:, :], in_=gamma_row.partition_broadcast(128)),
              w=["gam"])
        ssn = "g_ss" if have_ss else "n_ss"
        if not have_ss:
            for j in range(NT):
                b = j % 3
                P.dma("sp", lambda e, T, j=j, b=b: e.dma_start(out=T["n_ht"][:, b, :], in_=tok_rows(src, g, j)),
                      w=[("n_ht", b)])
                P.op("act", lambda e, T, j=j, b=b: e.activation(out=T["n_junk"][:, :], in_=T["n_ht"][:, b, :],
                                                                 func=AF.Square, accum_out=T["n_ss"][:, j:j + 1]),
                     r=[("n_ht", b)], w=[("n_ss", j)])
        allss = [(ssn, j) for j in range(NT)]
        P.op("dve", lambda e, T: e.tensor_scalar(out=T["n_rstd"][:, :], in0=T[ssn][:, :], scalar1=1.0 / D,
                                                  scalar2=EPS, op0=ALU.mult, op1=ALU.add),
             r=allss, w=["n_var"])
        P.op("act", lambda e, T: e.activation(out=T["n_rstd"][:, :], in_=T["n_rstd"][:, :], func=AF.Sqrt),
             r=["n_var"], w=["n_std"])
        P.op("dve", lambda e, T: e.reciprocal(out=T["n_rstd"][:, :], in_=T["n_rstd"][:, :]),
             r=["n_std"], w=["n_rstd"])
        for j in range(NT):
            b = j % 3
            P.dma("sp", lambda e, T, j=j, b=b: e.dma_start(out=T["n_ht"][:, b, :], in_=tok_rows(src, g, j)),
                  w=[("n_ht", b)])
            if transpose:
                xb = j % 2
                P.op("dve", lambda e, T, j=j, b=b, xb=xb: e.scalar_tensor_tensor(
                    out=T["n_xn"][:, xb, :], in0=T["n_ht"][:, b, :], scalar=T["n_rstd"][:, j:j + 1],
                    in1=T["gam"][:, :], op0=ALU.mult, op1=ALU.mult),
                    r=[("n_ht", b), "n_rstd", "gam"], w=[("n_xn", xb)])
                pt = "n_pt%d" % xb
                for k in range(KC):
                    P.op("pe", lambda e, T, k=k, xb=xb, pt=pt: e.transpose(
                        out=T[pt][:, k * 128:(k + 1) * 128], in_=T["n_xn"][:, xb, k * 128:(k + 1) * 128],
                        identity=T["ident"][:, :]),
                        r=[("n_xn", xb), "ident"], w=[pt])
                P.op("act", lambda e, T, j=j, pt=pt: e.activation(
                    out=T["xnT"][:, :, j * 128:(j + 1) * 128],
                    in_=T[pt][:, :].rearrange("p (k t) -> p k t", k=KC), func=AF.Copy),
                    r=[pt], w=[("xnT", j)])
            else:
                ob = j % 2
                P.op("dve", lambda e, T, j=j, b=b, ob=ob: e.scalar_tensor_tensor(
                    out=T["n_o"][:, ob, :], in0=T["n_ht"][:, b, :], scalar=T["n_rstd"][:, j:j + 1],
                    in1=T["gam"][:, :], op0=ALU.mult, op1=ALU.mult),
                    r=[("n_ht", b), "n_rstd", "gam"], w=[("n_o", ob)])
                P.dma("pool", lambda e, T, j=j, ob=ob: e.dma_start(out=dst[j * 128:(j + 1) * 128, :],
                                                                    in_=T["n_o"][:, ob, :]),
                      r=[("n_o", ob)], w=[("dst", j)])
        return P

    def load_w_slab(P, wname, stname, wsrc_cols, slab, rotperm):
        for k in range(KC):
            sb_ = k % 2
            P.dma("sp", lambda e, T, k=k, sb_=sb_: e.dma_start(out=T[stname][:, sb_, :], in_=wsrc_cols(k)),
                  w=[(stname, sb_)])
            if rotperm:
                P.op("dve", lambda e, T, k=k, sb_=sb_: e.tensor_copy(
                    out=T[wname][:, slab, k, 0:256].rearrange("p (h d) -> p h d", d=16),
                    in_=T[stname][:, sb_, :].rearrange("p (h d) -> p h d", d=64)[:, :, 0:16]),
                    r=[(stname, sb_)], w=[(wname, slab, k, "a")])
                P.op("dve", lambda e, T, k=k, sb_=sb_: e.tensor_copy(
                    out=T[wname][:, slab, k, 256:1024].rearrange("p (h d) -> p h d", d=48),
                    in_=T[stname][:, sb_, :].rearrange("p (h d) -> p h d", d=64)[:, :, 16:64]),
                    r=[(stname, sb_)], w=[(wname, slab, k, "b")])
            else:
                P.op("dve", lambda e, T, k=k, sb_=sb_: e.tensor_copy(out=T[wname][:, slab, k, :],
                                                                      in_=T[stname][:, sb_, :]),
                     r=[(stname, sb_)], w=[(wname, slab, k, "a"), (wname, slab, k, "b")])

    def wkeys(wname, slab):
        ks = []
        for k in range(KC):
            ks.append((wname, slab, k, "a"))
            ks.append((wname, slab, k, "b"))
        return ks

    def seg_list(c):
        segs = []
        if c < 2:
            for hh in range(8):
                segs.append((hh * 16, 16, 8 * c + hh, 0))
        else:
            f0 = 128 * (c - 2)
            f = f0
            while f < f0 + 128:
                h = f // 48
                dd = f % 48
                n = min(48 - dd, f0 + 128 - f)
                segs.append((f - f0, n, h, 16 + dd))
                f += n
        return segs

    def inproj_phase(name, g, wcols, qt_dst, kt_dst, v_dst, tabg, extra=None, rotary=True):
        P = pg.phase(name)
        P.sb("w_st", [128, 2, 1024], F32)
        P.sb("w_sl", [128, 2, KC, 1024], BF16)
        P.sb("stage", [128, 2, S], BF16)
        P.sb("raw", [128, 2, 512], BF16)
        P.sb("cs", [128, 2, 2, 512], F32)
        P.sb("t1", [128, 2, 512], F32)
        P.sb("t2", [128, 2, 512], F32)
        P.sb("pswap", [128, 128], BF16)
        P.sb("vst", [128, 2, D], BF16)
        for i in range(3):
            P.ps("pq%d" % i, [128, 512], F32)
        for i in range(2):
            P.ps("psw%d" % i, [128, 512], F32)
        P.dma("sp", lambda e, T: e.dma_start(out=T["pswap"][:, :], in_=c_pswap[:, :]), w=["pswap"])
        cnt = {"pq": 0, "rot": 0, "stage": 0}
        slab_i = [0]

        def next_slab(t, rotperm):
            sl = slab_i[0] % 2
            slab_i[0] += 1
            load_w_slab(P, "w_sl", "w_st", lambda k, t=t: wcols(t, k), sl, rotperm)
            return sl

        pending = []

        def flush_pending():
            while pending:
                pending.pop(0)()

        sl_next = next_slab(0, True)
        for t in range(2):
            sl = sl_next
            sl_next = next_slab(t + 1, t + 1 < 2)
            dst = qt_dst if t == 0 else kt_dst
            for c in range(8):
                sg = cnt["stage"] % 2
                cnt["stage"] += 1
                for tt in range(8):
                    pi = cnt["pq"] % 3
                    cnt["pq"] += 1
                    pq = "pq%d" % pi
                    xk = [("xnT", 4 * tt + i) for i in range(4)]
                    for k in range(KC):
                        P.op("pe", lambda e, T, k=k, sl=sl, c=c, tt=tt, pq=pq: e.matmul(
                            T[pq][:, :], lhsT=T["w_sl"][:, sl, k, c * 128:(c + 1) * 128],
                            rhs=T["xnT"][:, k, tt * 512:(tt + 1) * 512], start=(k == 0), stop=(k == KC - 1)),
                            r=xk + [("w_sl", sl, k, "a" if c < 2 else "b")], w=[pq])
                    if c >= 2 or not rotary:
                        P.op("act", lambda e, T, sg=sg, tt=tt, pq=pq: e.activation(
                            out=T["stage"][:, sg, tt * 512:(tt + 1) * 512], in_=T[pq][:, :], func=AF.Copy),
                            r=[pq], w=[("stage", sg, tt)])
                    else:
                        ri = cnt["rot"] % 2
                        cnt["rot"] += 1
                        P.op("act", lambda e, T, ri=ri, pq=pq: e.activation(
                            out=T["raw"][:, ri, :], in_=T[pq][:, :], func=AF.Copy),
                            r=[pq], w=[("raw", ri)])
                        P.dma("sp", lambda e, T, ri=ri, tt=tt: e.dma_start(
                            out=T["cs"][:, ri, :, :], in_=c_rottab[tabg, :, :, tt * 512:(tt + 1) * 512]),
                            w=[("cs", ri)])

                        def rot(ri=ri, sg=sg, tt=tt):
                            psw = "psw%d" % ri
                            P.op("pe", lambda e, T: e.matmul(T[psw][:, :], lhsT=T["pswap"][:, :],
                                                              rhs=T["raw"][:, ri, :], start=True, stop=True),
                                 r=[("raw", ri), "pswap"], w=[psw])
                            P.op("dve", lambda e, T: e.tensor_tensor(out=T["t1"][:, ri, :], in0=T["raw"][:, ri, :],
                                                                      in1=T["cs"][:, ri, 0, :], op=ALU.mult),
                                 r=[("raw", ri), ("cs", ri)], w=[("t1", ri)])
                            P.op("dve", lambda e, T: e.tensor_tensor(out=T["t2"][:, ri, :], in0=T[psw][:, :],
                                                                      in1=T["cs"][:, ri, 1, :], op=ALU.mult),
                                 r=[psw, ("cs", ri)], w=[("t2", ri)])
                            P.op("dve", lambda e, T: e.tensor_tensor(
                                out=T["stage"][:, sg, tt * 512:(tt + 1) * 512], in0=T["t1"][:, ri, :],
                                in1=T["t2"][:, ri, :], op=ALU.add),
                                r=[("t1", ri), ("t2", ri)], w=[("stage", sg, tt)])
                        flush_pending()
                        pending.append(rot)
                flush_pending()
                for (r0, n, h, d0) in seg_list(c):
                    P.dma("pool", lambda e, T, sg=sg, r0=r0, n=n, h=h, d0=d0, dst=dst: e.dma_start(
                        out=dst[h, d0:d0 + n, :], in_=T["stage"][r0:r0 + n, sg, :]),
                        r=[("stage", sg, tt) for tt in range(8)], w=[("qkdst", t, h, d0)])
        sl = sl_next
        for b in range(NT):
            vb = b % 2
            for s in range(2):
                pi = cnt["pq"] % 3
                cnt["pq"] += 1
                pq = "pq%d" % pi
                for k in range(KC):
                    P.op("pe", lambda e, T, k=k, sl=sl, b=b, s=s, pq=pq: e.matmul(
                        T[pq][:, :], lhsT=T["xnT"][:, k, b * 128:(b + 1) * 128],
                        rhs=T["w_sl"][:, sl, k, s * 512:(s + 1) * 512], start=(k == 0), stop=(k == KC - 1)),
                        r=[("xnT", b), ("w_sl", sl, k, "a"), ("w_sl", sl, k, "b")], w=[pq])
                P.op("act", lambda e, T, vb=vb, s=s, pq=pq: e.activation(
                    out=T["vst"][:, vb, s * 512:(s + 1) * 512], in_=T[pq][:, :], func=AF.Copy),
                    r=[pq], w=[("vst", vb, s)])
            P.dma("pool", lambda e, T, vb=vb, b=b: e.dma_start(out=v_dst[b * 128:(b + 1) * 128, :],
                                                                in_=T["vst"][:, vb, :]),
                  r=[("vst", vb, 0), ("vst", vb, 1)], w=[("vdst", b)])
        if extra is not None:
            extra(P)
        return P

    def att_alloc(P, krows):
        P.sb("mk", [128, 3, 512], BF16)
        P.sb("qT", [krows, 2, S], BF16)
        P.sb("kT", [krows, 2, S], BF16)
        P.sb("vv", [128, 2, NT, 128], BF16)
        P.sb("pt", [128, 4, 512], BF16)
        P.sb("acc", [HD + 1, 2, S], F32)
        P.sb("bc", [HD, S], F32)
        P.sb("ot", [HD, 2, S], BF16)
        for i in range(4):
            P.ps("pss%d" % i, [128, 512], F32)
        for i in range(3):
            P.ps("pso%d" % i, [128, 512], F32)
        P.dma("sp", lambda e, T: e.dma_start(out=T["mk"][:, :, :], in_=c_masks[:, :, :]), w=["mk"])
        for b in range(2):
            P.op("dve", lambda e, T, b=b: e.memset(T["vv"][:, b, :, :], 1.0), w=[("vv1", b), ("vv", b, 0), ("vv", b, 1)])

    def att_load(P, b, qsrc, ksrc, vsrc, h, krows):
        P.dma("sp", lambda e, T: e.dma_start(out=T["qT"][0:krows, b, :], in_=qsrc[h, :, :]), w=[("qT", b)])
        P.dma("sp", lambda e, T: e.dma_start(out=T["kT"][0:krows, b, :], in_=ksrc[h, :, :]), w=[("kT", b)])
        for hf in range(2):
            P.dma("sp", lambda e, T, hf=hf: e.dma_start(
                out=T["vv"][:, b, hf * 16:(hf + 1) * 16, 0:HD],
                in_=vsrc[hf * 2048:(hf + 1) * 2048, h * HD:(h + 1) * HD].rearrange("(n p) d -> p n d", p=128)),
                w=[("vv", b, hf)])

    def att_finish(P, h, a):
        P.op("dve", lambda e, T: e.reciprocal(out=T["acc"][HD:HD + 1, a, :], in_=T["acc"][HD:HD + 1, a, :]),
             r=[("acc", a)], w=[("acc", a)])
        P.dma("pool", lambda e, T: e.dma_start(out=rden_scr[h:h + 1, :], in_=T["acc"][HD:HD + 1, a, :]),
              r=[("acc", a)], w=[("rden", h)])
        P.dma("sp", lambda e, T: e.dma_start(out=T["bc"][:, :],
                                              in_=rden_scr[h:h + 1, :].partition_broadcast(HD)),
              r=[("rden", h)], w=["bc"])
        P.op("dve", lambda e, T: e.tensor_tensor(out=T["ot"][:, a, :], in0=T["acc"][0:HD, a, :],
                                                  in1=T["bc"][:, :], op=ALU.mult),
             r=[("acc", a), "bc"], w=[("ot", a)])
        P.dma("pool", lambda e, T: e.dma_start(out=att_scr[h * HD:(h + 1) * HD, :], in_=T["ot"][:, a, :]),
              r=[("ot", a)], w=[("att_scr", h)])

    NPB = 4
    LA = 3

    def pipeline(items, stage1, stage2):
        n = len(items)
        for k in range(n + LA):
            if k < n:
                stage1(k, items[k])
            if k >= LA:
                stage2(k - LA, items[k - LA])

    def l0_att_phase():
        P = pg.phase("l0att")
        att_alloc(P, HD)
        units = [(h, g) for h in range(NH) for g in range(3)]
        for u0 in range(2):
            att_load(P, u0, qt0[units[u0][1]], kt0[units[u0][1]], v0[units[u0][1]], units[u0][0], HD)
        items = [(ui, pr) for ui in range(len(units)) for pr in range(16)]

        def stage1(k, it):
            ui, pr = it
            h, g = units[ui]
            b = ui % 2
            nbc = NT // DILS[g]
            ld = [("qT", b), ("kT", b)]
            n0 = 2 * pr
            mi = 1 if (n0 % nbc) == 0 else 0
            pss = "pss%d" % (k % NPB)
            pi = k % NPB
            mms = []
            for e_ in range(2):
                n = n0 + e_
                mms.append((e_ * 256, n, n))
                if n % nbc > 0:
                    mms.append((e_ * 256 + 128, n - 1, n))
            P.op("pe", lambda e, T: e.matmul(T[pss][:, :], lhsT=T["ident"][:, :], rhs=T["mk"][:, mi, :],
                                              start=True, stop=False),
                 r=["ident", "mk"], w=[pss])
            for ii, (c0, kb, qb) in enumerate(mms):
                P.op("pe", lambda e, T, c0=c0, kb=kb, qb=qb, last=(ii == len(mms) - 1): e.matmul(
                    T[pss][:, c0:c0 + 128], lhsT=T["kT"][0:HD, b, kb * 128:(kb + 1) * 128],
                    rhs=T["qT"][0:HD, b, qb * 128:(qb + 1) * 128], start=False, stop=last),
                    r=ld, w=[pss])
            P.op("act", lambda e, T: e.activation(out=T["pt"][:, pi, :], in_=T[pss][:, :], func=AF.Exp, scale=0.125),
                 r=[pss], w=[("pt", pi)])

        def stage2(k, it):
            ui, pr = it
            h, g = units[ui]
            b = ui % 2
            a = h % 2
            dil = DILS[g]
            nbc = NT // dil
            L = S // dil
            n0 = 2 * pr
            pi = k % NPB
            pso = "pso%d" % ((ui * 8 + pr // 2) % 3)
            for e_ in range(2):
                n = n0 + e_
                m = n % nbc
                cols = (n % 4) * 128
                P.op("pe", lambda e, T, cols=cols, n=n, e_=e_, m=m: e.matmul(
                    T[pso][0:HD + 1, cols:cols + 128], lhsT=T["vv"][:, b, n, 0:HD + 1],
                    rhs=T["pt"][:, pi, e_ * 256:e_ * 256 + 128], start=True, stop=(m == 0)),
                    r=[("vv", b, 0), ("vv", b, 1), ("vv1", b), ("pt", pi)], w=[pso])
                if m > 0:
                    P.op("pe", lambda e, T, cols=cols, n=n, e_=e_: e.matmul(
                        T[pso][0:HD + 1, cols:cols + 128], lhsT=T["vv"][:, b, n - 1, 0:HD + 1],
                        rhs=T["pt"][:, pi, e_ * 256 + 128:e_ * 256 + 256], start=False, stop=True),
                        r=[("vv", b, 0), ("vv", b, 1), ("vv1", b), ("pt", pi)], w=[pso])
            if pr % 2 == 1:
                B = pr // 2
                if g == 0:
                    P.op("dve", lambda e, T: e.tensor_copy(
                        out=T["acc"][:, a, B * 512:(B + 1) * 512], in_=T[pso][0:HD + 1, :]),
                        r=[pso], w=[("acc", a)])
                else:
                    nsub = 512 // min(512, L)
                    w_ = 512 // nsub
                    for sub in range(nsub):
                        npos = 512 * B + sub * w_
                        r_ = npos // L
                        p0 = npos % L
                        st_ = p0 * dil + r_
                        en_ = st_ + (w_ - 1) * dil + 1
                        P.op("dve", lambda e, T, st_=st_, en_=en_, sub=sub: e.tensor_tensor(
                            out=T["acc"][:, a, st_:en_:dil], in0=T["acc"][:, a, st_:en_:dil],
                            in1=T[pso][0:HD + 1, sub * w_:(sub + 1) * w_], op=ALU.add),
                            r=[pso, ("acc", a)], w=[("acc", a)])
            if pr == 15 and g == 2:
                att_finish(P, h, a)
            if pr == 15 and ui + 2 < len(units):
                h2, g2 = units[ui + 2]
                att_load(P, b, qt0[g2], kt0[g2], v0[g2], h2, HD)

        pipeline(items, stage1, stage2)
        return P

    def outproj_phase(name, wsrc, res_src, res_dst):
        P = pg.phase(name)
        P.sb("w_st", [128, 2, D], F32)
        P.sb("wo", [128, KC, D], BF16)
        P.sb("junk", [128, D], BF16)
        P.sb("xt", [128, 3, D], F32)
        for i in range(4):
            P.ps("po%d" % i, [128, 512], F32)
        for k in range(KC):
            P.dma("sp", lambda e, T, k=k: e.dma_start(out=T["xnT"][:, k, :], in_=att_scr[k * 128:(k + 1) * 128, :]),
                  w=[("attT", k)])
        for k in range(KC):
            sb_ = k % 2
            P.dma("sp", lambda e, T, k=k, sb_=sb_: e.dma_start(out=T["w_st"][:, sb_, :],
                                                                in_=wsrc[k * 128:(k + 1) * 128, :]),
                  w=[("w_st", sb_)])
            P.op("dve", lambda e, T, k=k, sb_=sb_: e.tensor_copy(out=T["wo"][:, k, :], in_=T["w_st"][:, sb_, :]),
                 r=[("w_st", sb_)], w=[("wo", k)])
        for j in range(NT):
            xb = j % 3
            P.dma("sp", lambda e, T, j=j, xb=xb: e.dma_start(out=T["xt"][:, xb, :],
                                                              in_=res_src[j * 128:(j + 1) * 128, :]),
                  r=[("hs", j)], w=[("xt", xb)])
            for s_ in range(2):
                po = "po%d" % ((2 * j + s_) % 4)
                for k in range(KC):
                    P.op("pe", lambda e, T, j=j, s_=s_, k=k, po=po: e.matmul(
                        T[po][:, :], lhsT=T["xnT"][:, k, j * 128:(j + 1) * 128],
                        rhs=T["wo"][:, k, s_ * 512:(s_ + 1) * 512], start=(k == 0), stop=(k == KC - 1)),
                        r=[("attT", k), ("wo", k)], w=[po])
                P.op("dve", lambda e, T, xb=xb, s_=s_, po=po: e.tensor_tensor(
                    out=T["xt"][:, xb, s_ * 512:(s_ + 1) * 512], in0=T["xt"][:, xb, s_ * 512:(s_ + 1) * 512],
                    in1=T[po][:, :], op=ALU.add),
                    r=[po, ("xt", xb)], w=[("xt", xb)])
            P.op("act", lambda e, T, j=j, xb=xb: e.activation(out=T["junk"][:, :], in_=T["xt"][:, xb, :],
                                                               func=AF.Square, accum_out=T["g_ss"][:, j:j + 1]),
                 r=[("xt", xb)], w=[("g_ss", j)])
            P.dma("pool", lambda e, T, j=j, xb=xb: e.dma_start(out=res_dst[j * 128:(j + 1) * 128, :],
                                                                in_=T["xt"][:, xb, :]),
                  r=[("xt", xb)], w=[("hs", j)])
        return P

    def ffn_phase(name, li):
        P = pg.phase(name)
        P.sb("wd", [128, NJ, D], BF16)
        P.sb("actT", [128, NJ, 1024], BF16)
        P.sb("wgs", [128, 2, 2048], F32)
        P.sb("wg", [128, 2, 2048], BF16)
        P.sb("sg", [128, 2, 512], F32)
        P.sb("xt", [128, 2, D], F32)
        P.sb("junk", [128, D], BF16)
        for i in range(8):
            P.ps("pb%d" % i, [128, 512], F32)
        wgu = ffn_w_gu[li]
        wdn = ffn_w_down[li]
        c = {"st": 0, "wg": 0, "sg": 0, "xt": 0}

        def stg():
            b = c["st"] % 2
            c["st"] += 1
            return b

        for j in range(NJ):
            b = stg()
            P.dma("sp", lambda e, T, j=j, b=b: e.dma_start(out=T["wgs"][:, b, 0:D], in_=wdn[j * 128:(j + 1) * 128, :]),
                  w=[("wgs", b, 0), ("wgs", b, 1)])
            P.op("dve", lambda e, T, j=j, b=b: e.tensor_copy(out=T["wd"][:, j, :], in_=T["wgs"][:, b, 0:D]),
                 r=[("wgs", b, 0), ("wgs", b, 1)], w=[("wd", j)])

        def load_gu(j):
            b = stg()
            b2 = c["wg"] % 2
            c["wg"] += 1
            P.dma("sp", lambda e, T: e.dma_start(
                out=T["wgs"][:, b, :].rearrange("p (k c) -> p k c", k=KC)[:, :, 0:128],
                in_=wgu[:, j * 128:(j + 1) * 128].rearrange("(k p) c -> p k c", p=128)), w=[("wgs", b, 0)])
            P.dma("sp", lambda e, T: e.dma_start(
                out=T["wgs"][:, b, :].rearrange("p (k c) -> p k c", k=KC)[:, :, 128:256],
                in_=wgu[:, DFF + j * 128:DFF + (j + 1) * 128].rearrange("(k p) c -> p k c", p=128)),
                w=[("wgs", b, 1)])
            P.op("dve", lambda e, T: e.tensor_copy(out=T["wg"][:, b2, :], in_=T["wgs"][:, b, :]),
                 r=[("wgs", b, 0), ("wgs", b, 1)], w=[("wg", b2)])
            return b2

        for st_ in range(4):
            nxt = load_gu(0)
            for j in range(NJ):
                b2 = nxt
                if j + 1 < NJ:
                    nxt = load_gu(j + 1)
                set_ = j % 2
                for half in range(2):
                    tok0 = st_ * 1024 + half * 512
                    xk = [("xnT", tok0 // 128 + i) for i in range(4)]
                    pgb = "pb%d" % (4 * set_ + half)
                    pub = "pb%d" % (4 * set_ + 2 + half)
                    for (bank, off) in ((pgb, 0), (pub, 128)):
                        for k in range(KC):
                            P.op("pe", lambda e, T, bank=bank, off=off, k=k, b2=b2, tok0=tok0: e.matmul(
                                T[bank][:, :],
                                lhsT=T["wg"][:, b2, :].rearrange("p (k c) -> p k c", k=KC)[:, k, off:off + 128],
                                rhs=T["xnT"][:, k, tok0:tok0 + 512], start=(k == 0), stop=(k == KC - 1)),
                                r=xk + [("wg", b2)], w=[bank])
                    sgi = c["sg"] % 2
                    c["sg"] += 1
                    P.op("act", lambda e, T, pgb=pgb, sgi=sgi: e.activation(out=T["sg"][:, sgi, :], in_=T[pgb][:, :],
                                                                             func=AF.Silu),
                         r=[pgb], w=[("sg", sgi)])
                    P.op("dve", lambda e, T, pub=pub, sgi=sgi, j=j, half=half: e.tensor_tensor(
                        out=T["actT"][:, j, half * 512:(half + 1) * 512], in0=T["sg"][:, sgi, :], in1=T[pub][:, :],
                        op=ALU.mult),
                        r=[pub, ("sg", sgi)], w=[("actT", j, half)])
            for tb in range(8):
                jt = st_ * 8 + tb
                xb = c["xt"] % 2
                c["xt"] += 1
                P.dma("sp", lambda e, T, jt=jt, xb=xb: e.dma_start(out=T["xt"][:, xb, :],
                                                                    in_=h_scr[jt * 128:(jt + 1) * 128, :]),
                      r=[("hs", jt)], w=[("xt", xb)])
                for s_ in range(2):
                    bank = "pb%d" % ((2 * tb + s_) % 8)
                    for j in range(NJ):
                        P.op("pe", lambda e, T, bank=bank, j=j, tb=tb, s_=s_: e.matmul(
                            T[bank][:, :], lhsT=T["actT"][:, j, tb * 128:(tb + 1) * 128],
                            rhs=T["wd"][:, j, s_ * 512:(s_ + 1) * 512], start=(j == 0), stop=(j == NJ - 1)),
                            r=[("actT", j, tb // 4), ("wd", j)], w=[bank])
                    P.op("dve", lambda e, T, bank=bank, xb=xb, s_=s_: e.tensor_tensor(
                        out=T["xt"][:, xb, s_ * 512:(s_ + 1) * 512], in0=T["xt"][:, xb, s_ * 512:(s_ + 1) * 512],
                        in1=T[bank][:, :], op=ALU.add),
                        r=[bank, ("xt", xb)], w=[("xt", xb)])
                P.op("act", lambda e, T, jt=jt, xb=xb: e.activation(out=T["junk"][:, :], in_=T["xt"][:, xb, :],
                                                                     func=AF.Square, accum_out=T["g_ss"][:, jt:jt + 1]),
                     r=[("xt", xb)], w=[("g_ss", jt)])
                P.dma("pool", lambda e, T, jt=jt, xb=xb: e.dma_start(out=h_scr[jt * 128:(jt + 1) * 128, :],
                                                                      in_=T["xt"][:, xb, :]),
                      r=[("xt", xb)], w=[("hs", jt)])
        return P

    def gate_extra(P):
        P.sb("wfs", [128, KC, 128], F32)
        P.sb("wfb", [128, KC, NH], BF16)
        P.sb("negb", [NH, 1], F32)
        P.sb("lf", [NH, S], F32)
        P.sb("csum", [NH, S], F32)
        P.sb("qrow", [NH, S], BF16)
        P.sb("identf", [128, 128], F32)
        P.sb("cst", [128, NT * NH], F32)
        P.ps("psf", [128, 512], F32)
        P.ps("pct", [128, 512], F32)
        P.dma("sp", lambda e, T: e.dma_start(out=T["identf"][:, :], in_=c_identf[:, :]), w=["identf"])
        P.dma("sp", lambda e, T: e.dma_start(
            out=T["wfs"][:, :, 0:NH], in_=b_w_in[:, 3 * D:3 * D + NH].rearrange("(k p) c -> p k c", p=128)), w=["wfs"])
        P.op("dve", lambda e, T: e.tensor_copy(out=T["wfb"][:, :, :], in_=T["wfs"][:, :, 0:NH]), r=["wfs"], w=["wfb"])
        P.dma("sp", lambda e, T: e.dma_start(out=T["negb"][:, :], in_=b_f[:, :]), w=["negb0"])
        P.op("dve", lambda e, T: e.tensor_scalar(out=T["negb"][:, :], in0=T["negb"][:, :], scalar1=-1.0, scalar2=None,
                                                  op0=ALU.mult), r=["negb0"], w=["negb"])
        for tt in range(8):
            xk = [("xnT", 4 * tt + i) for i in range(4)]
            for k in range(KC):
                P.op("pe", lambda e, T, k=k, tt=tt: e.matmul(
                    T["psf"][0:NH, :], lhsT=T["wfb"][:, k, :], rhs=T["xnT"][:, k, tt * 512:(tt + 1) * 512],
                    start=(k == 0), stop=(k == KC - 1)), r=xk + ["wfb"], w=["psf"])
            P.op("act", lambda e, T, tt=tt: e.activation(out=T["lf"][:, tt * 512:(tt + 1) * 512], in_=T["psf"][0:NH, :],
                                                          func=AF.Exp, bias=T["negb"][:, 0:1], scale=-1.0),
                 r=["psf", "negb"], w=[("lf", tt)])
        P.op("act", lambda e, T: e.activation(out=T["lf"][:, :], in_=T["lf"][:, :], func=AF.Ln, bias=1.0),
             r=[("lf", tt) for tt in range(8)], w=["lf"])
        P.op("dve", lambda e, T: e.tensor_tensor_scan(out=T["csum"][:, :], data0=T["lf"][:, :], data1=T["lf"][:, :],
                                                       initial=0.0, op0=ALU.add, op1=ALU.max),
             r=["lf"], w=["csum"])
        P.op("dve", lambda e, T: e.tensor_scalar(out=T["qrow"][:, :], in0=T["csum"][:, :], scalar1=-8.0, scalar2=None,
                                                  op0=ALU.mult), r=["csum"], w=["qrow"])
        P.dma("pool", lambda e, T: e.dma_start(out=qt1[:, HD, :], in_=T["qrow"][:, :]), r=["qrow"], w=["qt1row"])
        P.op("dve", lambda e, T: e.memset(T["qrow"][:, :], 1.0), r=["qrow"], w=["qrow"])
        P.dma("pool", lambda e, T: e.dma_start(out=kt1[:, HD, :], in_=T["qrow"][:, :]), r=["qrow"], w=["kt1row"])
        for blk in range(NT):
            P.op("pe", lambda e, T, blk=blk: e.transpose(
                out=T["pct"][:, blk * NH:(blk + 1) * NH], in_=T["csum"][0:NH, blk * 128:(blk + 1) * 128],
                identity=T["identf"][0:NH, 0:NH]), r=["csum", "identf"], w=["pct"])
        P.op("act", lambda e, T: e.activation(out=T["cst"][:, :], in_=T["pct"][:, :], func=AF.Copy),
             r=["pct"], w=["cst"])
        P.dma("pool", lambda e, T: e.dma_start(out=cs_scr[:, :], in_=T["cst"][:, :]), r=["cst"], w=["cs_scr"])

    def l1_att_phase():
        P = pg.phase("l1att")
        KR = HD + 1
        att_alloc(P, KR)
        P.sb("cst", [128, NT * NH], F32)
        P.dma("sp", lambda e, T: e.dma_start(out=T["cst"][:, :], in_=cs_scr[:, :]), w=["cst"])
        att_load(P, 0, qt1, kt1, v1, 0, KR)
        att_load(P, 1, qt1, kt1, v1, 1, KR)
        items = [(h, Qi, J) for h in range(NH) for Qi in range(8) for J in range(4 * Qi + 4)]

        def stage1(k, it):
            h, Qi, J = it
            b = h % 2
            ld = [("qT", b), ("kT", b)]
            d_ = J - 4 * Qi
            c0 = 128 * d_ if d_ > 0 else 0
            W = 512 - c0
            pss = "pss%d" % (k % NPB)
            pi = k % NPB
            if d_ >= 0:
                P.op("pe", lambda e, T: e.matmul(T[pss][:, c0:512], lhsT=T["ident"][:, :], rhs=T["mk"][:, 2, 0:W],
                                                  start=True, stop=False),
                     r=["ident", "mk"], w=[pss])
            P.op("pe", lambda e, T: e.matmul(
                T[pss][:, c0:512], lhsT=T["kT"][0:KR, b, J * 128:(J + 1) * 128],
                rhs=T["qT"][0:KR, b, Qi * 512 + c0:(Qi + 1) * 512], start=(d_ < 0), stop=True),
                r=ld, w=[pss])
            P.op("act", lambda e, T: e.activation(
                out=T["pt"][:, pi, c0:512], in_=T[pss][:, c0:512], func=AF.Exp,
                bias=T["cst"][:, J * NH + h:J * NH + h + 1], scale=0.125),
                r=[pss, "cst"], w=[("pt", pi)])

        def stage2(k, it):
            h, Qi, J = it
            b = h % 2
            a = h % 2
            nJ = 4 * Qi + 4
            d_ = J - 4 * Qi
            c0 = 128 * d_ if d_ > 0 else 0
            pi = k % NPB
            pso = "pso%d" % ((h * 8 + Qi) % 3)
            P.op("pe", lambda e, T: e.matmul(
                T[pso][0:KR, c0:512], lhsT=T["vv"][:, b, J, 0:HD + 1], rhs=T["pt"][:, pi, c0:512],
                start=(J == 0), stop=(J == nJ - 1)),
                r=[("vv", b, 0), ("vv", b, 1), ("vv1", b), ("pt", pi)], w=[pso])
            if J == nJ - 1:
                P.op("dve", lambda e, T: e.tensor_copy(
                    out=T["acc"][:, a, Qi * 512:(Qi + 1) * 512], in_=T[pso][0:KR, :]),
                    r=[pso], w=[("acc", a)])
                if Qi == 7:
                    att_finish(P, h, a)
                    if h + 2 < NH:
                        att_load(P, b, qt1, kt1, v1, h + 2, KR)

        pipeline(items, stage1, stage2)
        return P

    P0 = pg.phase("consts")
    load_consts(P0)

    def build_all():
        for g in range(3):
            norm_phase("l0n%d" % g, x, a_norm[0:1, :], g)
            inproj_phase("l0p%d" % g, g,
                         lambda t, k, g=g: a_w_in[k * 128:(k + 1) * 128,
                                                  g * 3072 + t * 1024: g * 3072 + (t + 1) * 1024],
                         qt0[g], kt0[g], v0[g], g)
            if stop_after == "l0p%d" % g:
                return
        l0_att_phase()
        if stop_after == "l0att":
            return
        outproj_phase("l0out", a_w_out, x, h_scr)
        if stop_after == "l0out":
            return
        norm_phase("f0n", h_scr, ffn_norm[0:1, :], 0, have_ss=True)
        ffn_phase("f0", 0)
        if stop_after == "f0":
            return
        norm_phase("l1n", h_scr, b_norm[0:1, :], 0, have_ss=True)
        inproj_phase("l1p", 0, lambda t, k: b_w_in[k * 128:(k + 1) * 128, t * 1024:(t + 1) * 1024],
                     qt1, kt1, v1, 0, extra=gate_extra, rotary=False)
        if stop_after == "l1p":
            return
        l1_att_phase()
        if stop_after == "l1att":
            return
        outproj_phase("l1out", b_w_out, h_scr, h_scr)
        if stop_after == "l1out":
            return
        norm_phase("f1n", h_scr, ffn_norm[1:2, :], 0, have_ss=True)
        ffn_phase("f1", 1)
        if stop_after == "f1":
            return
        norm_phase("fin", h_scr, final_norm[0:1, :], 0, transpose=False, dst=out, have_ss=True)

    build_all()
    pg.emit()
    return nc, pg


_CACHE = {}


def core_inputs(inp, c, consts):
    f = lambda a: np.ascontiguousarray(np.asarray(a, dtype=np.float32))
    m = {
        "x": f(inp["x"][c]),
        "a_norm": f(inp["a_norm"]).reshape(1, D),
        "a_w_in": f(inp["a_w_in"][0]),
        "a_w_out": f(inp["a_w_out"][0]),
        "b_norm": f(inp["b_norm"]).reshape(1, D),
        "b_w_in": f(inp["b_w_in"][0]),
        "b_f": f(inp["b_f"]).reshape(NH, 1),
        "b_w_out": f(inp["b_w_out"][0]),
        "ffn_norm": f(inp["ffn_norm"]),
        "ffn_w_gu": f(inp["ffn_w_gu"]),
        "ffn_w_down": f(inp["ffn_w_down"]),
        "final_norm": f(inp["final_norm"]).reshape(1, D),
    }
    m.update(consts)
    return m


def kernel(**inputs):
    if "nc" not in _CACHE:
        _CACHE["nc"] = build()[0]
        _CACHE["consts"] = host_consts()
    nc = _CACHE["nc"]
    consts = _CACHE["consts"]
    nb = inputs["x"].shape[0]
    maps = [core_inputs(inputs, c, consts) for c in range(nb)]
    res = run_bass_kernel_spmd(nc, maps, core_ids=list(range(nb)))
    return np.stack([np.asarray(r["out"], dtype=np.float32) for r in res.results], axis=0)
```
